# Optimizing a Trainium2 kernel written in Bass

```python
import jax, jax.numpy as jnp
from jax import lax
import numpy as np

D_MODEL = 1024
BATCH = 8
SEQ = 2048
DEPTH = 1

EPS = 1e-6
DN_HEADS = 8
DN_DK = 128
DN_DV = 128
DN_QK_WIDTH = DN_HEADS * DN_DK
DN_WIDTH = DN_HEADS * DN_DV
CONV_K = 4
CHUNK = 64
DIL_CONFIGS = ((128, 1), (512, 4), (2048, 16))
HEADS_PER_GROUP = 4
ATTN_HEADS = HEADS_PER_GROUP * len(DIL_CONFIGS)
ATTN_HEAD_DIM = 64
ATTN_WIDTH = ATTN_HEADS * ATTN_HEAD_DIM
ATTN_BLOCK = 128
IN_SPLITS = (DN_QK_WIDTH, DN_QK_WIDTH, DN_WIDTH, DN_WIDTH, DN_HEADS, DN_HEADS,
             ATTN_WIDTH, ATTN_WIDTH, ATTN_WIDTH, ATTN_WIDTH, D_MODEL, D_MODEL)
N_IN = sum(IN_SPLITS)

kernel_name = "hybrid_gated_deltanet_dilated_attn_block"


def rms_norm(x, w):
    xf = x.astype(jnp.float32)
    y = xf * lax.rsqrt(jnp.mean(xf * xf, axis=-1, keepdims=True) + EPS)
    return (y * w.astype(jnp.float32)).astype(x.dtype)


def l2_normalize(x):
    xf = x.astype(jnp.float32)
    return xf * lax.rsqrt(jnp.sum(xf * xf, axis=-1, keepdims=True) + EPS)


def causal_dwconv_silu(x, w):
    k_w = w.shape[0]
    s = x.shape[1]
    xp = jnp.pad(x, ((0, 0), (k_w - 1, 0), (0, 0)))
    y = sum(xp[:, j:j + s] * w[j] for j in range(k_w))
    return jax.nn.silu(y)


def gated_delta_rule(q, k, v, g, beta):
    b, s, h, dk = q.shape
    n = s // CHUNK

    def to_chunks(t):
        return jnp.moveaxis(t.reshape((b, n, CHUNK) + t.shape[2:]), 3, 2)

    q, k, v, g, beta = map(to_chunks, (q * dk ** -0.5, k, v, g, beta))
    gc = jnp.cumsum(g, axis=-1)
    causal = jnp.tril(jnp.ones((CHUNK, CHUNK), dtype=bool))
    strict = jnp.tril(jnp.ones((CHUNK, CHUNK), dtype=bool), -1)
    decay = jnp.exp(jnp.where(causal, gc[..., :, None] - gc[..., None, :], -jnp.inf))
    k_beta = k * beta[..., None]
    v_beta = v * beta[..., None]
    lmat = jnp.where(strict, jnp.einsum('bnhcd,bnhed->bnhce', k_beta, k) * decay, 0.0)
    eye = jnp.eye(CHUNK, dtype=jnp.float32)
    t_inv = lax.linalg.triangular_solve(eye + lmat, jnp.broadcast_to(eye, lmat.shape),
                                        left_side=True, lower=True, unit_diagonal=True)
    u = jnp.einsum('bnhce,bnhed->bnhcd', t_inv, v_beta)
    w = jnp.einsum('bnhce,bnhed->bnhcd', t_inv, k_beta * jnp.exp(gc)[..., None])
    a_qk = jnp.einsum('bnhcd,bnhed->bnhce', q, k) * decay
    q_dec = q * jnp.exp(gc)[..., None]
    k_dec = k * jnp.exp(gc[..., -1:] - gc)[..., None]
    g_last = jnp.exp(gc[..., -1])

    def step(state, xs):
        q_c, k_c, u_c, w_c, a_c, gl = xs
        v_new = u_c - jnp.einsum('bhcd,bhde->bhce', w_c, state)
        o_c = jnp.einsum('bhcd,bhde->bhce', q_c, state) + jnp.einsum('bhce,bhed->bhcd', a_c, v_new)
        state = state * gl[..., None, None] + jnp.einsum('bhcd,bhce->bhde', k_c, v_new)
        return state, o_c

    xs = tuple(jnp.moveaxis(t, 1, 0) for t in (q_dec, k_dec, u, w, a_qk, g_last))
    state0 = jnp.zeros((b, h, dk, v.shape[-1]), jnp.float32)
    _, o = lax.scan(step, state0, xs)
    o = jnp.moveaxis(jnp.moveaxis(o, 0, 1), 2, 3)
    return o.reshape(b, s, h, -1)


def dilated_window_attention(q, k, v, window, dilation):
    b, s, h, dh = q.shape
    n_back = window // dilation
    sub_len = s // dilation
    nb = -(-sub_len // ATTN_BLOCK)
    lp = nb * ATTN_BLOCK
    bb = b * dilation

    def to_sub(t):
        t = t.reshape(b, sub_len, dilation, h, dh).transpose(0, 2, 1, 3, 4).reshape(bb, sub_len, h, dh)
        return jnp.pad(t, ((0, 0), (0, lp - sub_len), (0, 0), (0, 0)))

    def band(t):
        tp = jnp.pad(t, ((0, 0), (ATTN_BLOCK, 0), (0, 0), (0, 0))).reshape(bb, nb + 1, ATTN_BLOCK, h, dh)
        return jnp.concatenate([tp[:, :-1], tp[:, 1:]], axis=2)

    qb = to_sub(q).reshape(bb, nb, ATTN_BLOCK, h, dh)
    kb = band(to_sub(k))
    vb = band(to_sub(v))
    scores = jnp.einsum('bnqhd,bnkhd->bnhqk', qb, kb,
                        preferred_element_type=jnp.float32) * (dh ** -0.5)
    blk = jnp.arange(nb)[:, None, None]
    qpos = blk * ATTN_BLOCK + jnp.arange(ATTN_BLOCK)[None, :, None]
    kpos = (blk - 1) * ATTN_BLOCK + jnp.arange(2 * ATTN_BLOCK)[None, None, :]
    dist = qpos - kpos
    mask = (dist >= 0) & (dist <= n_back) & (kpos >= 0)
    scores = jnp.where(mask[None, :, None], scores, -jnp.inf)
    m = jnp.max(scores, axis=-1, keepdims=True)
    p = jnp.exp(scores - m)
    denom = jnp.sum(p, axis=-1, keepdims=True)
    o = jnp.einsum('bnhqk,bnkhd->bnqhd', (p / denom).astype(v.dtype), vb)
    lse = (m + jnp.log(denom))[..., 0]
    o = o.reshape(bb, lp, h, dh)[:, :sub_len]
    o = o.reshape(b, dilation, sub_len, h, dh).transpose(0, 2, 1, 3, 4).reshape(b, s, h, dh)
    lse = lse.transpose(0, 1, 3, 2).reshape(bb, lp, h)[:, :sub_len]
    lse = lse.reshape(b, dilation, sub_len, h).transpose(0, 2, 1, 3).reshape(b, s, h)
    return o, lse


def setup_inputs(seed: int = 0) -> dict:
    key = jax.random.key(seed)
    ks = jax.random.split(key, 16)
    f32 = jnp.float32
    d = D_MODEL
    x = jax.random.normal(ks[0], (BATCH, SEQ, d), f32)
    c = jax.random.normal(ks[1], (BATCH, d), f32)
    norm_w = 1.0 + 0.02 * jax.random.normal(ks[2], (DEPTH, d), f32)
    ada_w = jax.random.normal(ks[3], (DEPTH, d, 3 * d), f32) * d ** -0.5
    ada_b = 0.01 * jax.random.normal(ks[4], (DEPTH, 3 * d), f32)
    w_in = jax.random.normal(ks[5], (DEPTH, d, N_IN), f32) * d ** -0.5
    conv_w = jax.random.normal(ks[6], (DEPTH, CONV_K, 2 * DN_QK_WIDTH + DN_WIDTH), f32) * CONV_K ** -0.5
    a_log = jnp.log(jax.random.uniform(ks[7], (DEPTH, DN_HEADS), f32, 1.0, 16.0))
    dt = jnp.exp(jax.random.uniform(ks[8], (DEPTH, DN_HEADS), f32, np.log(1e-3), np.log(1e-1)))
    dt_bias = dt + jnp.log(-jnp.expm1(-dt))
    dn_norm_w = 1.0 + 0.02 * jax.random.normal(ks[9], (DEPTH, DN_DV), f32)
    w_proj_a = jax.random.normal(ks[10], (DEPTH, DN_WIDTH, d), f32) * DN_WIDTH ** -0.5
    w_proj_b = jax.random.normal(ks[11], (DEPTH, ATTN_WIDTH, d), f32) * ATTN_WIDTH ** -0.5
    w_out = jax.random.normal(ks[12], (DEPTH, d, d), f32) * d ** -0.5
    final_norm_w = 1.0 + 0.02 * jax.random.normal(ks[13], (d,), f32)
    return {"x": x, "c": c, "norm_w": norm_w, "ada_w": ada_w, "ada_b": ada_b, "w_in": w_in,
            "conv_w": conv_w, "a_log": a_log, "dt_bias": dt_bias, "dn_norm_w": dn_norm_w,
            "w_proj_a": w_proj_a, "w_proj_b": w_proj_b, "w_out": w_out, "final_norm_w": final_norm_w}


def reference(x, c, norm_w, ada_w, ada_b, w_in, conv_w, a_log, dt_bias, dn_norm_w,
              w_proj_a, w_proj_b, w_out, final_norm_w):
    b, s, _ = x.shape
    split_points = [int(p) for p in np.cumsum(IN_SPLITS)[:-1]]
    for l in range(DEPTH):
        mod = jax.nn.silu(c) @ ada_w[l] + ada_b[l]
        shift, scale, gate = jnp.split(mod, 3, axis=-1)
        h = rms_norm(x, norm_w[l]) * (1.0 + scale[:, None]) + shift[:, None]
        proj = h @ w_in[l]
        (qa, ka, va, za, beta_in, a_in, qb, kb, vb, zb, ga, gb) = jnp.split(proj, split_points, axis=-1)

        qkv = causal_dwconv_silu(jnp.concatenate([qa, ka, va], axis=-1), conv_w[l])
        qa, ka, va = jnp.split(qkv, [DN_QK_WIDTH, 2 * DN_QK_WIDTH], axis=-1)
        q_dn = l2_normalize(qa.reshape(b, s, DN_HEADS, DN_DK))
        k_dn = l2_normalize(ka.reshape(b, s, DN_HEADS, DN_DK))
        v_dn = va.reshape(b, s, DN_HEADS, DN_DV).astype(jnp.float32)
        beta = jax.nn.sigmoid(beta_in.astype(jnp.float32))
        g = -jnp.exp(a_log[l].astype(jnp.float32)) * jax.nn.softplus(
            a_in.astype(jnp.float32) + dt_bias[l].astype(jnp.float32))
        o_a = gated_delta_rule(q_dn, k_dn, v_dn, g, beta).astype(x.dtype)
        y_a = rms_norm(o_a, dn_norm_w[l]).reshape(b, s, DN_WIDTH) * jax.nn.silu(za)
        y_a = y_a @ w_proj_a[l]

        qb = qb.reshape(b, s, ATTN_HEADS, ATTN_HEAD_DIM)
        kb = kb.reshape(b, s, ATTN_HEADS, ATTN_HEAD_DIM)
        vb = vb.reshape(b, s, ATTN_HEADS, ATTN_HEAD_DIM)
        outs, lses = [], []
        for gi, (win, dil) in enumerate(DIL_CONFIGS):
            hs = slice(gi * HEADS_PER_GROUP, (gi + 1) * HEADS_PER_GROUP)
            o_g, lse_g = dilated_window_attention(qb[:, :, hs], kb[:, :, hs], vb[:, :, hs], win, dil)
            outs.append(o_g)
            lses.append(lse_g)
        o_b = jnp.stack(outs, axis=2)
        w_g = jax.nn.softmax(jnp.stack(lses, axis=2), axis=2)
        y_b = (o_b * w_g[..., None].astype(o_b.dtype)).reshape(b, s, ATTN_WIDTH) * jax.nn.silu(zb)
        y_b = y_b @ w_proj_b[l]

        merged = jax.nn.sigmoid(ga) * y_a + jax.nn.sigmoid(gb) * y_b
        x = x + gate[:, None] * (merged @ w_out[l])
    return rms_norm(x, final_norm_w)
```

```python
import numpy as np
import ml_dtypes
from contextlib import ExitStack
import concourse.bass as bass
import concourse.mybir as mybir
from concourse.bass_utils import run_bass_kernel_spmd

F32 = mybir.dt.float32
BF16 = mybir.dt.bfloat16
AF = mybir.ActivationFunctionType
ALU = mybir.AluOpType
AX = mybir.AxisListType

T = 2048
D = 1024
NT = 16
NIN = 9232
EPS = 1e-6
NEG = -30000.0
DEBUG = False
ATTACH_WAIT = True
import os
STOP = int(os.environ.get("KSTOP", "9"))


class Ctr:
    def __init__(self, sem):
        self.sem = sem
        self.count = 0


class Eng:
    def __init__(self, name, eng, ctr):
        self.name, self.eng, self.ctr = name, eng, ctr
        self.seen = {}
        self.ring = []
        self.ri = 0


class K:
    def __init__(self, nc, st):
        self.nc, self.st = nc, st
        self.E = {}
        for name, e in (("pe", nc.tensor), ("act", nc.scalar), ("dve", nc.vector),
                        ("pool", nc.gpsimd), ("sp", nc.sync)):
            self.E[name] = Eng(name, e, Ctr(st.enter_context(nc.semaphore("s_" + name))))
        for name, n in (("sp", 12), ("pool", 6)):
            self.E[name].ring = [Ctr(st.enter_context(nc.semaphore(f"d_{name}{i}"))) for i in range(n)]
        self.lastw, self.readers = {}, {}
        self.nbank = 0
        self.nq = 0
        self.banks = [0, 1, 2, 3]
        self.quarters = [(b, 0) for b in (4, 5)]
        self.qfreelist = [0, 1, 2, 3, 4, 5, 6, 7]

    def wait(self, E, ctr, val):
        if val <= 0 or (ctr is E.ctr and E.name in ("pe", "sp")):
            return
        if E.seen.get(id(ctr), 0) >= val:
            return
        E.eng.wait_ge(ctr.sem, val)
        E.seen[id(ctr)] = val

    def _deps(self, E, r, w):
        deps = {}

        def add(cv):
            c, v = cv
            if id(c) not in deps or deps[id(c)][1] < v:
                deps[id(c)] = (c, v)
        for k in r:
            if k in self.lastw:
                add(self.lastw[k])
        for k in w:
            if k in self.lastw:
                add(self.lastw[k])
            for cv in self.readers.get(k, {}).values():
                add(cv)
        need = []
        for c, v in deps.values():
            if v <= 0 or (c is E.ctr and E.name in ("pe", "sp")):
                continue
            if E.seen.get(id(c), 0) >= v:
                continue
            need.append((c, v))
        if not ATTACH_WAIT:
            for c, v in need:
                self.wait(E, c, v)
            return None
        for c, v in need[:-1]:
            self.wait(E, c, v)
        if need:
            c, v = need[-1]
            E.seen[id(c)] = v
            return (c, v)
        return None

    def _mark(self, ctr, val, r, w):
        for k in w:
            self.lastw[k] = (ctr, val)
            self.readers[k] = {}
        for k in r:
            self.readers.setdefault(k, {})[id(ctr)] = (ctr, val)

    @staticmethod
    def _norm(r, w):
        isps = lambda x: isinstance(x, tuple) and x[0] in ("ps", "psb")
        nb = lambda x: (x[0], x[1])
        w2 = [nb(x) if isps(x) else x for x in w] + [nb(x) for x in r if isps(x)]
        r2 = [x for x in r if not isps(x)]
        return r2, list(dict.fromkeys(w2))

    def op(self, en, fn, r=(), w=(), sig=True):
        E = self.E[en]
        r, w = self._norm(r, w)
        att = self._deps(E, r, w)
        inst = fn(E.eng)
        if att is not None:
            inst._wait_ge(att[0].sem, att[1])
        val = E.ctr.count + 1
        if sig:
            inst.then_inc(E.ctr.sem, 1)
            E.ctr.count = val
        self._mark(E.ctr, val, r, w)
        return inst

    def dma(self, qn, out, in_, r=(), w=()):
        E = self.E[qn]
        att = self._deps(E, r, w)
        if att is not None:
            E.eng.wait_ge(att[0].sem, att[1])
        slot = E.ring[E.ri % len(E.ring)]
        E.ri += 1
        self.wait(E, slot, slot.count)
        E.eng.dma_start(out=out, in_=in_).then_inc(slot.sem, 16)
        slot.count += 16
        self._mark(slot, slot.count, r, w)

    def barrier(self):
        ctrs = [e.ctr for e in self.E.values()]
        for e in self.E.values():
            ctrs += e.ring
        for e in self.E.values():
            for c in ctrs:
                self.wait(e, c, c.count)
        self.lastw, self.readers = {}, {}

    def bank(self):
        b = self.banks[self.nbank % len(self.banks)]
        self.nbank += 1
        return b

    def qalloc(self, n):
        if len(self.qfreelist) < n:
            return None
        out = [(self.qfreelist.pop(0), 0) for _ in range(n)]
        return out

    def qfree(self, *bqs):
        for bq in bqs:
            self.qfreelist.append(bq[0])

    def quarter(self):
        bq = self.quarters[self.nq % len(self.quarters)]
        self.nq += 1
        return bq


def bkeys(b):
    return [("ps", b, q) for q in range(4)]


def build_program():
    nc = bass.Bass("TRN2", target_bir_lowering=False)
    dt = lambda name, shape, dty, kind: nc.dram_tensor(name, shape, dty, kind=kind).ap()
    x_d = dt("x", [T, D], F32, "ExternalInput")
    cT_d = dt("cT", [128, 8], F32, "ExternalInput")
    normw_d = dt("normw_bc", [128, D], F32, "ExternalInput")
    adaw_d = dt("ada_w", [D, 3 * D], F32, "ExternalInput")
    adab_d = dt("adab_bc", [128, 3 * D], F32, "ExternalInput")
    win_d = dt("w_in", [D, NIN], F32, "ExternalInput")
    convw_d = dt("convw", [128, 24, 4], F32, "ExternalInput")
    alog_d = dt("alog_bc", [128, 8], F32, "ExternalInput")
    dtb_d = dt("dtb_bc", [128, 8], F32, "ExternalInput")
    dnw_d = dt("dnw_col", [128, 1], F32, "ExternalInput")
    pa_d = dt("w_proj_a", [D, D], F32, "ExternalInput")
    pb_d = dt("w_proj_b", [768, D], F32, "ExternalInput")
    wo_d = dt("w_out", [D, D], F32, "ExternalInput")
    fnw_d = dt("fnw_bc", [128, D], F32, "ExternalInput")
    cf_d = dt("cf", [128, 768], F32, "ExternalInput")
    cb_d = dt("cb", [128, 1664], BF16, "ExternalInput")
    out_d = dt("out", [T, D], F32, "ExternalOutput")
    dbg_d = dt("dbg", [128, 8 * T], F32, "ExternalOutput") if DEBUG else None

    with ExitStack() as st:
        k = K(nc, st)
        sb = lambda name, shape, dty=F32: st.enter_context(nc.sbuf_tensor("sb_" + name, shape, dty))
        psf = [st.enter_context(nc.psum_tensor(f"psf{i}", [128, 512], F32)) for i in range(6)]
        psb = [st.enter_context(nc.psum_tensor(f"psb{i}", [128, 1024], BF16)) for i in range(2)]
        psf = psf + [psb[i][:].bitcast(F32) for i in range(2)]

        def psq(bq):
            b, q = bq
            return psf[b][:, q * 128:(q + 1) * 128]

        cf = sb("cf", [128, 768])
        cb = sb("cb", [128, 1664], BF16)
        k.dma("sp", cf[:], cf_d[:, :], w=["cf"])
        k.dma("sp", cb[:], cb_d[:, :], w=["cb"])
        ones_f, TLE, TGT, ident_f, mU_incl, mU_strict = [cf[:, i * 128:(i + 1) * 128] for i in range(6)]
        (ident, ones_b, NEGprev, NEGcur, onesA, onesB) = [cb[:, i * 128:(i + 1) * 128] for i in range(6)]
        offall = cb[:, 6 * 128:13 * 128]
        OFFI = {1: 0, 2: 1, 4: 2, 8: 3, 16: 4, 32: 5, 64: 6}

        gate = sb("gate", [128, D])
        epsc = sb("epsc", [128, 4])
        k.op("dve", lambda e: e.memset(epsc[:, 0:1], EPS), w=["epsc"])
        k.op("dve", lambda e: e.memset(epsc[:, 1:2], 4 * EPS), w=["epsc"])
        k.op("dve", lambda e: e.memset(epsc[:, 2:3], 1.0), w=["epsc"])
        k.barrier()
        h_fm = sb("h_fm", [128, 8, T], BF16)
        p01 = ExitStack()
        mod = p01.enter_context(nc.sbuf_tensor("sb_mod", [128, 3 * D], F32))
        a_bc = p01.enter_context(nc.sbuf_tensor("sb_a_bc", [128, D], F32))

        p0 = ExitStack()
        if True:
            sb0 = lambda name, shape, dty=F32: p0.enter_context(nc.sbuf_tensor("sb_" + name, shape, dty))
            ss = sb0("ss1", [128, NT])
            rs = sb0("rs1", [128, NT])
            k.op("pool", lambda e: e.memset(ss[:], 0.0), w=["ss"])
            cT = sb0("cT", [128, 8])
            th = sb0("c_th", [128, 8])
            sc = sb0("c_sc", [128, 8])
            screp = sb0("screp", [128, 8, 128], BF16)
            normw = sb0("normw", [128, D])
            adab = sb0("adab", [128, 3 * D])
            aw = [sb0(f"aw{i}", [128, 8, 512], BF16) for i in range(2)]
            k.dma("sp", cT[:], cT_d[:, :], w=["cT"])
            k.dma("sp", normw[:], normw_d[:, :], w=["normw"])
            k.dma("sp", adab[:], adab_d[:, :], w=["adab"])
            k.op("act", lambda e: e.activation(out=th[:], in_=cT[:], func=AF.Tanh, scale=0.5), r=["cT"], w=["th"])
            k.op("dve", lambda e: e.scalar_tensor_tensor(out=sc[:], in0=th[:], scalar=1.0, in1=cT[:],
                                                         op0=ALU.add, op1=ALU.mult), r=["th", "cT"], w=["sc"])
            k.op("dve", lambda e: e.tensor_scalar(out=sc[:], in0=sc[:], scalar1=0.5, scalar2=None, op0=ALU.mult),
                 r=["sc"], w=["sc"])
            for kt in range(8):
                k.op("dve", lambda e: e.tensor_copy(out=screp[:, kt, :], in_=sc[:, kt:kt + 1].to_broadcast([128, 128])),
                     r=["sc"], w=["screp"])
            def ada_block(blk):
                a = aw[blk % 2]
                b = k.bank()
                for kt in range(8):
                    k.op("pe", lambda e: e.matmul(psf[b][:, :], lhsT=screp[:, kt, :], rhs=a[:, kt, :],
                                                  start=(kt == 0), stop=(kt == 7)),
                         r=["screp", ("aw", blk % 2)], w=bkeys(b), sig=(kt == 7))
                k.op("dve", lambda e: e.tensor_tensor(out=mod[:, blk * 512:(blk + 1) * 512], in0=psf[b][:, :],
                                                      in1=adab[:, blk * 512:(blk + 1) * 512], op=ALU.add),
                     r=bkeys(b) + ["adab"], w=[("mod", blk)])

            for blk in range(4):
                k.dma("pool", aw[blk % 2][:], adaw_d[:, blk * 512:(blk + 1) * 512].rearrange("(kt p) n -> p kt n", p=128),
                      w=[("aw", blk % 2)])
                ada_block(blk)
            for blk in (4, 5):
                k.dma("pool", aw[blk % 2][:], adaw_d[:, blk * 512:(blk + 1) * 512].rearrange("(kt p) n -> p kt n", p=128),
                      w=[("aw", blk % 2)])
            k.op("dve", lambda e: e.scalar_tensor_tensor(out=a_bc[:], in0=mod[:, D:2 * D], scalar=1.0, in1=normw[:],
                                                         op0=ALU.add, op1=ALU.mult),
                 r=[("mod", 2), ("mod", 3), "normw"], w=["a_bc"])
            if DEBUG == "mod":
                k.dma("sp", dbg_d[:, 0:3 * D], mod[:], r=[("mod", i) for i in range(6)])
                k.dma("sp", dbg_d[:, 3 * D:4 * D], a_bc[:], r=["a_bc"])
                k.dma("sp", dbg_d[:, 4 * D:4 * D + 8], sc[:], r=["sc"])
                k.dma("sp", dbg_d[:, 4 * D + 8:4 * D + 16], cT[:], r=["cT"])
                k.dma("sp", dbg_d[:, 4 * D + 16:4 * D + 24], th[:], r=["th"])

        with ExitStack() as p1:
            sb1 = lambda name, shape, dty=F32: p1.enter_context(nc.sbuf_tensor("sb_" + name, shape, dty))
            xt = [sb1(f"xt{i}", [128, D]) for i in range(4)]
            junk = [sb1(f"junk1{i}", [128, D], BF16) for i in range(4)]
            hn = [sb1(f"hn{i}", [128, D]) for i in range(4)]
            ht = [sb1(f"ht{i}", [128, D], BF16) for i in range(4)]
            tbank = [(psb[0][:], ("psb", 0)), (psb[1][:], ("psb", 1)),
                     (psf[4][:].bitcast(BF16), ("ps", 4)), (psf[5][:].bitcast(BF16), ("ps", 5))]

            def tile_gen(tt, bi):
                xb, hb, hb2, jk = xt[bi], hn[bi], ht[bi], junk[bi]
                k.dma("sp", xb[:], x_d[tt * 128:(tt + 1) * 128, :], w=[("xt", bi)])
                k.op("act", lambda e: e.activation(out=jk[:], in_=xb[:], func=AF.Square, accum_out=ss[:, tt:tt + 1]),
                     r=[("xt", bi), "ss"], w=[("junk", bi), ("ss", tt)])
                yield
                k.op("act", lambda e: e.activation(out=rs[:, tt:tt + 1], in_=ss[:, tt:tt + 1], func=AF.Ln, scale=1.0 / D, bias=epsc[:, 0:1]),
                     r=[("ss", tt)], w=[("rs", tt)])
                k.op("act", lambda e: e.activation(out=rs[:, tt:tt + 1], in_=rs[:, tt:tt + 1], func=AF.Exp, scale=-0.5),
                     r=[("rs", tt)], w=[("rs", tt)])
                yield
                k.op("dve", lambda e: e.scalar_tensor_tensor(out=hb[:], in0=xb[:], scalar=rs[:, tt:tt + 1], in1=a_bc[:],
                                                             op0=ALU.mult, op1=ALU.mult),
                     r=[("xt", bi), ("rs", tt), "a_bc"], w=[("hn", bi)])
                yield
                k.op("pool", lambda e: e.tensor_tensor(out=hb2[:], in0=hb[:], in1=mod[:, 0:D], op=ALU.add),
                     r=[("hn", bi), ("mod", 0), ("mod", 1)], w=[("ht", bi)])
                yield
                tb_, tk_ = tbank[tt % 4]
                for kt in range(8):
                    k.op("pe", lambda e: e.transpose(tb_[:, kt * 128:(kt + 1) * 128], hb2[:, kt * 128:(kt + 1) * 128], ident),
                         r=[("ht", bi), "cb"], w=[tk_], sig=(kt == 7))
                yield
                if tt % 2 == 0:
                    k.op("act", lambda e: e.activation(out=h_fm[:, :, tt * 128:(tt + 1) * 128],
                                                       in_=tb_.rearrange("p (k t) -> p k t", k=8), func=AF.Copy),
                         r=[tk_], w=[("h", tt)])
                else:
                    k.op("dve", lambda e: e.tensor_copy(out=h_fm[:, :, tt * 128:(tt + 1) * 128], in_=tb_.rearrange("p (k t) -> p k t", k=8)),
                         r=[tk_], w=[("h", tt)])

            tl_todo = list(range(NT))
            tl_act = [None] * 4
            while True:
                prog = False
                for bi in range(4):
                    if tl_act[bi] is None and tl_todo:
                        tl_act[bi] = tile_gen(tl_todo.pop(0), bi)
                    if tl_act[bi] is not None:
                        prog = True
                        try:
                            next(tl_act[bi])
                        except StopIteration:
                            tl_act[bi] = None
                if not prog:
                    break
            ada_block(4)
            ada_block(5)
            k.op("dve", lambda e: e.tensor_scalar(out=gate[:], in0=mod[:, 2 * D:3 * D], scalar1=0.5,
                                                  scalar2=None, op0=ALU.mult),
                 r=[("mod", 4), ("mod", 5)], w=["gate"])
            k.barrier()
        p0.close()
        p01.close()

        if DEBUG == "h":
            dbg32 = sb("dbg32", [128, T])
            for kt in range(8):
                k.op("dve", lambda e: e.tensor_copy(out=dbg32[:], in_=h_fm[:, kt, :]), r=[("h", i) for i in range(NT)], w=["dbg32"])
                k.dma("sp", dbg_d[:, kt * T:(kt + 1) * T], dbg32[:], r=["dbg32"])

        hkeys = [("h", i) for i in range(NT)]

        def proj_fm(wt, wkey, wcol, tt4, b):
            for kt in range(8):
                k.op("pe", lambda e: e.matmul(psf[b][:, :], lhsT=wt[:, kt, wcol:wcol + 128],
                                              rhs=h_fm[:, kt, tt4 * 512:(tt4 + 1) * 512],
                                              start=(kt == 0), stop=(kt == 7)),
                     r=[wkey] + hkeys[tt4 * 4:tt4 * 4 + 4], w=bkeys(b), sig=(kt == 7))

        def TT(en, out, a, b, op, r, w):
            return k.op(en, lambda e: e.tensor_tensor(out=out, in0=a, in1=b, op=op), r=r, w=w)

        def TS(en, out, a, s1, op0, r, w, s2=None, op1=None):
            if op1 is None:
                return k.op(en, lambda e: e.tensor_scalar(out=out, in0=a, scalar1=s1, scalar2=None, op0=op0), r=r, w=w)
            return k.op(en, lambda e: e.tensor_scalar(out=out, in0=a, scalar1=s1, scalar2=s2, op0=op0, op1=op1), r=r, w=w)

        def STT(out, a, s, b, op0, op1, r, w):
            return k.op("dve", lambda e: e.scalar_tensor_tensor(out=out, in0=a, scalar=s, in1=b, op0=op0, op1=op1), r=r, w=w)

        def ACTV(out, in_, func, r, w, scale=1.0, bias=None):
            if bias is None:
                return k.op("act", lambda e: e.activation(out=out, in_=in_, func=func, scale=scale), r=r, w=w)
            return k.op("act", lambda e: e.activation(out=out, in_=in_, func=func, scale=scale, bias=bias), r=r, w=w)

        def MM(out, lhsT, rhs, r, w, start=True, stop=True, sig=True):
            return k.op("pe", lambda e: e.matmul(out, lhsT=lhsT, rhs=rhs, start=start, stop=stop), r=r, w=w, sig=sig)

        def DVT(out, in_, r, w):
            return k.op("dve", lambda e: e.transpose(out=out, in_=in_), r=r, w=w)

        def wload(dst, key, src_cols, nkt=8):
            k.dma("pool", dst, src_cols.rearrange("(kt p) n -> p kt n", p=128), w=[key])

        ya_fm = sb("ya_fm", [128, 8, T], BF16)
        ma_fm = ya_fm
        RS = float(128 ** -0.5)

        k.banks = [0, 1]
        k.quarters = [(b, 0) for b in (2, 3, 4, 5, 0, 1)]
        with ExitStack() as pA:
          if os.environ.get('KSKIPA') != '1':
              sbA = lambda name, shape, dty=F32: pA.enter_context(nc.sbuf_tensor("sb_" + name, shape, dty))
              convw = sbA("convw", [128, 24, 4])
              alog = sbA("alog", [128, 8]); dtb = sbA("dtb", [128, 8]); dnw = sbA("dnw", [128, 1])
              k.dma("sp", convw[:], convw_d[:, :, :], w=["convw"])
              k.dma("sp", alog[:], alog_d[:, :], w=["alog"])
              k.dma("sp", dtb[:], dtb_d[:, :], w=["dtb"])
              k.dma("sp", dnw[:], dnw_d[:, :], w=["dnw"])
              TS("dve", dnw[:], dnw[:], 0.5, ALU.mult, r=["dnw"], w=["dnw"])
              wba = sbA("wba", [128, 8, 128], BF16)
              if os.environ.get("KX") != "2":
                  wload(wba[:], "wba", win_d[:, 3984:4112])
              else:
                  k.op("pool", lambda e: e.memset(wba[:], 0.0), w=["wba"])
              ba = sbA("ba", [128, NT, 16])
              beta = sbA("beta", [128, NT, 8]); lnb = sbA("lnb", [128, NT, 8]); gt = sbA("gt", [128, NT, 8])
              ngc = sbA("ngc", [128, NT, 8]); egc = sbA("egc", [128, NT, 8]); ekd = sbA("ekd", [128, NT, 8])
              bg = sbA("bg", [128, NT, 8]); ea = sbA("ea", [128, 8])
              for tt in range(NT):
                  q_ = k.quarter()
                  for kt in range(8):
                      MM(psq(q_)[:, 0:16], h_fm[:, kt, tt * 128:(tt + 1) * 128], wba[:, kt, 112:128], r=["wba", ("h", tt)],
                         w=[("ps",) + q_], start=(kt == 0), stop=(kt == 7), sig=(kt == 7))
                  k.op("dve", lambda e: e.tensor_copy(out=ba[:, tt, :], in_=psq(q_)[:, 0:16]), r=[("ps",) + q_], w=[("ba", tt)])
              ACTV(ea[:], alog[:], AF.Exp, r=["alog"], w=["ea"])
              for tt in range(NT):
                  kk = [("col", tt)]
                  ACTV(beta[:, tt, :], ba[:, tt, 0:8], AF.Tanh, r=[("ba", tt)], w=kk, scale=0.5)
                  TS("dve", beta[:, tt, :], beta[:, tt, :], 0.5, ALU.mult, r=kk, w=kk, s2=0.5, op1=ALU.add)
                  TT("dve", gt[:, tt, :], ba[:, tt, 8:16], dtb[:], ALU.add, r=[("ba", tt), "dtb"], w=kk)
              for tt in range(NT):
                  kk = [("col", tt)]
                  ACTV(lnb[:, tt, :], beta[:, tt, :], AF.Ln, r=kk, w=kk)
                  ACTV(gt[:, tt, :], gt[:, tt, :], AF.Exp, r=kk, w=kk)
                  ACTV(gt[:, tt, :], gt[:, tt, :], AF.Ln, r=kk, w=kk, bias=epsc[:, 2:3])
                  STT(gt[:, tt, :], gt[:, tt, :], -1.0, ea[:], ALU.mult, ALU.mult, r=kk + ["ea"], w=kk)
                  if os.environ.get("KX") == "1":
                      continue
                  q1, q2 = k.quarter(), k.quarter()
                  MM(psq(q1)[:, 0:8], TLE, gt[:, tt, :], r=kk + ["cf"], w=[("ps",) + q1])
                  MM(psq(q2)[:, 0:8], TGT, gt[:, tt, :], r=kk + ["cf"], w=[("ps",) + q2])
                  TS("dve", ngc[:, tt, :], psq(q1)[:, 0:8], -1.0, ALU.mult, r=[("ps",) + q1], w=kk)
                  ACTV(egc[:, tt, :], psq(q1)[:, 0:8], AF.Exp, r=[("ps",) + q1], w=kk)
                  ACTV(ekd[:, tt, :], psq(q2)[:, 0:8], AF.Exp, r=[("ps",) + q2], w=kk)
                  TT("dve", bg[:, tt, :], beta[:, tt, :], egc[:, tt, :], ALU.mult, r=kk, w=kk)

              q_fm2 = sbA("q_fm2", [128, 2, T], BF16)
              k_fm2 = sbA("k_fm2", [128, 2, T], BF16)
              k_tok2 = sbA("k_tok2", [128, NT, 2, 128], BF16)
              v_tok2 = sbA("v_tok2", [128, NT, 2, 128], BF16)
              q_fm = [q_fm2[:, i, :] for i in range(2)]
              k_fm = [k_fm2[:, i, :] for i in range(2)]
              k_tok = [k_tok2[:, :, i, :] for i in range(2)]
              v_tok = [v_tok2[:, :, i, :] for i in range(2)]
              zg2 = sbA("zg2", [128, 2, T], BF16)
              zg = [zg2[:, i, :] for i in range(2)]
              S_f2 = sbA("S_f2", [128, 2, 128])
              S_b2 = sbA("S_b2", [128, 2, 128], BF16)
              S_f = [S_f2[:, i, :] for i in range(2)]
              S_b = [S_b2[:, i, :] for i in range(2)]
              RT = int(os.environ.get('KRT', '8'))
              NB = int(os.environ.get('KNB', '4'))
              UDELAY = int(os.environ.get('KUDELAY', '4'))
              BSTAG = int(os.environ.get('KBSTAG', '4'))
              wk_p = sbA("wk_p", [128, 8, 256], BF16)
              wload(wk_p[:], ("wblk", 1), win_d[:, 1024: 1024 + 256])
              npsb = [0]

              for gi in range(4 if STOP >= 2 else 0):
                  h0 = 2 * gi
                  sA1 = ExitStack()
                  sb1_ = lambda name, shape, dty=F32: sA1.enter_context(nc.sbuf_tensor(f"sb_{name}_g{gi}", shape, dty))
                  k.banks = [0, 1, 2, 3, 4, 5]
                  wblk = [sb1_(f"wblk{i}", [128, 8, 256], BF16) for i in range(3)]
                  wblk = [wblk[0], wk_p, wblk[1], wblk[2]]
                  for j, base in enumerate((0, 1024, 2048, 3072)):
                      if j != 1:
                          wload(wblk[j][:], ("wblk", j), win_d[:, base + h0 * 128: base + h0 * 128 + 256])
                  pre = [sb1_(f"pre{i}", [128, 3 + T], BF16) for i in range(2)]
                  dwg = sb1_("dwg", [128, 6, 4, 128], BF16)
                  for kd_ in range(3):
                      TT("pool", dwg[:, 2 * kd_:2 * kd_ + 2, :, :], ident.unsqueeze(1).unsqueeze(1).to_broadcast([128, 2, 4, 128]),
                         convw[:, kd_ * 8 + h0: kd_ * 8 + h0 + 2, :].unsqueeze(3).to_broadcast([128, 2, 4, 128]), ALU.mult,
                         r=["cb", "convw"], w=["dwg"])
                  accs = [[sb1_(f"acc{u}{i}", [128, 512]) for i in range(4)] for u in range(2)]
                  tnhs = [[sb1_(f"tnh{u}{i}", [128, 512]) for i in range(4)] for u in range(2)]
                  sqbs = [[sb1_(f"sqb{u}{i}", [128, 512], BF16) for i in range(4)] for u in range(2)]
                  for u in range(2):
                      k.op("pool", lambda e: e.memset(pre[u][:, 0:3], 0.0), w=[("prehalo", u)])

                  def unit_gen(kind, hh, ub, delay=0):
                      for _ in range(delay):
                          yield
                      ci = kind * 8 + h0 + hh
                      pre_ = pre[ub]
                      for tt4 in range(4):
                          b = k.bank()
                          proj_fm(wblk[kind], ("wblk", kind), hh * 128, tt4, b)
                          k.op("dve", lambda e: e.tensor_copy(out=pre_[:, 3 + tt4 * 512: 3 + (tt4 + 1) * 512], in_=psf[b][:, :]),
                               r=bkeys(b), w=[("pre", ub, tt4)])
                          yield
                      subs = [sub_gen(kind, hh, ub, ci, pre_, tt4) for tt4 in range(4)]
                      while subs:
                          for sg_ in list(subs):
                              try:
                                  next(sg_)
                              except StopIteration:
                                  subs.remove(sg_)
                          yield

                  def sub_gen(kind, hh, ub, ci, pre_, tt4):
                      if True:
                          sx = tt4
                          acc, tnh, sqb = accs[ub][sx], tnhs[ub][sx], sqbs[ub][sx]
                          ka, kt_, ks = ("acc", ub, sx), ("tnh", ub, sx), ("sqb", ub, sx)
                          prek = [("pre", ub, tt4)] + ([("pre", ub, tt4 - 1)] if tt4 else [("prehalo", ub)])
                          c0 = tt4 * 512
                          ui_ = kind * 2 + hh
                          b = k.bank()
                          for j in range(4):
                              MM(psf[b][:, :], dwg[:, ui_, j, :], pre_[:, c0 + j:c0 + j + 512], r=prek + ["dwg"], w=bkeys(b),
                                 start=(j == 0), stop=(j == 3), sig=(j == 3))
                          ACTV(tnh[:], psf[b][:, :], AF.Tanh, r=bkeys(b), w=[kt_], scale=0.5)
                          if kind < 2:
                              STT(acc[:], tnh[:], 1.0, psf[b][:, :], ALU.add, ALU.mult, r=[kt_] + bkeys(b), w=[ka])
                          else:
                              STT(sqb[:], tnh[:], 1.0, psf[b][:, :], ALU.add, ALU.mult, r=[kt_] + bkeys(b), w=[ks])
                          yield
                          if kind < 2:
                              dst = (q_fm, k_fm)[kind][hh]
                              dk_ = (("qfm", hh, tt4) if kind == 0 else ("kfm", hh, tt4))
                              ACTV(sqb[:], acc[:], AF.Square, r=[ka], w=[ks])
                              yield
                              b = k.bank()
                              MM(psf[b][:, :], ones_b, sqb[:], r=["cb", ks], w=bkeys(b))
                              ACTV(tnh[:], psf[b][:, :], AF.Ln, r=bkeys(b), w=[kt_], bias=epsc[:, 1:2])
                              yield
                              ACTV(tnh[:], tnh[:], AF.Exp, r=[kt_], w=[kt_], scale=-0.5)
                              yield
                              if kind == 0:
                                  STT(dst[:, c0:c0 + 512], acc[:], RS, tnh[:], ALU.mult, ALU.mult, r=[ka, kt_], w=[dk_])
                              else:
                                  TT("dve", dst[:, c0:c0 + 512], acc[:], tnh[:], ALU.mult, r=[ka, kt_], w=[dk_])
                              yield
                              if kind == 1:
                                  pb_ = npsb[0] % 2
                                  npsb[0] += 1
                                  for j in range(4):
                                      k.op("pe", lambda e: e.transpose(psb[pb_][:, j * 128:(j + 1) * 128],
                                                                       dst[:, c0 + j * 128:c0 + (j + 1) * 128], ident),
                                           r=[dk_, "cb"], w=[("psb", pb_)], sig=(j == 3))
                                  k.op("dve", lambda e: e.tensor_copy(out=k_tok[hh][:, tt4 * 4:(tt4 + 1) * 4, :],
                                                                      in_=psb[pb_][:, 0:512].rearrange("p (a b) -> p a b", a=4)),
                                       r=[("psb", pb_)], w=[("ktok", hh, tt4)])
                                  yield
                          else:
                              pb_ = npsb[0] % 2
                              npsb[0] += 1
                              for j in range(4):
                                  k.op("pe", lambda e: e.transpose(psb[pb_][:, j * 128:(j + 1) * 128], sqb[:, j * 128:(j + 1) * 128], ident),
                                       r=[ks, "cb"], w=[("psb", pb_)], sig=(j == 3))
                              TS("dve", v_tok[hh][:, tt4 * 4:(tt4 + 1) * 4, :], psb[pb_][:, 0:512].rearrange("p (a b) -> p a b", a=4),
                                 0.5, ALU.mult, r=[("psb", pb_)], w=[("vtok", hh, tt4)])
                              yield

                  def za_gen(hh, ub):
                      for tt4 in range(4):
                          tnh = tnhs[ub][tt4 % 2]
                          kt_ = ("tnh", ub, tt4 % 2)
                          b = k.bank()
                          proj_fm(wblk[3], ("wblk", 3), hh * 128, tt4, b)
                          ACTV(tnh[:], psf[b][:, :], AF.Tanh, r=bkeys(b), w=[kt_], scale=0.5)
                          STT(zg2[:, hh, tt4 * 512:(tt4 + 1) * 512], tnh[:], 1.0, psf[b][:, :], ALU.add, ALU.mult,
                              r=[kt_] + bkeys(b), w=[("zg", hh, tt4)])
                          yield

                  ulist = [("u", kind, hh) for kind in (1, 0, 2) for hh in range(2)] + [("z", 0, hh) for hh in range(2)]
                  uact = [None, None]
                  ufirst = [False]
                  while True:
                      prog = False
                      for ub in range(2):
                          if uact[ub] is None and ulist:
                              t_, kind, hh = ulist.pop(0)
                              dly = UDELAY if (ub == 1 and not ufirst[0]) else 0
                              if ub == 1:
                                  ufirst[0] = True
                              uact[ub] = unit_gen(kind, hh, ub, dly) if t_ == "u" else za_gen(hh, ub)
                          if uact[ub] is not None:
                              prog = True
                              try:
                                  next(uact[ub])
                              except StopIteration:
                                  uact[ub] = None
                      if not prog:
                          break
                  for hh in range(2):
                      k.op("pool", lambda e: e.memset(S_f[hh], 0.0), w=[("S", hh)])
                      k.op("pool", lambda e: e.memset(S_b[hh], 0.0), w=[("Sb", hh)])
                  k.barrier()
                  sA1.close()

                  sA2 = ExitStack()
                  sb2_ = lambda name, shape, dty=F32: sA2.enter_context(nc.sbuf_tensor(f"sb_{name}_g{gi}", shape, dty))
                  scr = {}
                  pbs = []
                  for si in range(NB):
                      d_ = {}
                      for nm, dty in (("Gm", F32), ("M2", F32), ("U", BF16),
                                      ("M1", BF16), ("vb", BF16), ("kbg", BF16),
                                      ("Uo0", BF16), ("Uo1", BF16)):
                          d_[nm] = sb2_(f"{nm}_b{si}", [128, 4, 128], dty)
                      for a_, b_ in (("t1", "Gm"), ("DT", "Gm"), ("t2", "M2"), ("CU", "M2"), ("M1b", "M1")):
                          d_[a_] = d_[b_]
                      gv8 = d_["Gm"][:].bitcast(BF16).rearrange("p g (h c) -> p (g h) c", h=2)
                      mv8 = d_["M2"][:].bitcast(BF16).rearrange("p g (h c) -> p (g h) c", h=2)
                      d_["N0"], d_["N1"] = gv8[:, 0:4, :], gv8[:, 4:8, :]
                      d_["V0"], d_["V1"] = mv8[:, 0:4, :], mv8[:, 4:8, :]
                      pbs.append(d_)
                  ring = {}
                  for nm, dty in (("EG", F32), ("u", F32), ("wT", BF16), ("Qd", BF16), ("Aq", BF16), ("kdec", BF16)):
                      ring[nm] = sb2_(f"ring_{nm}", [128, RT, 2, 128], dty)
                  scr2 = {}
                  for nm, dty in (("vnew", BF16), ("sq", BF16), ("osb", F32), ("ln", F32), ("tmp", F32)):
                      scr2[nm] = sb2_(f"{nm}_2", [128, 2, 128], dty)
                  if gi < 3:
                      wload(wk_p[:], ("wblk", 1), win_d[:, 1024 + (h0 + 2) * 128: 1024 + (h0 + 2) * 128 + 256])
                  ALIAS = {"t1": "Gm", "DT": "Gm", "t2": "M2", "CU": "M2", "M1b": "M1", "N0": "Gm", "N1": "Gm", "V0": "M2", "V1": "M2"}

                  def prepb_gen(t0, si):
                      P = pbs[si]
                      s0 = t0 % RT
                      chains = [(tl, hh) for tl in range(2) for hh in range(2)]
                      A3 = lambda nm: P[nm][:]
                      A2 = lambda nm: P[nm][:].rearrange("p g c -> p (g c)")
                      A4 = lambda nm: P[nm][:].rearrange("p (a b) c -> p a b c", a=2)
                      Ag = lambda nm, g: P[nm][:, g, :]
                      PK = lambda nm: ("pb_" + ALIAS.get(nm, nm), si)
                      colv = lambda t_: t_[:, t0:t0 + 2, h0:h0 + 2].unsqueeze(3).to_broadcast([128, 2, 2, 128])
                      c3 = lambda c_: c_.unsqueeze(1).to_broadcast([128, 4, 128])
                      c4 = lambda c_: c_.unsqueeze(1).unsqueeze(1).to_broadcast([128, 2, 2, 128])
                      R4 = lambda nm: ring[nm][:, s0:s0 + 2, :, :]
                      RKs = lambda nm: [(nm, s0 + tl, hh) for tl, hh in chains]
                      ps3 = lambda b: psf[b][:, :].rearrange("p (g c) -> p g c", g=4)
                      ps4 = lambda b: psf[b][:, :].rearrange("p (a b c) -> p a b c", a=2, b=2)
                      psg = lambda b, g: psf[b][:, g * 128:(g + 1) * 128]
                      ck = [("col", t0), ("col", t0 + 1)]
                      qk_ = [("qfm", hh, t_ // 4) for hh in range(2) for t_ in (t0, t0 + 1)]
                      kk_ = [("kfm", hh, t_ // 4) for hh in range(2) for t_ in (t0, t0 + 1)]
                      vk_ = [("vtok", hh, t_ // 4) for hh in range(2) for t_ in (t0, t0 + 1)]
                      tk_ = [("ktok", hh, t_ // 4) for hh in range(2) for t_ in (t0, t0 + 1)]

                      def bank1():
                          while (bs_ := k.qalloc(1)) is None:
                              yield None
                          yield bs_[0][0]

                      TT("pool", A4("Gm"), c4(TLE), colv(gt), ALU.mult, r=ck + ["cf"], w=[PK("Gm")])
                      TT("pool", A4("M2"), c4(ident_f), colv(lnb), ALU.mult, r=ck + ["cf"], w=[PK("M2")])
                      for g, (tl, hh) in enumerate(chains):
                          tt = t0 + tl
                          h = h0 + hh
                          ACTV(Ag("vb", g), v_tok[hh][:, tt, :], AF.Copy, r=vk_ + ck, w=[PK("vb")], scale=beta[:, tt, h:h + 1])
                          ACTV(Ag("kbg", g), k_tok[hh][:, tt, :], AF.Copy, r=tk_ + ck, w=[PK("kbg")], scale=bg[:, tt, h:h + 1])
                          ACTV(ring["kdec"][:, s0 + tl, hh, :], k_tok[hh][:, tt, :], AF.Copy, r=tk_ + ck, w=[("kdec", s0 + tl, hh)],
                               scale=ekd[:, tt, h:h + 1])
                      yield
                      TT("pool", A3("M2"), A3("M2"), A3("Gm"), ALU.add, r=[PK("M2"), PK("Gm")], w=[PK("M2")])
                      while (bs_ := k.qalloc(2)) is None:
                          yield
                      bA, bB = bs_[0][0], bs_[1][0]
                      MM(psf[bA][:, :], ones_f, A2("Gm"), r=["cf", PK("Gm")], w=bkeys(bA))
                      MM(psf[bB][:, :], ones_f, A2("M2"), r=["cf", PK("M2")], w=bkeys(bB))
                      yield
                      TT("dve", A3("t1"), ps3(bA), c3(mU_incl), ALU.add, r=bkeys(bA) + ["cf"], w=[PK("t1")])
                      ACTV(R4("EG"), ps4(bA), AF.Exp, r=bkeys(bA), w=RKs("EG"))
                      TT("dve", A3("t2"), ps3(bB), c3(mU_strict), ALU.add, r=bkeys(bB) + ["cf"], w=[PK("t2")])
                      k.qfree((bA, 0), (bB, 0))
                      yield
                      TT("dve", A4("t1"), A4("t1"), colv(ngc), ALU.add, r=[PK("t1")] + ck, w=[PK("t1")])
                      TT("dve", A4("t2"), A4("t2"), colv(ngc), ALU.add, r=[PK("t2")] + ck, w=[PK("t2")])
                      yield
                      ACTV(A2("DT"), A2("t1"), AF.Exp, r=[PK("t1")], w=[PK("DT")])
                      ACTV(A2("CU"), A2("t2"), AF.Exp, r=[PK("t2")], w=[PK("CU")])
                      while (bs_ := k.qalloc(2)) is None:
                          yield
                      bC, bD = bs_[0][0], bs_[1][0]
                      for g, (tl, hh) in enumerate(chains):
                          sl = slice((t0 + tl) * 128, (t0 + tl + 1) * 128)
                          MM(psg(bC, g), k_fm[hh][:, sl], k_fm[hh][:, sl], r=kk_, w=bkeys(bC), sig=(g == 3))
                      for g, (tl, hh) in enumerate(chains):
                          sl = slice((t0 + tl) * 128, (t0 + tl + 1) * 128)
                          MM(psg(bD, g), k_fm[hh][:, sl], q_fm[hh][:, sl], r=kk_ + qk_, w=bkeys(bD), sig=(g == 3))
                      yield
                      TT("dve", A3("U"), ps3(bC), A3("CU"), ALU.mult, r=bkeys(bC) + [PK("CU")], w=[PK("U")])
                      TT("dve", R4("Aq"), ps4(bD), A4("DT"), ALU.mult, r=bkeys(bD) + [PK("DT")], w=RKs("Aq"))
                      k.qfree((bC, 0), (bD, 0))
                      TT("pool", R4("Qd"), q_fm2[:, :, t0 * 128:(t0 + 2) * 128].rearrange("p h (t c) -> p t h c", t=2), R4("EG"),
                         ALU.mult, r=qk_ + RKs("EG"), w=RKs("Qd"))
                      yield
                      Tn, Ttn, Un = ("N0", "N1"), ("V0", "V1"), ("Uo0", "Uo1")
                      offb = lambda b_: c3(offall[:, OFFI[b_] * 128:(OFFI[b_] + 1) * 128])
                      TT("pool", A3(Un[0]), A3("U"), offb(1), ALU.mult, r=[PK("U"), "cb"], w=[PK(Un[0])])
                      TT("pool", A3(Ttn[0]), c3(ident), A3(Un[0]), ALU.subtract, r=["cb", PK(Un[0])], w=[PK(Ttn[0])])
                      TT("pool", A3(Un[1]), A3("U"), offb(2), ALU.mult, r=[PK("U"), "cb"], w=[PK(Un[1])])
                      yield
                      DVT(A2(Tn[0]), A2(Ttn[0]), r=[PK(Ttn[0])], w=[PK(Tn[0])])
                      cur = 0
                      levels = (2, 4, 8, 16, 32, 64)
                      for li, bsz in enumerate(levels):
                          un = Un[(li + 1) % 2]
                          while (bs_ := k.qalloc(1)) is None:
                              yield
                          b1 = bs_[0][0]
                          for g in range(4):
                              MM(psg(b1, g), Ag(un, g), Ag(Tn[cur], g), r=[PK(un), PK(Tn[cur])], w=bkeys(b1), sig=(g == 3))
                          yield
                          ACTV(A2("M1"), psf[b1][:, :], AF.Copy, r=bkeys(b1), w=[PK("M1")])
                          k.qfree((b1, 0))
                          if li + 1 < len(levels):
                              TT("pool", A3(Un[li % 2]), A3("U"), offb(levels[li + 1]), ALU.mult, r=[PK("U"), "cb"], w=[PK(Un[li % 2])])
                          yield
                          if bsz < 32:
                              while (bs_ := k.qalloc(1)) is None:
                                  yield
                              b2 = bs_[0][0]
                              for g in range(4):
                                  MM(psg(b2, g), Ag(Ttn[cur], g), Ag("M1", g), r=[PK(Ttn[cur]), PK("M1")], w=bkeys(b2), sig=(g == 3))
                              yield
                              TT("dve", A3(Tn[1 - cur]), A3(Tn[cur]), ps3(b2), ALU.subtract, r=bkeys(b2) + [PK(Tn[cur])], w=[PK(Tn[1 - cur])])
                              k.qfree((b2, 0))
                              DVT(A2(Ttn[1 - cur]), A2(Tn[1 - cur]), r=[PK(Tn[1 - cur])], w=[PK(Ttn[1 - cur])])
                              cur = 1 - cur
                              yield
                          elif bsz == 32:
                              T32, T32T = Tn[cur], Ttn[cur]
                              while (bs_ := k.qalloc(2)) is None:
                                  yield
                              b2, b3 = bs_[0][0], bs_[1][0]
                              for g in range(4):
                                  MM(psg(b2, g), Ag("M1", g), Ag(T32T, g), r=[PK("M1"), PK(T32T)], w=bkeys(b2), sig=(g == 3))
                              for g in range(4):
                                  MM(psg(b3, g), Ag(T32T, g), Ag("M1", g), r=[PK("M1"), PK(T32T)], w=bkeys(b3), sig=(g == 3))
                              yield
                              T64n, T64Tn = Tn[1 - cur], Ttn[1 - cur]
                              TT("dve", A3(T64Tn), A3(T32T), ps3(b2), ALU.subtract, r=bkeys(b2) + [PK(T32T)], w=[PK(T64Tn)])
                              TT("dve", A3(T64n), A3(T32), ps3(b3), ALU.subtract, r=bkeys(b3) + [PK(T32)], w=[PK(T64n)])
                              k.qfree((b2, 0), (b3, 0))
                              TTn = T32T
                              Tn = (T64n, T64n)
                              cur = 0
                              yield
                          else:
                              while (bs_ := k.qalloc(1)) is None:
                                  yield
                              b2 = bs_[0][0]
                              for g in range(4):
                                  MM(psg(b2, g), Ag("M1", g), Ag(T64Tn, g), r=[PK("M1"), PK(T64Tn)], w=bkeys(b2), sig=(g == 3))
                              yield
                              TT("dve", A3(TTn), A3(T64Tn), ps3(b2), ALU.subtract, r=bkeys(b2) + [PK(T64Tn)], w=[PK(TTn)])
                              k.qfree((b2, 0))
                              yield
                      while (bs_ := k.qalloc(2)) is None:
                          yield
                      bU, bW = bs_[0][0], bs_[1][0]
                      for g in range(4):
                          MM(psg(bU, g), Ag(TTn, g), Ag("vb", g), r=[PK(TTn), PK("vb")], w=bkeys(bU), sig=(g == 3))
                      for g in range(4):
                          MM(psg(bW, g), Ag("kbg", g), Ag(TTn, g), r=[PK(TTn), PK("kbg")], w=bkeys(bW), sig=(g == 3))
                      yield
                      ACTV(R4("u"), ps4(bU), AF.Copy, r=bkeys(bU), w=RKs("u"))
                      ACTV(R4("wT"), ps4(bW), AF.Copy, r=bkeys(bW), w=RKs("wT"))
                      k.qfree((bU, 0), (bW, 0))

                  def rec_gen(hh_unused, tt):
                      sl = slice(tt * 128, (tt + 1) * 128)
                      tt4 = tt // 4
                      s_ = tt % RT
                      X2 = lambda nm: scr2[nm][:]
                      X2f = lambda nm: scr2[nm][:].rearrange("p h c -> p (h c)")
                      XK = lambda nm: ("rec", nm)
                      R3 = lambda nm: ring[nm][:, s_, :, :]
                      RKs = lambda nm: [(nm, s_, 0), (nm, s_, 1)]
                      half = lambda b, hh: psf[b][:, hh * 128:(hh + 1) * 128]
                      ps2 = lambda b: psf[b][:, 0:256].rearrange("p (h c) -> p h c", h=2)
                      SK = [("S", 0), ("S", 1)]
                      SbK = [("Sb", 0), ("Sb", 1)]
                      while (bs_ := k.qalloc(1)) is None:
                          yield
                      qa = bs_[0][0]
                      for hh in range(2):
                          MM(half(qa, hh), ring["wT"][:, s_, hh, :], S_b2[:, hh, :], r=RKs("wT") + SbK, w=bkeys(qa), sig=(hh == 1))
                      yield
                      TT("dve", X2("vnew"), R3("u"), ps2(qa), ALU.subtract, r=RKs("u") + bkeys(qa), w=[XK("vnew")])
                      k.qfree((qa, 0))
                      yield
                      while (bs_ := k.qalloc(2)) is None:
                          yield
                      qo, qs = bs_[0][0], bs_[1][0]
                      for hh in range(2):
                          MM(half(qs, hh), ring["kdec"][:, s_, hh, :], scr2["vnew"][:, hh, :], r=RKs("kdec") + [XK("vnew")], w=bkeys(qs), sig=(hh == 1))
                      for hh in range(2):
                          MM(half(qo, hh), S_b2[:, hh, :], ring["Qd"][:, s_, hh, :], r=SbK + RKs("Qd"), w=bkeys(qo), stop=False, sig=False)
                          MM(half(qo, hh), scr2["vnew"][:, hh, :], ring["Aq"][:, s_, hh, :], r=[XK("vnew")] + RKs("Aq"), w=bkeys(qo), start=False,
                             sig=(hh == 1))
                      yield
                      for hh in range(2):
                          STT(S_f2[:, hh, :], S_f2[:, hh, :], ring["EG"][:, s_, hh, 127:128], half(qs, hh), ALU.mult, ALU.add,
                              r=SK + RKs("EG") + bkeys(qs), w=[("S", hh)])
                      k.qfree((qs, 0))
                      ACTV(X2("sq"), ps2(qo), AF.Square, r=bkeys(qo), w=[XK("sq")])
                      yield
                      ACTV(S_b2[:], S_f2[:], AF.Copy, r=SK, w=SbK)
                      k.op("dve", lambda e: e.tensor_copy(out=X2("osb"), in_=ps2(qo)), r=bkeys(qo), w=[XK("osb")])
                      k.qfree((qo, 0))
                      while (bs_ := k.qalloc(1)) is None:
                          yield
                      qn = bs_[0][0]
                      MM(psf[qn][:, 0:256], ones_b, X2f("sq"), r=["cb", XK("sq")], w=bkeys(qn))
                      yield
                      ACTV(X2("ln"), ps2(qn), AF.Ln, r=bkeys(qn), w=[XK("ln")], scale=1.0 / 128, bias=epsc[:, 0:1])
                      k.qfree((qn, 0))
                      ACTV(X2("ln"), X2("ln"), AF.Exp, r=[XK("ln")], w=[XK("ln")], scale=-0.5)
                      yield
                      STT(X2f("tmp"), X2f("osb"), dnw[:, 0:1], X2f("ln"), ALU.mult, ALU.mult, r=[XK("osb"), XK("ln"), "dnw"], w=[XK("tmp")])
                      TT("dve", ya_fm[:, h0:h0 + 2, sl], X2("tmp"), zg2[:, :, sl], ALU.mult, r=[XK("tmp"), ("zg", 0, tt4), ("zg", 1, tt4)],
                         w=[("ya", h0, tt), ("ya", h0 + 1, tt)])

                  ntl = NT if STOP >= 3 else 0
                  todo = list(range(0, ntl, 2))
                  act = [None] * NB
                  done = set()
                  rec_act = [None, None]
                  rec_next = [0, 0]
                  rnd_ = 0
                  while True:
                      progressed = False
                      rnd_ += 1
                      for si in range(NB):
                          if act[si] is None and todo:
                              t0_ = todo[0]
                              if rnd_ <= si * BSTAG:
                                  progressed = True
                              elif t0_ + 1 < min(rec_next) + RT:
                                  todo.pop(0)
                                  act[si] = (prepb_gen(t0_, si), t0_)
                          if act[si] is not None:
                              progressed = True
                              try:
                                  next(act[si][0])
                              except StopIteration:
                                  for hh in range(2):
                                      done.add((hh, act[si][1]))
                                      done.add((hh, act[si][1] + 1))
                                  act[si] = None
                      if rec_act[0] is None and rec_next[0] < ntl and (0, rec_next[0]) in done:
                          rec_act[0] = rec_gen(0, rec_next[0])
                      if rec_act[0] is not None:
                          progressed = True
                          try:
                              next(rec_act[0])
                          except StopIteration:
                              rec_act[0] = None
                              rec_next[0] += 1
                              rec_next[1] += 1
                      if not progressed:
                          break
                  assert rec_next == [ntl, ntl] and not todo
                  k.barrier()
                  sA2.close()
              k.barrier()

        def dump_fm(tag, src, nk):
            if DEBUG != tag:
                return
            with ExitStack() as dbs:
                dbg32 = dbs.enter_context(nc.sbuf_tensor("sb_dbg32_" + tag, [128, T], F32))
                for kt in range(nk):
                    k.op("dve", lambda e: e.tensor_copy(out=dbg32[:], in_=src[:, kt, :]), w=["dbg32"])
                    k.dma("sp", dbg_d[:, kt * T:(kt + 1) * T], dbg32[:], r=["dbg32"])
                k.barrier()

        dump_fm("ya", ya_fm, 8)

        k.banks = [0, 1, 2, 3, 4, 5]
        with ExitStack() as pA2:
            sbx = lambda name, shape, dty=F32: pA2.enter_context(nc.sbuf_tensor("sb_" + name, shape, dty))
            wga = [sbx(f"wga{i}", [128, 8, 128], BF16) for i in range(2)]
            paj = [sbx(f"paj{i}", [128, 8, 128], BF16) for i in range(2)]
            tnh2 = [sbx(f"tnhA{i}", [128, 512]) for i in range(2)]
            ma_tmp = sbx("ma_tmp", [128, 8, T], BF16)
            n_ = 0
            def a2_loads(j):
                wload(wga[j % 2][:], ("wga", j % 2), win_d[:, 7184 + j * 128: 7184 + (j + 1) * 128])
                wload(paj[j % 2][:], ("paj", j % 2), pa_d[:, j * 128:(j + 1) * 128])
            a2_loads(0)
            for j in range(8):
                if j + 1 < 8:
                    a2_loads(j + 1)
                for tt4 in range(4):
                    cs = slice(tt4 * 512, (tt4 + 1) * 512)
                    tb = tnh2[n_ % 2]
                    tk = ("tnhA", n_ % 2)
                    n_ += 1
                    b1 = k.bank()
                    proj_fm(wga[j % 2], ("wga", j % 2), 0, tt4, b1)
                    ACTV(tb[:], psf[b1][:, :], AF.Tanh, r=bkeys(b1), w=[tk], scale=0.5)
                    b2 = k.bank()
                    for hk in range(8):
                        MM(psf[b2][:, :], paj[j % 2][:, hk, :], ya_fm[:, hk, cs], r=[("paj", j % 2)], w=bkeys(b2),
                           start=(hk == 0), stop=(hk == 7), sig=(hk == 7))
                    STT(ma_tmp[:, j, cs], tb[:], 1.0, psf[b2][:, :], ALU.add, ALU.mult, r=[tk] + bkeys(b2), w=[("mat", j, tt4)])
            k.barrier()
            for j in range(8):
                if j % 4 == 3:
                    ACTV(ma_fm[:, j, :], ma_tmp[:, j, :], AF.Copy, r=[], w=[("ma", j, t_) for t_ in range(4)])
                else:
                    k.op("dve", lambda e: e.tensor_copy(out=ma_fm[:, j, :], in_=ma_tmp[:, j, :]), w=[("ma", j, t_) for t_ in range(4)])
            k.barrier()

        pB = ExitStack()
        if True:
            sbx = lambda name, shape, dty=F32: pB.enter_context(nc.sbuf_tensor("sb_" + name, shape, dty))
            yb_fm = sbx("yb_fm", [128, 6, T], BF16)
            pBi = ExitStack()
            sbi = lambda name, shape, dty=F32: pBi.enter_context(nc.sbuf_tensor("sb_" + name, shape, dty))
            zbg = [sbi(f"zbg{i}", [128, T], BF16) for i in range(3)]
            num = [sbi(f"num{i}", [128, T], BF16) for i in range(3)]
            den = [sbi(f"den{i}", [128, T]) for i in range(3)]
            qb = sbi("qb", [128, T], BF16)
            kb = sbi("kb", [128, T], BF16)
            vp = [sbi(f"vp{i}", [128, 16, 2, 128], BF16) for i in range(2)]
            PTb = [sbi(f"PT{i}", [128, 512], BF16) for i in range(3)]
            wq2 = [sbi(f"wq{i}", [128, 8, 128], BF16) for i in range(2)]; wk2 = [sbi(f"wk{i}", [128, 8, 128], BF16) for i in range(2)]
            wv2 = [sbi(f"wv{i}", [128, 8, 128], BF16) for i in range(2)]; wz2 = [sbi(f"wz{i}", [128, 8, 128], BF16) for i in range(2)]
            tnhB = [sbi(f"tnhBi{i}", [128, 512]) for i in range(2)]
            dtot = sbi("dtot", [128, T])
            tmpB = [sbi(f"tmpBi{i}", [128, 512]) for i in range(2)]
            for i in range(2):
                k.op("pool", lambda e: e.memset(vp[i][:], 0.0), w=[("vp", i, blk) for blk in range(16)])
            onesP = (onesA, onesB)
            npt = 0
            nB = 0
            for p in range(2):
                for g in range(3):
                    pt = 2 * g + p
                    d = (1, 4, 16)[g]
                    nbk = 16 // d
                    vpi = npt % 2
                    npt += 1
                    vpt = vp[vpi]
                    wq, wk, wv, wz = wq2[vpi], wk2[vpi], wv2[vpi], wz2[vpi]
                    wqk, wkk, wvk, wzk = ("wq", vpi), ("wk", vpi), ("wv", vpi), ("wz", vpi)

                    def att_loads(pt_, vi_):
                        wload(wq2[vi_][:], ("wq", vi_), win_d[:, 4112 + pt_ * 128: 4112 + (pt_ + 1) * 128])
                        wload(wk2[vi_][:], ("wk", vi_), win_d[:, 4880 + pt_ * 128: 4880 + (pt_ + 1) * 128])
                        wload(wv2[vi_][:], ("wv", vi_), win_d[:, 5648 + pt_ * 128: 5648 + (pt_ + 1) * 128])
                        wload(wz2[vi_][:], ("wz", vi_), win_d[:, 6416 + pt_ * 128: 6416 + (pt_ + 1) * 128])
                    if npt == 1:
                        att_loads(pt, vpi)
                    nxt = [(p_, g_) for p_ in range(2) for g_ in range(3)]
                    ni_ = nxt.index((p, g)) + 1
                    if ni_ < len(nxt):
                        att_loads(2 * nxt[ni_][1] + nxt[ni_][0], 1 - vpi)
                    for tt4 in range(4):
                        cs = slice(tt4 * 512, (tt4 + 1) * 512)
                        b = k.bank()
                        proj_fm(wq, wqk, 0, tt4, b)
                        ACTV(qb[:, cs], psf[b][:, :], AF.Copy, r=bkeys(b), w=[("qb", tt4)])
                        b = k.bank()
                        proj_fm(wk, wkk, 0, tt4, b)
                        k.op("dve", lambda e: e.tensor_copy(out=kb[:, cs], in_=psf[b][:, :]), r=bkeys(b), w=[("kb", tt4)])
                        b = k.bank()
                        proj_fm(wz, wzk, 0, tt4, b)
                        tb = tnhB[tt4 % 2]
                        ACTV(tb[:], psf[b][:, :], AF.Tanh, r=bkeys(b), w=[("tnhB", tt4 % 2)], scale=0.5)
                        STT(zbg[g][:, cs], tb[:], 1.0, psf[b][:, :], ALU.add, ALU.mult, r=[("tnhB", tt4 % 2)] + bkeys(b), w=[("zbg", g, tt4)])

                    def tsl(i, r_):
                        s0 = 128 * i * d + r_
                        return slice(s0, s0 + 127 * d + 1, d), list(range(s0 // 128, (s0 + 127 * d) // 128 + 1))

                    for r_ in range(d):
                        for i in range(nbk):
                            blk = r_ * nbk + i
                            sl_, tiles = tsl(i, r_)
                            b = k.bank()
                            for kt in range(8):
                                MM(psf[b][:, 0:128], h_fm[:, kt, sl_], wv[:, kt, :], r=[wvk] + [("h", t_) for t_ in tiles],
                                   w=bkeys(b), start=(kt == 0), stop=(kt == 7), sig=(kt == 7))
                            ACTV(vpt[:, blk, 0, 0:64], psf[b][:, 0:64], AF.Copy, r=bkeys(b), w=[("vp", vpi, blk)])
                            k.op("dve", lambda e: e.tensor_copy(out=vpt[:, blk, 1, 64:128], in_=psf[b][:, 64:128]),
                                 r=bkeys(b), w=[("vp", vpi, blk)])
                    def scores_blk(r_, i):
                        nonlocal nB
                        qs, qtiles = tsl(i, r_)
                        qk_ = [("qb", t_ // 4) for t_ in qtiles]
                        kck = [("kb", t_ // 4) for t_ in qtiles]
                        if i > 0:
                            kps, ptiles = tsl(i - 1, r_)
                            kpk = [("kb", t_ // 4) for t_ in ptiles]
                        b = k.bank()
                        for hd in range(2):
                            prt = slice(hd * 64, hd * 64 + 64)
                            if i > 0:
                                o_ = psf[b][:, hd * 128:(hd + 1) * 128]
                                MM(o_, kb[prt, kps], qb[prt, qs], r=kpk + qk_, w=bkeys(b), stop=False, sig=False)
                                MM(o_, ident, NEGprev, r=["cb"], w=bkeys(b), start=False, sig=False)
                            o_ = psf[b][:, 256 + hd * 128: 256 + (hd + 1) * 128]
                            MM(o_, kb[prt, qs], qb[prt, qs], r=kck + qk_, w=bkeys(b), stop=False, sig=False)
                            MM(o_, ident, NEGcur, r=["cb"], w=bkeys(b), start=False, sig=(hd == 1))
                        pt_ = PTb[nB % 3]
                        ptk = ("PT", nB % 3)
                        nB += 1
                        c_lo = 0 if i > 0 else 256
                        ACTV(pt_[:, c_lo:512], psf[b][:, c_lo:512], AF.Exp, r=bkeys(b), w=[ptk], scale=0.125)
                        return pt_, ptk, qs

                    def pv_blk(r_, i, ctx):
                        pt_, ptk, qs = ctx
                        blk = r_ * nbk + i
                        terms = []
                        for hd in range(2):
                            if i > 0:
                                terms.append((hd * 128, blk - 1, hd))
                            terms.append((256 + hd * 128, blk, hd))
                        bo = k.bank()
                        for n2, (c0, bk_, hd) in enumerate(terms):
                            MM(psf[bo][:, 0:128], vpt[:, bk_, hd, :], pt_[:, c0:c0 + 128], r=[ptk, ("vp", vpi, bk_)], w=bkeys(bo),
                               start=(n2 == 0), stop=(n2 == len(terms) - 1), sig=False)
                        for n2, (c0, bk_, hd) in enumerate(terms):
                            MM(psf[bo][:, 128:256], onesP[hd], pt_[:, c0:c0 + 128], r=[ptk, "cb"], w=bkeys(bo),
                               start=(n2 == 0), stop=(n2 == len(terms) - 1), sig=(n2 == len(terms) - 1))
                        k.op("dve", lambda e: e.tensor_copy(out=num[g][:, qs], in_=psf[bo][:, 0:128]), r=bkeys(bo), w=[("num", g, blk)])
                        ACTV(den[g][:, qs], psf[bo][:, 128:256], AF.Copy, r=bkeys(bo), w=[("den", g, blk)])

                    blks = [(r_, i) for r_ in range(d) for i in range(nbk)]
                    ctxs = {0: scores_blk(*blks[0])}
                    for n_, (r_, i) in enumerate(blks):
                        if n_ + 1 < len(blks):
                            ctxs[n_ + 1] = scores_blk(*blks[n_ + 1])
                        pv_blk(r_, i, ctxs.pop(n_))
                allnd = [(nm, g_, blk) for nm in ("num", "den") for g_ in range(3) for blk in range(16)]
                for c4 in range(4):
                    cs = slice(c4 * 512, (c4 + 1) * 512)
                    TT("dve", dtot[:, cs], den[0][:, cs], den[1][:, cs], ALU.add, r=allnd, w=[("dtot", c4)])
                    TT("dve", dtot[:, cs], dtot[:, cs], den[2][:, cs], ALU.add, r=allnd, w=[("dtot", c4)])
                    ACTV(dtot[:, cs], dtot[:, cs], AF.Ln, r=[("dtot", c4)], w=[("dtot", c4)])
                    ACTV(dtot[:, cs], dtot[:, cs], AF.Exp, r=[("dtot", c4)], w=[("dtot", c4)], scale=-1.0)
                    for g in range(3):
                        tb = tmpB[(c4 * 3 + g) % 2]
                        tk = ("tmpB", (c4 * 3 + g) % 2)
                        STT(tb[:], num[g][:, cs], 0.5, dtot[:, cs], ALU.mult, ALU.mult, r=allnd + [("dtot", c4)], w=[tk])
                        TT("pool", yb_fm[:, 2 * g + p, cs], tb[:], zbg[g][:, cs], ALU.mult, r=[tk, ("zbg", g, c4)], w=[("yb", 2 * g + p, c4)])
            k.barrier()
            pBi.close()
            dump_fm("yb", yb_fm, 6)

            wo = sbx("wo", [128, 8, D], BF16)
            fnw = sbx("fnw", [128, D])
            wgb = [sbx(f"wgb{i}", [128, 8, 128], BF16) for i in range(2)]
            pbj = [sbx(f"pbj{i}", [128, 6, 128], BF16) for i in range(2)]
            tnhB = [sbx(f"tnhB{i}", [128, 512]) for i in range(2)]
            tmpB = [sbx(f"tmpB{i}", [128, 512]) for i in range(2)]
            wload(wgb[0][:], ("wgb", 0), win_d[:, 8208: 8208 + 128])
            wload(pbj[0][:], ("pbj", 0), pb_d[:, 0:128])
            k.dma("sp", fnw[:], fnw_d[:, :], w=["fnw"])
            wload(wo[:, :, 0:512], ("wo", 0), wo_d[:, 0:512])
            wload(wo[:, :, 512:1024], ("wo", 1), wo_d[:, 512:1024])
            n_ = 0
            for j in range(8):
                if j + 1 < 8:
                    wload(wgb[(j + 1) % 2][:], ("wgb", (j + 1) % 2), win_d[:, 8208 + (j + 1) * 128: 8208 + (j + 2) * 128])
                    wload(pbj[(j + 1) % 2][:], ("pbj", (j + 1) % 2), pb_d[:, (j + 1) * 128:(j + 2) * 128])
                TT("dve", wo[:, j, :], wo[:, j, :], gate[:], ALU.mult, r=[("wo", 0), ("wo", 1), "gate"], w=[("wofold", j)])
                for tt4 in range(4):
                    cs = slice(tt4 * 512, (tt4 + 1) * 512)
                    tb, tk = tnhB[n_ % 2], ("tnhB", n_ % 2)
                    t2b, t2k = tmpB[n_ % 2], ("tmpB", n_ % 2)
                    n_ += 1
                    b1 = k.bank()
                    proj_fm(wgb[j % 2], ("wgb", j % 2), 0, tt4, b1)
                    ACTV(tb[:], psf[b1][:, :], AF.Tanh, r=bkeys(b1), w=[tk], scale=0.5)
                    b2 = k.bank()
                    for pt in range(6):
                        MM(psf[b2][:, :], pbj[j % 2][:, pt, :], yb_fm[:, pt, cs], r=[("pbj", j % 2), ("yb", pt, tt4)], w=bkeys(b2),
                           start=(pt == 0), stop=(pt == 5), sig=(pt == 5))
                    STT(t2b[:], tb[:], 1.0, psf[b2][:, :], ALU.add, ALU.mult, r=[tk] + bkeys(b2), w=[t2k])
                    TT("pool", ma_fm[:, j, cs], ma_fm[:, j, cs], t2b[:], ALU.add, r=[t2k, ("ma", j, tt4)], w=[("ma", j, tt4)])
            k.barrier()
        dump_fm("merged", ma_fm, 8)

        with ExitStack() as pC:
            sbx = lambda name, shape, dty=F32: pC.enter_context(nc.sbuf_tensor("sb_" + name, shape, dty))
            xin = [sbx(f"xin{i}", [128, D]) for i in range(3)]
            xo = [sbx(f"xo{i}", [128, D]) for i in range(3)]
            outt = [sbx(f"outt{i}", [128, D]) for i in range(3)]
            junkc = sbx("junkc", [128, D], BF16)
            ssc = sbx("ssc", [128, NT]); rsc = sbx("rsc", [128, NT])
            k.op("dve", lambda e: e.memset(ssc[:], 0.0), w=["ssc"])
            for tt in range(NT):
                i2 = tt % 3
                k.dma("sp", xin[i2][:], x_d[tt * 128:(tt + 1) * 128, :], w=[("xin", i2)])
                for nh in range(2):
                    b = k.bank()
                    cs = slice(nh * 512, (nh + 1) * 512)
                    for kt in range(8):
                        MM(psf[b][:, :], ma_fm[:, kt, tt * 128:(tt + 1) * 128], wo[:, kt, cs], r=[("wo", nh)], w=bkeys(b),
                           start=(kt == 0), stop=(kt == 7), sig=(kt == 7))
                    TT("dve", xo[i2][:, cs], psf[b][:, :], xin[i2][:, cs], ALU.add, r=bkeys(b) + [("xin", i2)], w=[("xo", i2, nh)])
                k.op("act", lambda e: e.activation(out=junkc[:], in_=xo[i2][:], func=AF.Square, accum_out=ssc[:, tt:tt + 1]),
                     r=[("xo", i2, 0), ("xo", i2, 1), "ssc"], w=["junkc", ("ssc", tt)])
                ACTV(rsc[:, tt:tt + 1], ssc[:, tt:tt + 1], AF.Ln, r=[("ssc", tt)], w=[("rsc", tt)], scale=1.0 / D, bias=epsc[:, 0:1])
                ACTV(rsc[:, tt:tt + 1], rsc[:, tt:tt + 1], AF.Exp, r=[("rsc", tt)], w=[("rsc", tt)], scale=-0.5)
                STT(outt[i2][:], xo[i2][:], rsc[:, tt:tt + 1], fnw[:], ALU.mult, ALU.mult,
                    r=[("xo", i2, 0), ("xo", i2, 1), ("rsc", tt), "fnw"], w=[("outt", i2)])
                k.dma("pool", out_d[tt * 128:(tt + 1) * 128, :], outt[i2][:], r=[("outt", i2)])
            k.barrier()
        pB.close()
        k.barrier()
        for e in ("sp",):
            for c in k.E[e].ring:
                k.wait(k.E[e], c, c.count)
    return nc


def make_consts():
    i = np.arange(128)
    e, c = i[:, None], i[None, :]
    cf = np.zeros((128, 768), np.float32)
    cf[:, 0:128] = 1.0
    cf[:, 128:256] = (e <= c)
    cf[:, 256:384] = (e > c)
    cf[:, 384:512] = np.eye(128)
    cf[:, 512:640] = np.where(e <= c, 0.0, NEG)
    cf[:, 640:768] = np.where(e < c, 0.0, NEG)
    cb = np.zeros((128, 1664), np.float32)
    cb[:, 0:128] = np.eye(128)
    cb[:, 128:256] = 1.0
    cb[:, 256:384] = np.where(c <= e, 0.0, NEG)
    cb[:, 384:512] = np.where(c >= e, 0.0, NEG)
    cb[:, 512:640] = (c < 64)
    cb[:, 640:768] = (c >= 64)
    for n_, b_ in enumerate((1, 2, 4, 8, 16, 32, 64)):
        cb[:, 768 + n_ * 128: 768 + (n_ + 1) * 128] = ((e // b_) % 2 == 0) & (c // b_ == e // b_ + 1)
    return cf, cb.astype(ml_dtypes.bfloat16)


def make_in_maps(inp):
    cf, cb = make_consts()
    f = lambda a: np.ascontiguousarray(a, dtype=np.float32)
    rep = lambda v: f(np.broadcast_to(np.asarray(v).reshape(1, -1), (128, np.asarray(v).size)))
    shared = {
        "normw_bc": rep(inp["norm_w"][0]), "ada_w": f(inp["ada_w"][0]), "adab_bc": rep(inp["ada_b"][0]),
        "w_in": f(inp["w_in"][0]),
        "convw": f(np.asarray(inp["conv_w"][0]).reshape(4, 24, 128).transpose(2, 1, 0)),
        "alog_bc": rep(inp["a_log"][0]), "dtb_bc": rep(inp["dt_bias"][0]),
        "dnw_col": f(np.asarray(inp["dn_norm_w"][0]).reshape(128, 1)),
        "w_proj_a": f(inp["w_proj_a"][0]), "w_proj_b": f(inp["w_proj_b"][0]), "w_out": f(inp["w_out"][0]),
        "fnw_bc": rep(inp["final_norm_w"]), "cf": cf, "cb": cb,
    }
    maps = []
    for b in range(8):
        m = dict(shared)
        m["x"] = f(inp["x"][b])
        m["cT"] = f(np.asarray(inp["c"][b]).reshape(8, 128).T)
        maps.append(m)
    return maps


def kernel(**inputs):
    nc = build_program()
    maps = make_in_maps(inputs)
    res = run_bass_kernel_spmd(nc, maps, core_ids=list(range(8)))
    return np.stack([np.asarray(r["out"], dtype=np.float32) for r in res.results], axis=0)
```

```python
import numpy as np
import ml_dtypes
from contextlib import ExitStack
import concourse.bass as bass
import concourse.mybir as mybir
from concourse.bass_utils import run_bass_kernel_spmd

F32 = mybir.dt.float32
BF16 = mybir.dt.bfloat16
AF = mybir.ActivationFunctionType
ALU = mybir.AluOpType
AX = mybir.AxisListType

T = 2048
D = 1024
NT = 16
NIN = 9232
EPS = 1e-6
NEG = -30000.0
DEBUG = False
ATTACH_WAIT = True
import os
STOP = int(os.environ.get("KSTOP", "9"))


class Ctr:
    def __init__(self, sem):
        self.sem = sem
        self.count = 0


class Eng:
    def __init__(self, name, eng, ctr):
        self.name, self.eng, self.ctr = name, eng, ctr
        self.seen = {}
        self.ring = []
        self.ri = 0


class K:
    def __init__(self, nc, st):
        self.nc, self.st = nc, st
        self.E = {}
        for name, e in (("pe", nc.tensor), ("act", nc.scalar), ("dve", nc.vector),
                        ("pool", nc.gpsimd), ("sp", nc.sync)):
            self.E[name] = Eng(name, e, Ctr(st.enter_context(nc.semaphore("s_" + name))))
        for name, n in (("sp", 12), ("pool", 6)):
            self.E[name].ring = [Ctr(st.enter_context(nc.semaphore(f"d_{name}{i}"))) for i in range(n)]
        self.lastw, self.readers = {}, {}
        self.nbank = 0
        self.nq = 0
        self.banks = [0, 1, 2, 3]
        self.quarters = [(b, 0) for b in (4, 5)]
        self.qfreelist = [0, 1, 2, 3, 4, 5, 6, 7]

    def wait(self, E, ctr, val):
        if val <= 0 or (ctr is E.ctr and E.name in ("pe", "sp")):
            return
        if E.seen.get(id(ctr), 0) >= val:
            return
        E.eng.wait_ge(ctr.sem, val)
        E.seen[id(ctr)] = val

    def _deps(self, E, r, w):
        deps = {}

        def add(cv):
            c, v = cv
            if id(c) not in deps or deps[id(c)][1] < v:
                deps[id(c)] = (c, v)
        for k in r:
            if k in self.lastw:
                add(self.lastw[k])
        for k in w:
            if k in self.lastw:
                add(self.lastw[k])
            for cv in self.readers.get(k, {}).values():
                add(cv)
        need = []
        for c, v in deps.values():
            if v <= 0 or (c is E.ctr and E.name in ("pe", "sp")):
                continue
            if E.seen.get(id(c), 0) >= v:
                continue
            need.append((c, v))
        if not ATTACH_WAIT:
            for c, v in need:
                self.wait(E, c, v)
            return None
        for c, v in need[:-1]:
            self.wait(E, c, v)
        if need:
            c, v = need[-1]
            E.seen[id(c)] = v
            return (c, v)
        return None

    def _mark(self, ctr, val, r, w):
        for k in w:
            self.lastw[k] = (ctr, val)
            self.readers[k] = {}
        for k in r:
            self.readers.setdefault(k, {})[id(ctr)] = (ctr, val)

    @staticmethod
    def _norm(r, w):
        isps = lambda x: isinstance(x, tuple) and x[0] in ("ps", "psb")
        nb = lambda x: (x[0], x[1])
        w2 = [nb(x) if isps(x) else x for x in w] + [nb(x) for x in r if isps(x)]
        r2 = [x for x in r if not isps(x)]
        return r2, list(dict.fromkeys(w2))

    def op(self, en, fn, r=(), w=(), sig=True):
        E = self.E[en]
        r, w = self._norm(r, w)
        att = self._deps(E, r, w)
        inst = fn(E.eng)
        if att is not None:
            inst._wait_ge(att[0].sem, att[1])
        val = E.ctr.count + 1
        if sig:
            inst.then_inc(E.ctr.sem, 1)
            E.ctr.count = val
        self._mark(E.ctr, val, r, w)
        return inst

    def dma(self, qn, out, in_, r=(), w=()):
        E = self.E[qn]
        att = self._deps(E, r, w)
        if att is not None:
            E.eng.wait_ge(att[0].sem, att[1])
        slot = E.ring[E.ri % len(E.ring)]
        E.ri += 1
        self.wait(E, slot, slot.count)
        E.eng.dma_start(out=out, in_=in_).then_inc(slot.sem, 16)
        slot.count += 16
        self._mark(slot, slot.count, r, w)

    def barrier(self):
        ctrs = [e.ctr for e in self.E.values()]
        for e in self.E.values():
            ctrs += e.ring
        for e in self.E.values():
            for c in ctrs:
                self.wait(e, c, c.count)
        self.lastw, self.readers = {}, {}

    def bank(self):
        b = self.banks[self.nbank % len(self.banks)]
        self.nbank += 1
        return b

    def qalloc(self, n):
        if len(self.qfreelist) < n:
            return None
        out = [(self.qfreelist.pop(0), 0) for _ in range(n)]
        return out

    def qfree(self, *bqs):
        for bq in bqs:
            self.qfreelist.append(bq[0])

    def quarter(self):
        bq = self.quarters[self.nq % len(self.quarters)]
        self.nq += 1
        return bq


def bkeys(b):
    return [("ps", b, q) for q in range(4)]


def build_program():
    nc = bass.Bass("TRN2", target_bir_lowering=False)
    dt = lambda name, shape, dty, kind: nc.dram_tensor(name, shape, dty, kind=kind).ap()
    x_d = dt("x", [T, D], F32, "ExternalInput")
    cT_d = dt("cT", [128, 8], F32, "ExternalInput")
    normw_d = dt("normw_bc", [128, D], F32, "ExternalInput")
    adaw_d = dt("ada_w", [D, 3 * D], F32, "ExternalInput")
    adab_d = dt("adab_bc", [128, 3 * D], F32, "ExternalInput")
    win_d = dt("w_in", [D, NIN], F32, "ExternalInput")
    convw_d = dt("convw", [128, 24, 4], F32, "ExternalInput")
    alog_d = dt("alog_bc", [128, 8], F32, "ExternalInput")
    dtb_d = dt("dtb_bc", [128, 8], F32, "ExternalInput")
    dnw_d = dt("dnw_col", [128, 1], F32, "ExternalInput")
    pa_d = dt("w_proj_a", [D, D], F32, "ExternalInput")
    pb_d = dt("w_proj_b", [768, D], F32, "ExternalInput")
    wo_d = dt("w_out", [D, D], F32, "ExternalInput")
    fnw_d = dt("fnw_bc", [128, D], F32, "ExternalInput")
    cf_d = dt("cf", [128, 768], F32, "ExternalInput")
    cb_d = dt("cb", [128, 1664], BF16, "ExternalInput")
    out_d = dt("out", [T, D], F32, "ExternalOutput")
    dbg_d = dt("dbg", [128, 8 * T], F32, "ExternalOutput") if DEBUG else None

    with ExitStack() as st:
        k = K(nc, st)
        sb = lambda name, shape, dty=F32: st.enter_context(nc.sbuf_tensor("sb_" + name, shape, dty))
        psf = [st.enter_context(nc.psum_tensor(f"psf{i}", [128, 512], F32)) for i in range(6)]
        psb = [st.enter_context(nc.psum_tensor(f"psb{i}", [128, 1024], BF16)) for i in range(2)]
        psf = psf + [psb[i][:].bitcast(F32) for i in range(2)]

        def psq(bq):
            b, q = bq
            return psf[b][:, q * 128:(q + 1) * 128]

        cf = sb("cf", [128, 768])
        cb = sb("cb", [128, 1664], BF16)
        k.dma("sp", cf[:], cf_d[:, :], w=["cf"])
        k.dma("sp", cb[:], cb_d[:, :], w=["cb"])
        ones_f, TLE, TGT, ident_f, mU_incl, mU_strict = [cf[:, i * 128:(i + 1) * 128] for i in range(6)]
        (ident, ones_b, NEGprev, NEGcur, onesA, onesB) = [cb[:, i * 128:(i + 1) * 128] for i in range(6)]
        offall = cb[:, 6 * 128:13 * 128]
        OFFI = {1: 0, 2: 1, 4: 2, 8: 3, 16: 4, 32: 5, 64: 6}

        gate = sb("gate", [128, D])
        epsc = sb("epsc", [128, 4])
        k.op("dve", lambda e: e.memset(epsc[:, 0:1], EPS), w=["epsc"])
        k.op("dve", lambda e: e.memset(epsc[:, 1:2], 4 * EPS), w=["epsc"])
        k.op("dve", lambda e: e.memset(epsc[:, 2:3], 1.0), w=["epsc"])
        k.barrier()
        h_fm = sb("h_fm", [128, 8, T], BF16)
        p01 = ExitStack()
        mod = p01.enter_context(nc.sbuf_tensor("sb_mod", [128, 3 * D], F32))
        a_bc = p01.enter_context(nc.sbuf_tensor("sb_a_bc", [128, D], F32))

        p0 = ExitStack()
        if True:
            sb0 = lambda name, shape, dty=F32: p0.enter_context(nc.sbuf_tensor("sb_" + name, shape, dty))
            ss = sb0("ss1", [128, NT])
            rs = sb0("rs1", [128, NT])
            k.op("pool", lambda e: e.memset(ss[:], 0.0), w=["ss"])
            cT = sb0("cT", [128, 8])
            th = sb0("c_th", [128, 8])
            sc = sb0("c_sc", [128, 8])
            screp = sb0("screp", [128, 8, 128], BF16)
            normw = sb0("normw", [128, D])
            adab = sb0("adab", [128, 3 * D])
            aw = [sb0(f"aw{i}", [128, 8, 512], BF16) for i in range(2)]
            k.dma("sp", cT[:], cT_d[:, :], w=["cT"])
            k.dma("sp", normw[:], normw_d[:, :], w=["normw"])
            k.dma("sp", adab[:], adab_d[:, :], w=["adab"])
            k.op("act", lambda e: e.activation(out=th[:], in_=cT[:], func=AF.Tanh, scale=0.5), r=["cT"], w=["th"])
            k.op("dve", lambda e: e.scalar_tensor_tensor(out=sc[:], in0=th[:], scalar=1.0, in1=cT[:],
                                                         op0=ALU.add, op1=ALU.mult), r=["th", "cT"], w=["sc"])
            k.op("dve", lambda e: e.tensor_scalar(out=sc[:], in0=sc[:], scalar1=0.5, scalar2=None, op0=ALU.mult),
                 r=["sc"], w=["sc"])
            for kt in range(8):
                k.op("dve", lambda e: e.tensor_copy(out=screp[:, kt, :], in_=sc[:, kt:kt + 1].to_broadcast([128, 128])),
                     r=["sc"], w=["screp"])
            def ada_block(blk):
                a = aw[blk % 2]
                b = k.bank()
                for kt in range(8):
                    k.op("pe", lambda e: e.matmul(psf[b][:, :], lhsT=screp[:, kt, :], rhs=a[:, kt, :],
                                                  start=(kt == 0), stop=(kt == 7)),
                         r=["screp", ("aw", blk % 2)], w=bkeys(b), sig=(kt == 7))
                k.op("dve", lambda e: e.tensor_tensor(out=mod[:, blk * 512:(blk + 1) * 512], in0=psf[b][:, :],
                                                      in1=adab[:, blk * 512:(blk + 1) * 512], op=ALU.add),
                     r=bkeys(b) + ["adab"], w=[("mod", blk)])

            for blk in range(4):
                k.dma("pool", aw[blk % 2][:], adaw_d[:, blk * 512:(blk + 1) * 512].rearrange("(kt p) n -> p kt n", p=128),
                      w=[("aw", blk % 2)])
                ada_block(blk)
            for blk in (4, 5):
                k.dma("pool", aw[blk % 2][:], adaw_d[:, blk * 512:(blk + 1) * 512].rearrange("(kt p) n -> p kt n", p=128),
                      w=[("aw", blk % 2)])
            k.op("dve", lambda e: e.scalar_tensor_tensor(out=a_bc[:], in0=mod[:, D:2 * D], scalar=1.0, in1=normw[:],
                                                         op0=ALU.add, op1=ALU.mult),
                 r=[("mod", 2), ("mod", 3), "normw"], w=["a_bc"])
            if DEBUG == "mod":
                k.dma("sp", dbg_d[:, 0:3 * D], mod[:], r=[("mod", i) for i in range(6)])
                k.dma("sp", dbg_d[:, 3 * D:4 * D], a_bc[:], r=["a_bc"])
                k.dma("sp", dbg_d[:, 4 * D:4 * D + 8], sc[:], r=["sc"])
                k.dma("sp", dbg_d[:, 4 * D + 8:4 * D + 16], cT[:], r=["cT"])
                k.dma("sp", dbg_d[:, 4 * D + 16:4 * D + 24], th[:], r=["th"])

        with ExitStack() as p1:
            sb1 = lambda name, shape, dty=F32: p1.enter_context(nc.sbuf_tensor("sb_" + name, shape, dty))
            xt = [sb1(f"xt{i}", [128, D]) for i in range(4)]
            junk = [sb1(f"junk1{i}", [128, D], BF16) for i in range(4)]
            hn = [sb1(f"hn{i}", [128, D]) for i in range(4)]
            ht = [sb1(f"ht{i}", [128, D], BF16) for i in range(4)]
            tbank = [(psb[0][:], ("psb", 0)), (psb[1][:], ("psb", 1)),
                     (psf[4][:].bitcast(BF16), ("ps", 4)), (psf[5][:].bitcast(BF16), ("ps", 5))]

            def tile_gen(tt, bi):
                xb, hb, hb2, jk = xt[bi], hn[bi], ht[bi], junk[bi]
                k.dma("sp", xb[:], x_d[tt * 128:(tt + 1) * 128, :], w=[("xt", bi)])
                k.op("act", lambda e: e.activation(out=jk[:], in_=xb[:], func=AF.Square, accum_out=ss[:, tt:tt + 1]),
                     r=[("xt", bi), "ss"], w=[("junk", bi), ("ss", tt)])
                yield
                k.op("act", lambda e: e.activation(out=rs[:, tt:tt + 1], in_=ss[:, tt:tt + 1], func=AF.Ln, scale=1.0 / D, bias=epsc[:, 0:1]),
                     r=[("ss", tt)], w=[("rs", tt)])
                k.op("act", lambda e: e.activation(out=rs[:, tt:tt + 1], in_=rs[:, tt:tt + 1], func=AF.Exp, scale=-0.5),
                     r=[("rs", tt)], w=[("rs", tt)])
                yield
                k.op("dve", lambda e: e.scalar_tensor_tensor(out=hb[:], in0=xb[:], scalar=rs[:, tt:tt + 1], in1=a_bc[:],
                                                             op0=ALU.mult, op1=ALU.mult),
                     r=[("xt", bi), ("rs", tt), "a_bc"], w=[("hn", bi)])
                yield
                k.op("pool", lambda e: e.tensor_tensor(out=hb2[:], in0=hb[:], in1=mod[:, 0:D], op=ALU.add),
                     r=[("hn", bi), ("mod", 0), ("mod", 1)], w=[("ht", bi)])
                yield
                tb_, tk_ = tbank[tt % 4]
                for kt in range(8):
                    k.op("pe", lambda e: e.transpose(tb_[:, kt * 128:(kt + 1) * 128], hb2[:, kt * 128:(kt + 1) * 128], ident),
                         r=[("ht", bi), "cb"], w=[tk_], sig=(kt == 7))
                yield
                if tt % 2 == 0:
                    k.op("act", lambda e: e.activation(out=h_fm[:, :, tt * 128:(tt + 1) * 128],
                                                       in_=tb_.rearrange("p (k t) -> p k t", k=8), func=AF.Copy),
                         r=[tk_], w=[("h", tt)])
                else:
                    k.op("dve", lambda e: e.tensor_copy(out=h_fm[:, :, tt * 128:(tt + 1) * 128], in_=tb_.rearrange("p (k t) -> p k t", k=8)),
                         r=[tk_], w=[("h", tt)])

            tl_todo = list(range(NT))
            tl_act = [None] * 4
            while True:
                prog = False
                for bi in range(4):
                    if tl_act[bi] is None and tl_todo:
                        tl_act[bi] = tile_gen(tl_todo.pop(0), bi)
                    if tl_act[bi] is not None:
                        prog = True
                        try:
                            next(tl_act[bi])
                        except StopIteration:
                            tl_act[bi] = None
                if not prog:
                    break
            ada_block(4)
            ada_block(5)
            k.op("dve", lambda e: e.tensor_scalar(out=gate[:], in0=mod[:, 2 * D:3 * D], scalar1=0.5,
                                                  scalar2=None, op0=ALU.mult),
                 r=[("mod", 4), ("mod", 5)], w=["gate"])
            k.barrier()
        p0.close()
        p01.close()

        if DEBUG == "h":
            dbg32 = sb("dbg32", [128, T])
            for kt in range(8):
                k.op("dve", lambda e: e.tensor_copy(out=dbg32[:], in_=h_fm[:, kt, :]), r=[("h", i) for i in range(NT)], w=["dbg32"])
                k.dma("sp", dbg_d[:, kt * T:(kt + 1) * T], dbg32[:], r=["dbg32"])

        hkeys = [("h", i) for i in range(NT)]

        def proj_fm(wt, wkey, wcol, tt4, b):
            for kt in range(8):
                k.op("pe", lambda e: e.matmul(psf[b][:, :], lhsT=wt[:, kt, wcol:wcol + 128],
                                              rhs=h_fm[:, kt, tt4 * 512:(tt4 + 1) * 512],
                                              start=(kt == 0), stop=(kt == 7)),
                     r=[wkey] + hkeys[tt4 * 4:tt4 * 4 + 4], w=bkeys(b), sig=(kt == 7))

        def TT(en, out, a, b, op, r, w):
            return k.op(en, lambda e: e.tensor_tensor(out=out, in0=a, in1=b, op=op), r=r, w=w)

        def TS(en, out, a, s1, op0, r, w, s2=None, op1=None):
            if op1 is None:
                return k.op(en, lambda e: e.tensor_scalar(out=out, in0=a, scalar1=s1, scalar2=None, op0=op0), r=r, w=w)
            return k.op(en, lambda e: e.tensor_scalar(out=out, in0=a, scalar1=s1, scalar2=s2, op0=op0, op1=op1), r=r, w=w)

        def STT(out, a, s, b, op0, op1, r, w):
            return k.op("dve", lambda e: e.scalar_tensor_tensor(out=out, in0=a, scalar=s, in1=b, op0=op0, op1=op1), r=r, w=w)

        def ACTV(out, in_, func, r, w, scale=1.0, bias=None):
            if bias is None:
                return k.op("act", lambda e: e.activation(out=out, in_=in_, func=func, scale=scale), r=r, w=w)
            return k.op("act", lambda e: e.activation(out=out, in_=in_, func=func, scale=scale, bias=bias), r=r, w=w)

        def MM(out, lhsT, rhs, r, w, start=True, stop=True, sig=True):
            return k.op("pe", lambda e: e.matmul(out, lhsT=lhsT, rhs=rhs, start=start, stop=stop), r=r, w=w, sig=sig)

        def DVT(out, in_, r, w):
            return k.op("dve", lambda e: e.transpose(out=out, in_=in_), r=r, w=w)

        def wload(dst, key, src_cols, nkt=8):
            k.dma("pool", dst, src_cols.rearrange("(kt p) n -> p kt n", p=128), w=[key])

        ya_fm = sb("ya_fm", [128, 8, T], BF16)
        ma_fm = ya_fm
        RS = float(128 ** -0.5)

        k.banks = [0, 1]
        k.quarters = [(b, 0) for b in (2, 3, 4, 5, 0, 1)]
        with ExitStack() as pA:
          if os.environ.get('KSKIPA') != '1':
              sbA = lambda name, shape, dty=F32: pA.enter_context(nc.sbuf_tensor("sb_" + name, shape, dty))
              convw = sbA("convw", [128, 24, 4])
              alog = sbA("alog", [128, 8]); dtb = sbA("dtb", [128, 8]); dnw = sbA("dnw", [128, 1])
              k.dma("sp", convw[:], convw_d[:, :, :], w=["convw"])
              k.dma("sp", alog[:], alog_d[:, :], w=["alog"])
              k.dma("sp", dtb[:], dtb_d[:, :], w=["dtb"])
              k.dma("sp", dnw[:], dnw_d[:, :], w=["dnw"])
              TS("dve", dnw[:], dnw[:], 0.5, ALU.mult, r=["dnw"], w=["dnw"])
              wba = sbA("wba", [128, 8, 128], BF16)
              if os.environ.get("KX") != "2":
                  wload(wba[:], "wba", win_d[:, 3984:4112])
              else:
                  k.op("pool", lambda e: e.memset(wba[:], 0.0), w=["wba"])
              ba = sbA("ba", [128, NT, 16])
              beta = sbA("beta", [128, NT, 8]); lnb = sbA("lnb", [128, NT, 8]); gt = sbA("gt", [128, NT, 8])
              ngc = sbA("ngc", [128, NT, 8]); egc = sbA("egc", [128, NT, 8]); ekd = sbA("ekd", [128, NT, 8])
              bg = sbA("bg", [128, NT, 8]); ea = sbA("ea", [128, 8])
              for tt in range(NT):
                  q_ = k.quarter()
                  for kt in range(8):
                      MM(psq(q_)[:, 0:16], h_fm[:, kt, tt * 128:(tt + 1) * 128], wba[:, kt, 112:128], r=["wba", ("h", tt)],
                         w=[("ps",) + q_], start=(kt == 0), stop=(kt == 7), sig=(kt == 7))
                  k.op("dve", lambda e: e.tensor_copy(out=ba[:, tt, :], in_=psq(q_)[:, 0:16]), r=[("ps",) + q_], w=[("ba", tt)])
              ACTV(ea[:], alog[:], AF.Exp, r=["alog"], w=["ea"])
              allk = [("col", tt) for tt in range(NT)]
              bak = [("ba", tt) for tt in range(NT)]
              F2 = lambda t_: t_[:].rearrange("p a b -> p (a b)")
              bc8 = lambda t_: t_[:].unsqueeze(1).to_broadcast([128, NT, 8])
              ACTV(beta[:], ba[:, :, 0:8], AF.Tanh, r=bak, w=allk, scale=0.5)
              TS("dve", F2(beta), F2(beta), 0.5, ALU.mult, r=allk, w=allk, s2=0.5, op1=ALU.add)
              TT("dve", gt[:], ba[:, :, 8:16], bc8(dtb), ALU.add, r=bak + ["dtb"], w=allk)
              ACTV(F2(lnb), F2(beta), AF.Ln, r=allk, w=allk)
              ACTV(F2(gt), F2(gt), AF.Exp, r=allk, w=allk)
              ACTV(F2(gt), F2(gt), AF.Ln, r=allk, w=allk, bias=epsc[:, 2:3])
              TS("dve", F2(gt), F2(gt), -1.0, ALU.mult, r=allk, w=allk)
              TT("dve", gt[:], gt[:], bc8(ea), ALU.mult, r=allk + ["ea"], w=allk)
              q1, q2 = k.quarter(), k.quarter()
              MM(psq(q1), TLE, F2(gt), r=allk + ["cf"], w=[("ps",) + q1])
              MM(psq(q2), TGT, F2(gt), r=allk + ["cf"], w=[("ps",) + q2])
              TS("dve", F2(ngc), psq(q1), -1.0, ALU.mult, r=[("ps",) + q1], w=allk)
              ACTV(F2(egc), psq(q1), AF.Exp, r=[("ps",) + q1], w=allk)
              ACTV(F2(ekd), psq(q2), AF.Exp, r=[("ps",) + q2], w=allk)
              TT("dve", F2(bg), F2(beta), F2(egc), ALU.mult, r=allk, w=allk)

              q_fm2 = sbA("q_fm2", [128, 2, T], BF16)
              k_fm2 = sbA("k_fm2", [128, 2, T], BF16)
              k_tok2 = sbA("k_tok2", [128, NT, 2, 128], BF16)
              v_tok2 = sbA("v_tok2", [128, NT, 2, 128], BF16)
              q_fm = [q_fm2[:, i, :] for i in range(2)]
              k_fm = [k_fm2[:, i, :] for i in range(2)]
              k_tok = [k_tok2[:, :, i, :] for i in range(2)]
              v_tok = [v_tok2[:, :, i, :] for i in range(2)]
              zg2 = sbA("zg2", [128, 2, T], BF16)
              zg = [zg2[:, i, :] for i in range(2)]
              S_f2 = sbA("S_f2", [128, 2, 128])
              S_b2 = sbA("S_b2", [128, 2, 128], BF16)
              S_f = [S_f2[:, i, :] for i in range(2)]
              S_b = [S_b2[:, i, :] for i in range(2)]
              RT = int(os.environ.get('KRT', '8'))
              NB = int(os.environ.get('KNB', '4'))
              UDELAY = int(os.environ.get('KUDELAY', '4'))
              BSTAG = int(os.environ.get('KBSTAG', '4'))
              wk_p = sbA("wk_p", [128, 8, 256], BF16)
              wload(wk_p[:], ("wblk", 1), win_d[:, 1024: 1024 + 256])
              npsb = [0]

              for gi in range(4 if STOP >= 2 else 0):
                  h0 = 2 * gi
                  sA1 = ExitStack()
                  sb1_ = lambda name, shape, dty=F32: sA1.enter_context(nc.sbuf_tensor(f"sb_{name}_g{gi}", shape, dty))
                  k.banks = [0, 1, 2, 3, 4, 5]
                  wblk = [sb1_(f"wblk{i}", [128, 8, 256], BF16) for i in range(3)]
                  wblk = [wblk[0], wk_p, wblk[1], wblk[2]]
                  for j, base in enumerate((0, 1024, 2048, 3072)):
                      if j != 1:
                          wload(wblk[j][:], ("wblk", j), win_d[:, base + h0 * 128: base + h0 * 128 + 256])
                  pre = [sb1_(f"pre{i}", [128, 3 + T], BF16) for i in range(2)]
                  dwg = sb1_("dwg", [128, 6, 4, 128], BF16)
                  for kd_ in range(3):
                      TT("pool", dwg[:, 2 * kd_:2 * kd_ + 2, :, :], ident.unsqueeze(1).unsqueeze(1).to_broadcast([128, 2, 4, 128]),
                         convw[:, kd_ * 8 + h0: kd_ * 8 + h0 + 2, :].unsqueeze(3).to_broadcast([128, 2, 4, 128]), ALU.mult,
                         r=["cb", "convw"], w=["dwg"])
                  accs = [[sb1_(f"acc{u}{i}", [128, 512]) for i in range(4)] for u in range(2)]
                  tnhs = [[sb1_(f"tnh{u}{i}", [128, 512]) for i in range(4)] for u in range(2)]
                  sqbs = [[sb1_(f"sqb{u}{i}", [128, 512], BF16) for i in range(4)] for u in range(2)]
                  for u in range(2):
                      k.op("pool", lambda e: e.memset(pre[u][:, 0:3], 0.0), w=[("prehalo", u)])

                  def unit_gen(kind, hh, ub, delay=0):
                      for _ in range(delay):
                          yield
                      ci = kind * 8 + h0 + hh
                      pre_ = pre[ub]
                      for tt4 in range(4):
                          b = k.bank()
                          proj_fm(wblk[kind], ("wblk", kind), hh * 128, tt4, b)
                          k.op("dve", lambda e: e.tensor_copy(out=pre_[:, 3 + tt4 * 512: 3 + (tt4 + 1) * 512], in_=psf[b][:, :]),
                               r=bkeys(b), w=[("pre", ub, tt4)])
                          yield
                      subs = [sub_gen(kind, hh, ub, ci, pre_, tt4) for tt4 in range(4)]
                      while subs:
                          for sg_ in list(subs):
                              try:
                                  next(sg_)
                              except StopIteration:
                                  subs.remove(sg_)
                          yield

                  def sub_gen(kind, hh, ub, ci, pre_, tt4):
                      if True:
                          sx = tt4
                          acc, tnh, sqb = accs[ub][sx], tnhs[ub][sx], sqbs[ub][sx]
                          ka, kt_, ks = ("acc", ub, sx), ("tnh", ub, sx), ("sqb", ub, sx)
                          prek = [("pre", ub, tt4)] + ([("pre", ub, tt4 - 1)] if tt4 else [("prehalo", ub)])
                          c0 = tt4 * 512
                          ui_ = kind * 2 + hh
                          b = k.bank()
                          for j in range(4):
                              MM(psf[b][:, :], dwg[:, ui_, j, :], pre_[:, c0 + j:c0 + j + 512], r=prek + ["dwg"], w=bkeys(b),
                                 start=(j == 0), stop=(j == 3), sig=(j == 3))
                          ACTV(tnh[:], psf[b][:, :], AF.Tanh, r=bkeys(b), w=[kt_], scale=0.5)
                          if kind < 2:
                              STT(acc[:], tnh[:], 1.0, psf[b][:, :], ALU.add, ALU.mult, r=[kt_] + bkeys(b), w=[ka])
                          else:
                              STT(sqb[:], tnh[:], 1.0, psf[b][:, :], ALU.add, ALU.mult, r=[kt_] + bkeys(b), w=[ks])
                          yield
                          if kind < 2:
                              dst = (q_fm, k_fm)[kind][hh]
                              dk_ = (("qfm", hh, tt4) if kind == 0 else ("kfm", hh, tt4))
                              ACTV(sqb[:], acc[:], AF.Square, r=[ka], w=[ks])
                              yield
                              b = k.bank()
                              MM(psf[b][:, :], ones_b, sqb[:], r=["cb", ks], w=bkeys(b))
                              ACTV(tnh[:], psf[b][:, :], AF.Ln, r=bkeys(b), w=[kt_], bias=epsc[:, 1:2])
                              yield
                              ACTV(tnh[:], tnh[:], AF.Exp, r=[kt_], w=[kt_], scale=-0.5)
                              yield
                              if kind == 0:
                                  STT(dst[:, c0:c0 + 512], acc[:], RS, tnh[:], ALU.mult, ALU.mult, r=[ka, kt_], w=[dk_])
                              else:
                                  TT("dve", dst[:, c0:c0 + 512], acc[:], tnh[:], ALU.mult, r=[ka, kt_], w=[dk_])
                              yield
                              if kind == 1:
                                  pb_ = npsb[0] % 2
                                  npsb[0] += 1
                                  for j in range(4):
                                      k.op("pe", lambda e: e.transpose(psb[pb_][:, j * 128:(j + 1) * 128],
                                                                       dst[:, c0 + j * 128:c0 + (j + 1) * 128], ident),
                                           r=[dk_, "cb"], w=[("psb", pb_)], sig=(j == 3))
                                  k.op("dve", lambda e: e.tensor_copy(out=k_tok[hh][:, tt4 * 4:(tt4 + 1) * 4, :],
                                                                      in_=psb[pb_][:, 0:512].rearrange("p (a b) -> p a b", a=4)),
                                       r=[("psb", pb_)], w=[("ktok", hh, tt4)])
                                  yield
                          else:
                              pb_ = npsb[0] % 2
                              npsb[0] += 1
                              for j in range(4):
                                  k.op("pe", lambda e: e.transpose(psb[pb_][:, j * 128:(j + 1) * 128], sqb[:, j * 128:(j + 1) * 128], ident),
                                       r=[ks, "cb"], w=[("psb", pb_)], sig=(j == 3))
                              TS("dve", v_tok[hh][:, tt4 * 4:(tt4 + 1) * 4, :], psb[pb_][:, 0:512].rearrange("p (a b) -> p a b", a=4),
                                 0.5, ALU.mult, r=[("psb", pb_)], w=[("vtok", hh, tt4)])
                              yield

                  def za_gen(hh, ub):
                      for tt4 in range(4):
                          tnh = tnhs[ub][tt4 % 2]
                          kt_ = ("tnh", ub, tt4 % 2)
                          b = k.bank()
                          proj_fm(wblk[3], ("wblk", 3), hh * 128, tt4, b)
                          ACTV(tnh[:], psf[b][:, :], AF.Tanh, r=bkeys(b), w=[kt_], scale=0.5)
                          STT(zg2[:, hh, tt4 * 512:(tt4 + 1) * 512], tnh[:], 1.0, psf[b][:, :], ALU.add, ALU.mult,
                              r=[kt_] + bkeys(b), w=[("zg", hh, tt4)])
                          yield

                  ulist = [("u", kind, hh) for kind in (1, 0, 2) for hh in range(2)] + [("z", 0, hh) for hh in range(2)]
                  uact = [None, None]
                  ufirst = [False]
                  while True:
                      prog = False
                      for ub in range(2):
                          if uact[ub] is None and ulist:
                              t_, kind, hh = ulist.pop(0)
                              dly = UDELAY if (ub == 1 and not ufirst[0]) else 0
                              if ub == 1:
                                  ufirst[0] = True
                              uact[ub] = unit_gen(kind, hh, ub, dly) if t_ == "u" else za_gen(hh, ub)
                          if uact[ub] is not None:
                              prog = True
                              try:
                                  next(uact[ub])
                              except StopIteration:
                                  uact[ub] = None
                      if not prog:
                          break
                  for hh in range(2):
                      k.op("pool", lambda e: e.memset(S_f[hh], 0.0), w=[("S", hh)])
                      k.op("pool", lambda e: e.memset(S_b[hh], 0.0), w=[("Sb", hh)])
                  k.barrier()
                  sA1.close()

                  sA2 = ExitStack()
                  sb2_ = lambda name, shape, dty=F32: sA2.enter_context(nc.sbuf_tensor(f"sb_{name}_g{gi}", shape, dty))
                  scr = {}
                  pbs = []
                  for si in range(NB):
                      d_ = {}
                      for nm, dty in (("Gm", F32), ("M2", F32), ("U", BF16),
                                      ("M1", BF16), ("vb", BF16), ("kbg", BF16),
                                      ("Uo0", BF16), ("Uo1", BF16)):
                          d_[nm] = sb2_(f"{nm}_b{si}", [128, 4, 128], dty)
                      for a_, b_ in (("t1", "Gm"), ("DT", "Gm"), ("t2", "M2"), ("CU", "M2"), ("M1b", "M1")):
                          d_[a_] = d_[b_]
                      gv8 = d_["Gm"][:].bitcast(BF16).rearrange("p g (h c) -> p (g h) c", h=2)
                      mv8 = d_["M2"][:].bitcast(BF16).rearrange("p g (h c) -> p (g h) c", h=2)
                      d_["N0"], d_["N1"] = gv8[:, 0:4, :], gv8[:, 4:8, :]
                      d_["V0"], d_["V1"] = mv8[:, 0:4, :], mv8[:, 4:8, :]
                      pbs.append(d_)
                  ring = {}
                  for nm, dty in (("EG", F32), ("u", F32), ("wT", BF16), ("Qd", BF16), ("Aq", BF16), ("kdec", BF16)):
                      ring[nm] = sb2_(f"ring_{nm}", [128, RT, 2, 128], dty)
                  scr2 = {}
                  for nm, dty in (("vnew", BF16), ("sq", BF16), ("osb", F32), ("ln", F32), ("tmp", F32)):
                      scr2[nm] = sb2_(f"{nm}_2", [128, 2, 128], dty)
                  if gi < 3:
                      wload(wk_p[:], ("wblk", 1), win_d[:, 1024 + (h0 + 2) * 128: 1024 + (h0 + 2) * 128 + 256])
                  ALIAS = {"t1": "Gm", "DT": "Gm", "t2": "M2", "CU": "M2", "M1b": "M1", "N0": "Gm", "N1": "Gm", "V0": "M2", "V1": "M2"}

                  def prepb_gen(t0, si):
                      P = pbs[si]
                      s0 = t0 % RT
                      chains = [(tl, hh) for tl in range(2) for hh in range(2)]
                      A3 = lambda nm: P[nm][:]
                      A2 = lambda nm: P[nm][:].rearrange("p g c -> p (g c)")
                      A4 = lambda nm: P[nm][:].rearrange("p (a b) c -> p a b c", a=2)
                      Ag = lambda nm, g: P[nm][:, g, :]
                      PK = lambda nm: ("pb_" + ALIAS.get(nm, nm), si)
                      colv = lambda t_: t_[:, t0:t0 + 2, h0:h0 + 2].unsqueeze(3).to_broadcast([128, 2, 2, 128])
                      c3 = lambda c_: c_.unsqueeze(1).to_broadcast([128, 4, 128])
                      c4 = lambda c_: c_.unsqueeze(1).unsqueeze(1).to_broadcast([128, 2, 2, 128])
                      R4 = lambda nm: ring[nm][:, s0:s0 + 2, :, :]
                      RKs = lambda nm: [(nm, s0 + tl, hh) for tl, hh in chains]
                      ps3 = lambda b: psf[b][:, :].rearrange("p (g c) -> p g c", g=4)
                      ps4 = lambda b: psf[b][:, :].rearrange("p (a b c) -> p a b c", a=2, b=2)
                      psg = lambda b, g: psf[b][:, g * 128:(g + 1) * 128]
                      ck = [("col", t0), ("col", t0 + 1)]
                      qk_ = [("qfm", hh, t_ // 4) for hh in range(2) for t_ in (t0, t0 + 1)]
                      kk_ = [("kfm", hh, t_ // 4) for hh in range(2) for t_ in (t0, t0 + 1)]
                      vk_ = [("vtok", hh, t_ // 4) for hh in range(2) for t_ in (t0, t0 + 1)]
                      tk_ = [("ktok", hh, t_ // 4) for hh in range(2) for t_ in (t0, t0 + 1)]

                      def bank1():
                          while (bs_ := k.qalloc(1)) is None:
                              yield None
                          yield bs_[0][0]

                      TT("pool", A4("Gm"), c4(TLE), colv(gt), ALU.mult, r=ck + ["cf"], w=[PK("Gm")])
                      TT("pool", A4("M2"), c4(ident_f), colv(lnb), ALU.mult, r=ck + ["cf"], w=[PK("M2")])
                      for g, (tl, hh) in enumerate(chains):
                          tt = t0 + tl
                          h = h0 + hh
                          ACTV(Ag("vb", g), v_tok[hh][:, tt, :], AF.Copy, r=vk_ + ck, w=[PK("vb")], scale=beta[:, tt, h:h + 1])
                          ACTV(Ag("kbg", g), k_tok[hh][:, tt, :], AF.Copy, r=tk_ + ck, w=[PK("kbg")], scale=bg[:, tt, h:h + 1])
                          ACTV(ring["kdec"][:, s0 + tl, hh, :], k_tok[hh][:, tt, :], AF.Copy, r=tk_ + ck, w=[("kdec", s0 + tl, hh)],
                               scale=ekd[:, tt, h:h + 1])
                      yield
                      TT("pool", A3("M2"), A3("M2"), A3("Gm"), ALU.add, r=[PK("M2"), PK("Gm")], w=[PK("M2")])
                      while (bs_ := k.qalloc(2)) is None:
                          yield
                      bA, bB = bs_[0][0], bs_[1][0]
                      MM(psf[bA][:, :], ones_f, A2("Gm"), r=["cf", PK("Gm")], w=bkeys(bA))
                      MM(psf[bB][:, :], ones_f, A2("M2"), r=["cf", PK("M2")], w=bkeys(bB))
                      yield
                      TT("dve", A3("t1"), ps3(bA), c3(mU_incl), ALU.add, r=bkeys(bA) + ["cf"], w=[PK("t1")])
                      ACTV(R4("EG"), ps4(bA), AF.Exp, r=bkeys(bA), w=RKs("EG"))
                      TT("dve", A3("t2"), ps3(bB), c3(mU_strict), ALU.add, r=bkeys(bB) + ["cf"], w=[PK("t2")])
                      k.qfree((bA, 0), (bB, 0))
                      yield
                      TT("dve", A4("t1"), A4("t1"), colv(ngc), ALU.add, r=[PK("t1")] + ck, w=[PK("t1")])
                      TT("dve", A4("t2"), A4("t2"), colv(ngc), ALU.add, r=[PK("t2")] + ck, w=[PK("t2")])
                      yield
                      ACTV(A2("DT"), A2("t1"), AF.Exp, r=[PK("t1")], w=[PK("DT")])
                      ACTV(A2("CU"), A2("t2"), AF.Exp, r=[PK("t2")], w=[PK("CU")])
                      while (bs_ := k.qalloc(2)) is None:
                          yield
                      bC, bD = bs_[0][0], bs_[1][0]
                      for g, (tl, hh) in enumerate(chains):
                          sl = slice((t0 + tl) * 128, (t0 + tl + 1) * 128)
                          MM(psg(bC, g), k_fm[hh][:, sl], k_fm[hh][:, sl], r=kk_, w=bkeys(bC), sig=(g == 3))
                      for g, (tl, hh) in enumerate(chains):
                          sl = slice((t0 + tl) * 128, (t0 + tl + 1) * 128)
                          MM(psg(bD, g), k_fm[hh][:, sl], q_fm[hh][:, sl], r=kk_ + qk_, w=bkeys(bD), sig=(g == 3))
                      yield
                      TT("dve", A3("U"), ps3(bC), A3("CU"), ALU.mult, r=bkeys(bC) + [PK("CU")], w=[PK("U")])
                      TT("dve", R4("Aq"), ps4(bD), A4("DT"), ALU.mult, r=bkeys(bD) + [PK("DT")], w=RKs("Aq"))
                      k.qfree((bC, 0), (bD, 0))
                      TT("pool", R4("Qd"), q_fm2[:, :, t0 * 128:(t0 + 2) * 128].rearrange("p h (t c) -> p t h c", t=2), R4("EG"),
                         ALU.mult, r=qk_ + RKs("EG"), w=RKs("Qd"))
                      yield
                      Tn, Ttn, Un = ("N0", "N1"), ("V0", "V1"), ("Uo0", "Uo1")
                      offb = lambda b_: c3(offall[:, OFFI[b_] * 128:(OFFI[b_] + 1) * 128])
                      TT("pool", A3(Un[0]), A3("U"), offb(1), ALU.mult, r=[PK("U"), "cb"], w=[PK(Un[0])])
                      TT("pool", A3(Ttn[0]), c3(ident), A3(Un[0]), ALU.subtract, r=["cb", PK(Un[0])], w=[PK(Ttn[0])])
                      TT("pool", A3(Un[1]), A3("U"), offb(2), ALU.mult, r=[PK("U"), "cb"], w=[PK(Un[1])])
                      yield
                      DVT(A2(Tn[0]), A2(Ttn[0]), r=[PK(Ttn[0])], w=[PK(Tn[0])])
                      cur = 0
                      levels = (2, 4, 8, 16, 32, 64)
                      for li, bsz in enumerate(levels):
                          un = Un[(li + 1) % 2]
                          while (bs_ := k.qalloc(1)) is None:
                              yield
                          b1 = bs_[0][0]
                          for g in range(4):
                              MM(psg(b1, g), Ag(un, g), Ag(Tn[cur], g), r=[PK(un), PK(Tn[cur])], w=bkeys(b1), sig=(g == 3))
                          yield
                          ACTV(A2("M1"), psf[b1][:, :], AF.Copy, r=bkeys(b1), w=[PK("M1")])
                          k.qfree((b1, 0))
                          if li + 1 < len(levels):
                              TT("pool", A3(Un[li % 2]), A3("U"), offb(levels[li + 1]), ALU.mult, r=[PK("U"), "cb"], w=[PK(Un[li % 2])])
                          yield
                          if bsz < 32:
                              while (bs_ := k.qalloc(1)) is None:
                                  yield
                              b2 = bs_[0][0]
                              for g in range(4):
                                  MM(psg(b2, g), Ag(Ttn[cur], g), Ag("M1", g), r=[PK(Ttn[cur]), PK("M1")], w=bkeys(b2), sig=(g == 3))
                              yield
                              TT("dve", A3(Tn[1 - cur]), A3(Tn[cur]), ps3(b2), ALU.subtract, r=bkeys(b2) + [PK(Tn[cur])], w=[PK(Tn[1 - cur])])
                              k.qfree((b2, 0))
                              DVT(A2(Ttn[1 - cur]), A2(Tn[1 - cur]), r=[PK(Tn[1 - cur])], w=[PK(Ttn[1 - cur])])
                              cur = 1 - cur
                              yield
                          elif bsz == 32:
                              T32, T32T = Tn[cur], Ttn[cur]
                              while (bs_ := k.qalloc(2)) is None:
                                  yield
                              b2, b3 = bs_[0][0], bs_[1][0]
                              for g in range(4):
                                  MM(psg(b2, g), Ag("M1", g), Ag(T32T, g), r=[PK("M1"), PK(T32T)], w=bkeys(b2), sig=(g == 3))
                              for g in range(4):
                                  MM(psg(b3, g), Ag(T32T, g), Ag("M1", g), r=[PK("M1"), PK(T32T)], w=bkeys(b3), sig=(g == 3))
                              yield
                              T64n, T64Tn = Tn[1 - cur], Ttn[1 - cur]
                              TT("dve", A3(T64Tn), A3(T32T), ps3(b2), ALU.subtract, r=bkeys(b2) + [PK(T32T)], w=[PK(T64Tn)])
                              TT("dve", A3(T64n), A3(T32), ps3(b3), ALU.subtract, r=bkeys(b3) + [PK(T32)], w=[PK(T64n)])
                              k.qfree((b2, 0), (b3, 0))
                              TTn = T32T
                              Tn = (T64n, T64n)
                              cur = 0
                              yield
                          else:
                              while (bs_ := k.qalloc(1)) is None:
                                  yield
                              b2 = bs_[0][0]
                              for g in range(4):
                                  MM(psg(b2, g), Ag("M1", g), Ag(T64Tn, g), r=[PK("M1"), PK(T64Tn)], w=bkeys(b2), sig=(g == 3))
                              yield
                              TT("dve", A3(TTn), A3(T64Tn), ps3(b2), ALU.subtract, r=bkeys(b2) + [PK(T64Tn)], w=[PK(TTn)])
                              k.qfree((b2, 0))
                              yield
                      while (bs_ := k.qalloc(2)) is None:
                          yield
                      bU, bW = bs_[0][0], bs_[1][0]
                      for g in range(4):
                          MM(psg(bU, g), Ag(TTn, g), Ag("vb", g), r=[PK(TTn), PK("vb")], w=bkeys(bU), sig=(g == 3))
                      for g in range(4):
                          MM(psg(bW, g), Ag("kbg", g), Ag(TTn, g), r=[PK(TTn), PK("kbg")], w=bkeys(bW), sig=(g == 3))
                      yield
                      ACTV(R4("u"), ps4(bU), AF.Copy, r=bkeys(bU), w=RKs("u"))
                      ACTV(R4("wT"), ps4(bW), AF.Copy, r=bkeys(bW), w=RKs("wT"))
                      k.qfree((bU, 0), (bW, 0))

                  def rec_gen(hh_unused, tt):
                      sl = slice(tt * 128, (tt + 1) * 128)
                      tt4 = tt // 4
                      s_ = tt % RT
                      X2 = lambda nm: scr2[nm][:]
                      X2f = lambda nm: scr2[nm][:].rearrange("p h c -> p (h c)")
                      XK = lambda nm: ("rec", nm)
                      R3 = lambda nm: ring[nm][:, s_, :, :]
                      RKs = lambda nm: [(nm, s_, 0), (nm, s_, 1)]
                      half = lambda b, hh: psf[b][:, hh * 128:(hh + 1) * 128]
                      ps2 = lambda b: psf[b][:, 0:256].rearrange("p (h c) -> p h c", h=2)
                      SK = [("S", 0), ("S", 1)]
                      SbK = [("Sb", 0), ("Sb", 1)]
                      while (bs_ := k.qalloc(1)) is None:
                          yield
                      qa = bs_[0][0]
                      for hh in range(2):
                          MM(half(qa, hh), ring["wT"][:, s_, hh, :], S_b2[:, hh, :], r=RKs("wT") + SbK, w=bkeys(qa), sig=(hh == 1))
                      yield
                      TT("dve", X2("vnew"), R3("u"), ps2(qa), ALU.subtract, r=RKs("u") + bkeys(qa), w=[XK("vnew")])
                      k.qfree((qa, 0))
                      yield
                      while (bs_ := k.qalloc(2)) is None:
                          yield
                      qo, qs = bs_[0][0], bs_[1][0]
                      for hh in range(2):
                          MM(half(qs, hh), ring["kdec"][:, s_, hh, :], scr2["vnew"][:, hh, :], r=RKs("kdec") + [XK("vnew")], w=bkeys(qs), sig=(hh == 1))
                      for hh in range(2):
                          MM(half(qo, hh), S_b2[:, hh, :], ring["Qd"][:, s_, hh, :], r=SbK + RKs("Qd"), w=bkeys(qo), stop=False, sig=False)
                          MM(half(qo, hh), scr2["vnew"][:, hh, :], ring["Aq"][:, s_, hh, :], r=[XK("vnew")] + RKs("Aq"), w=bkeys(qo), start=False,
                             sig=(hh == 1))
                      yield
                      for hh in range(2):
                          STT(S_f2[:, hh, :], S_f2[:, hh, :], ring["EG"][:, s_, hh, 127:128], half(qs, hh), ALU.mult, ALU.add,
                              r=SK + RKs("EG") + bkeys(qs), w=[("S", hh)])
                      k.qfree((qs, 0))
                      ACTV(X2("sq"), ps2(qo), AF.Square, r=bkeys(qo), w=[XK("sq")])
                      yield
                      ACTV(S_b2[:], S_f2[:], AF.Copy, r=SK, w=SbK)
                      k.op("dve", lambda e: e.tensor_copy(out=X2("osb"), in_=ps2(qo)), r=bkeys(qo), w=[XK("osb")])
                      k.qfree((qo, 0))
                      while (bs_ := k.qalloc(1)) is None:
                          yield
                      qn = bs_[0][0]
                      MM(psf[qn][:, 0:256], ones_b, X2f("sq"), r=["cb", XK("sq")], w=bkeys(qn))
                      yield
                      ACTV(X2("ln"), ps2(qn), AF.Ln, r=bkeys(qn), w=[XK("ln")], scale=1.0 / 128, bias=epsc[:, 0:1])
                      k.qfree((qn, 0))
                      ACTV(X2("ln"), X2("ln"), AF.Exp, r=[XK("ln")], w=[XK("ln")], scale=-0.5)
                      yield
                      STT(X2f("tmp"), X2f("osb"), dnw[:, 0:1], X2f("ln"), ALU.mult, ALU.mult, r=[XK("osb"), XK("ln"), "dnw"], w=[XK("tmp")])
                      TT("dve", ya_fm[:, h0:h0 + 2, sl], X2("tmp"), zg2[:, :, sl], ALU.mult, r=[XK("tmp"), ("zg", 0, tt4), ("zg", 1, tt4)],
                         w=[("ya", h0, tt), ("ya", h0 + 1, tt)])

                  ntl = NT if STOP >= 3 else 0
                  todo = list(range(0, ntl, 2))
                  act = [None] * NB
                  done = set()
                  rec_act = [None, None]
                  rec_next = [0, 0]
                  rnd_ = 0
                  while True:
                      progressed = False
                      rnd_ += 1
                      for si in range(NB):
                          if act[si] is None and todo:
                              t0_ = todo[0]
                              if rnd_ <= si * BSTAG:
                                  progressed = True
                              elif t0_ + 1 < min(rec_next) + RT:
                                  todo.pop(0)
                                  act[si] = (prepb_gen(t0_, si), t0_)
                          if act[si] is not None:
                              progressed = True
                              try:
                                  next(act[si][0])
                              except StopIteration:
                                  for hh in range(2):
                                      done.add((hh, act[si][1]))
                                      done.add((hh, act[si][1] + 1))
                                  act[si] = None
                      if rec_act[0] is None and rec_next[0] < ntl and (0, rec_next[0]) in done:
                          rec_act[0] = rec_gen(0, rec_next[0])
                      if rec_act[0] is not None:
                          progressed = True
                          try:
                              next(rec_act[0])
                          except StopIteration:
                              rec_act[0] = None
                              rec_next[0] += 1
                              rec_next[1] += 1
                      if not progressed:
                          break
                  assert rec_next == [ntl, ntl] and not todo
                  k.barrier()
                  sA2.close()
              k.barrier()

        def dump_fm(tag, src, nk):
            if DEBUG != tag:
                return
            with ExitStack() as dbs:
                dbg32 = dbs.enter_context(nc.sbuf_tensor("sb_dbg32_" + tag, [128, T], F32))
                for kt in range(nk):
                    k.op("dve", lambda e: e.tensor_copy(out=dbg32[:], in_=src[:, kt, :]), w=["dbg32"])
                    k.dma("sp", dbg_d[:, kt * T:(kt + 1) * T], dbg32[:], r=["dbg32"])
                k.barrier()

        dump_fm("ya", ya_fm, 8)

        k.banks = [0, 1, 2, 3, 4, 5]
        with ExitStack() as pA2:
            sbx = lambda name, shape, dty=F32: pA2.enter_context(nc.sbuf_tensor("sb_" + name, shape, dty))
            wga = [sbx(f"wga{i}", [128, 8, 128], BF16) for i in range(2)]
            paj = [sbx(f"paj{i}", [128, 8, 128], BF16) for i in range(2)]
            tnh2 = [sbx(f"tnhA{i}", [128, 512]) for i in range(2)]
            ma_tmp = sbx("ma_tmp", [128, 8, T], BF16)
            n_ = 0
            def a2_loads(j):
                wload(wga[j % 2][:], ("wga", j % 2), win_d[:, 7184 + j * 128: 7184 + (j + 1) * 128])
                wload(paj[j % 2][:], ("paj", j % 2), pa_d[:, j * 128:(j + 1) * 128])
            a2_loads(0)
            for j in range(8):
                if j + 1 < 8:
                    a2_loads(j + 1)
                for tt4 in range(4):
                    cs = slice(tt4 * 512, (tt4 + 1) * 512)
                    tb = tnh2[n_ % 2]
                    tk = ("tnhA", n_ % 2)
                    n_ += 1
                    b1 = k.bank()
                    proj_fm(wga[j % 2], ("wga", j % 2), 0, tt4, b1)
                    ACTV(tb[:], psf[b1][:, :], AF.Tanh, r=bkeys(b1), w=[tk], scale=0.5)
                    b2 = k.bank()
                    for hk in range(8):
                        MM(psf[b2][:, :], paj[j % 2][:, hk, :], ya_fm[:, hk, cs], r=[("paj", j % 2)], w=bkeys(b2),
                           start=(hk == 0), stop=(hk == 7), sig=(hk == 7))
                    STT(ma_tmp[:, j, cs], tb[:], 1.0, psf[b2][:, :], ALU.add, ALU.mult, r=[tk] + bkeys(b2), w=[("mat", j, tt4)])
            k.barrier()
            for j in range(8):
                if j % 4 == 3:
                    ACTV(ma_fm[:, j, :], ma_tmp[:, j, :], AF.Copy, r=[], w=[("ma", j, t_) for t_ in range(4)])
                else:
                    k.op("dve", lambda e: e.tensor_copy(out=ma_fm[:, j, :], in_=ma_tmp[:, j, :]), w=[("ma", j, t_) for t_ in range(4)])
            k.barrier()

        pB = ExitStack()
        if True:
            sbx = lambda name, shape, dty=F32: pB.enter_context(nc.sbuf_tensor("sb_" + name, shape, dty))
            yb_fm = sbx("yb_fm", [128, 6, T], BF16)
            pBi = ExitStack()
            sbi = lambda name, shape, dty=F32: pBi.enter_context(nc.sbuf_tensor("sb_" + name, shape, dty))
            zbg = [sbi(f"zbg{i}", [128, T], BF16) for i in range(3)]
            num = [sbi(f"num{i}", [128, T], BF16) for i in range(3)]
            den = [sbi(f"den{i}", [128, T]) for i in range(3)]
            qb = sbi("qb", [128, T], BF16)
            kb = sbi("kb", [128, T], BF16)
            vp = [sbi(f"vp{i}", [128, 16, 2, 128], BF16) for i in range(2)]
            PTb = [sbi(f"PT{i}", [128, 512], BF16) for i in range(3)]
            wq2 = [sbi(f"wq{i}", [128, 8, 128], BF16) for i in range(2)]; wk2 = [sbi(f"wk{i}", [128, 8, 128], BF16) for i in range(2)]
            wv2 = [sbi(f"wv{i}", [128, 8, 128], BF16) for i in range(2)]; wz2 = [sbi(f"wz{i}", [128, 8, 128], BF16) for i in range(2)]
            tnhB = [sbi(f"tnhBi{i}", [128, 512]) for i in range(2)]
            dtot = sbi("dtot", [128, T])
            tmpB = [sbi(f"tmpBi{i}", [128, 512]) for i in range(2)]
            for i in range(2):
                k.op("pool", lambda e: e.memset(vp[i][:], 0.0), w=[("vp", i, blk) for blk in range(16)])
            onesP = (onesA, onesB)
            npt = 0
            nB = 0
            for p in range(2):
                for g in range(3):
                    pt = 2 * g + p
                    d = (1, 4, 16)[g]
                    nbk = 16 // d
                    vpi = npt % 2
                    npt += 1
                    vpt = vp[vpi]
                    wq, wk, wv, wz = wq2[vpi], wk2[vpi], wv2[vpi], wz2[vpi]
                    wqk, wkk, wvk, wzk = ("wq", vpi), ("wk", vpi), ("wv", vpi), ("wz", vpi)

                    def att_loads(pt_, vi_):
                        wload(wq2[vi_][:], ("wq", vi_), win_d[:, 4112 + pt_ * 128: 4112 + (pt_ + 1) * 128])
                        wload(wk2[vi_][:], ("wk", vi_), win_d[:, 4880 + pt_ * 128: 4880 + (pt_ + 1) * 128])
                        wload(wv2[vi_][:], ("wv", vi_), win_d[:, 5648 + pt_ * 128: 5648 + (pt_ + 1) * 128])
                        wload(wz2[vi_][:], ("wz", vi_), win_d[:, 6416 + pt_ * 128: 6416 + (pt_ + 1) * 128])
                    if npt == 1:
                        att_loads(pt, vpi)
                    nxt = [(p_, g_) for p_ in range(2) for g_ in range(3)]
                    ni_ = nxt.index((p, g)) + 1
                    if ni_ < len(nxt):
                        att_loads(2 * nxt[ni_][1] + nxt[ni_][0], 1 - vpi)
                    for tt4 in range(4):
                        cs = slice(tt4 * 512, (tt4 + 1) * 512)
                        b = k.bank()
                        proj_fm(wq, wqk, 0, tt4, b)
                        ACTV(qb[:, cs], psf[b][:, :], AF.Copy, r=bkeys(b), w=[("qb", tt4)])
                        b = k.bank()
                        proj_fm(wk, wkk, 0, tt4, b)
                        k.op("dve", lambda e: e.tensor_copy(out=kb[:, cs], in_=psf[b][:, :]), r=bkeys(b), w=[("kb", tt4)])
                        b = k.bank()
                        proj_fm(wz, wzk, 0, tt4, b)
                        tb = tnhB[tt4 % 2]
                        ACTV(tb[:], psf[b][:, :], AF.Tanh, r=bkeys(b), w=[("tnhB", tt4 % 2)], scale=0.5)
                        STT(zbg[g][:, cs], tb[:], 1.0, psf[b][:, :], ALU.add, ALU.mult, r=[("tnhB", tt4 % 2)] + bkeys(b), w=[("zbg", g, tt4)])

                    def tsl(i, r_):
                        s0 = 128 * i * d + r_
                        return slice(s0, s0 + 127 * d + 1, d), list(range(s0 // 128, (s0 + 127 * d) // 128 + 1))

                    for r_ in range(d):
                        for i in range(nbk):
                            blk = r_ * nbk + i
                            sl_, tiles = tsl(i, r_)
                            b = k.bank()
                            for kt in range(8):
                                MM(psf[b][:, 0:128], h_fm[:, kt, sl_], wv[:, kt, :], r=[wvk] + [("h", t_) for t_ in tiles],
                                   w=bkeys(b), start=(kt == 0), stop=(kt == 7), sig=(kt == 7))
                            ACTV(vpt[:, blk, 0, 0:64], psf[b][:, 0:64], AF.Copy, r=bkeys(b), w=[("vp", vpi, blk)])
                            k.op("dve", lambda e: e.tensor_copy(out=vpt[:, blk, 1, 64:128], in_=psf[b][:, 64:128]),
                                 r=bkeys(b), w=[("vp", vpi, blk)])
                    def scores_blk(r_, i):
                        nonlocal nB
                        qs, qtiles = tsl(i, r_)
                        qk_ = [("qb", t_ // 4) for t_ in qtiles]
                        kck = [("kb", t_ // 4) for t_ in qtiles]
                        if i > 0:
                            kps, ptiles = tsl(i - 1, r_)
                            kpk = [("kb", t_ // 4) for t_ in ptiles]
                        b = k.bank()
                        for hd in range(2):
                            prt = slice(hd * 64, hd * 64 + 64)
                            if i > 0:
                                o_ = psf[b][:, hd * 128:(hd + 1) * 128]
                                MM(o_, kb[prt, kps], qb[prt, qs], r=kpk + qk_, w=bkeys(b), stop=False, sig=False)
                                MM(o_, ident, NEGprev, r=["cb"], w=bkeys(b), start=False, sig=False)
                            o_ = psf[b][:, 256 + hd * 128: 256 + (hd + 1) * 128]
                            MM(o_, kb[prt, qs], qb[prt, qs], r=kck + qk_, w=bkeys(b), stop=False, sig=False)
                            MM(o_, ident, NEGcur, r=["cb"], w=bkeys(b), start=False, sig=(hd == 1))
                        pt_ = PTb[nB % 3]
                        ptk = ("PT", nB % 3)
                        nB += 1
                        c_lo = 0 if i > 0 else 256
                        ACTV(pt_[:, c_lo:512], psf[b][:, c_lo:512], AF.Exp, r=bkeys(b), w=[ptk], scale=0.125)
                        return pt_, ptk, qs

                    def pv_blk(r_, i, ctx):
                        pt_, ptk, qs = ctx
                        blk = r_ * nbk + i
                        terms = []
                        for hd in range(2):
                            if i > 0:
                                terms.append((hd * 128, blk - 1, hd))
                            terms.append((256 + hd * 128, blk, hd))
                        bo = k.bank()
                        for n2, (c0, bk_, hd) in enumerate(terms):
                            MM(psf[bo][:, 0:128], vpt[:, bk_, hd, :], pt_[:, c0:c0 + 128], r=[ptk, ("vp", vpi, bk_)], w=bkeys(bo),
                               start=(n2 == 0), stop=(n2 == len(terms) - 1), sig=False)
                        for n2, (c0, bk_, hd) in enumerate(terms):
                            MM(psf[bo][:, 128:256], onesP[hd], pt_[:, c0:c0 + 128], r=[ptk, "cb"], w=bkeys(bo),
                               start=(n2 == 0), stop=(n2 == len(terms) - 1), sig=(n2 == len(terms) - 1))
                        k.op("dve", lambda e: e.tensor_copy(out=num[g][:, qs], in_=psf[bo][:, 0:128]), r=bkeys(bo), w=[("num", g, blk)])
                        ACTV(den[g][:, qs], psf[bo][:, 128:256], AF.Copy, r=bkeys(bo), w=[("den", g, blk)])

                    blks = [(r_, i) for r_ in range(d) for i in range(nbk)]
                    ctxs = {0: scores_blk(*blks[0])}
                    for n_, (r_, i) in enumerate(blks):
                        if n_ + 1 < len(blks):
                            ctxs[n_ + 1] = scores_blk(*blks[n_ + 1])
                        pv_blk(r_, i, ctxs.pop(n_))
                allnd = [(nm, g_, blk) for nm in ("num", "den") for g_ in range(3) for blk in range(16)]
                for c4 in range(4):
                    cs = slice(c4 * 512, (c4 + 1) * 512)
                    TT("dve", dtot[:, cs], den[0][:, cs], den[1][:, cs], ALU.add, r=allnd, w=[("dtot", c4)])
                    TT("dve", dtot[:, cs], dtot[:, cs], den[2][:, cs], ALU.add, r=allnd, w=[("dtot", c4)])
                    ACTV(dtot[:, cs], dtot[:, cs], AF.Ln, r=[("dtot", c4)], w=[("dtot", c4)])
                    ACTV(dtot[:, cs], dtot[:, cs], AF.Exp, r=[("dtot", c4)], w=[("dtot", c4)], scale=-1.0)
                    for g in range(3):
                        tb = tmpB[(c4 * 3 + g) % 2]
                        tk = ("tmpB", (c4 * 3 + g) % 2)
                        STT(tb[:], num[g][:, cs], 0.5, dtot[:, cs], ALU.mult, ALU.mult, r=allnd + [("dtot", c4)], w=[tk])
                        TT("pool", yb_fm[:, 2 * g + p, cs], tb[:], zbg[g][:, cs], ALU.mult, r=[tk, ("zbg", g, c4)], w=[("yb", 2 * g + p, c4)])
            k.barrier()
            pBi.close()
            dump_fm("yb", yb_fm, 6)

            wo = sbx("wo", [128, 8, D], BF16)
            fnw = sbx("fnw", [128, D])
            wgb = [sbx(f"wgb{i}", [128, 8, 128], BF16) for i in range(2)]
            pbj = [sbx(f"pbj{i}", [128, 6, 128], BF16) for i in range(2)]
            tnhB = [sbx(f"tnhB{i}", [128, 512]) for i in range(2)]
            tmpB = [sbx(f"tmpB{i}", [128, 512]) for i in range(2)]
            wload(wgb[0][:], ("wgb", 0), win_d[:, 8208: 8208 + 128])
            wload(pbj[0][:], ("pbj", 0), pb_d[:, 0:128])
            k.dma("sp", fnw[:], fnw_d[:, :], w=["fnw"])
            wload(wo[:, :, 0:512], ("wo", 0), wo_d[:, 0:512])
            wload(wo[:, :, 512:1024], ("wo", 1), wo_d[:, 512:1024])
            n_ = 0
            for j in range(8):
                if j + 1 < 8:
                    wload(wgb[(j + 1) % 2][:], ("wgb", (j + 1) % 2), win_d[:, 8208 + (j + 1) * 128: 8208 + (j + 2) * 128])
                    wload(pbj[(j + 1) % 2][:], ("pbj", (j + 1) % 2), pb_d[:, (j + 1) * 128:(j + 2) * 128])
                TT("dve", wo[:, j, :], wo[:, j, :], gate[:], ALU.mult, r=[("wo", 0), ("wo", 1), "gate"], w=[("wofold", j)])
                for tt4 in range(4):
                    cs = slice(tt4 * 512, (tt4 + 1) * 512)
                    tb, tk = tnhB[n_ % 2], ("tnhB", n_ % 2)
                    t2b, t2k = tmpB[n_ % 2], ("tmpB", n_ % 2)
                    n_ += 1
                    b1 = k.bank()
                    proj_fm(wgb[j % 2], ("wgb", j % 2), 0, tt4, b1)
                    ACTV(tb[:], psf[b1][:, :], AF.Tanh, r=bkeys(b1), w=[tk], scale=0.5)
                    b2 = k.bank()
                    for pt in range(6):
                        MM(psf[b2][:, :], pbj[j % 2][:, pt, :], yb_fm[:, pt, cs], r=[("pbj", j % 2), ("yb", pt, tt4)], w=bkeys(b2),
                           start=(pt == 0), stop=(pt == 5), sig=(pt == 5))
                    STT(t2b[:], tb[:], 1.0, psf[b2][:, :], ALU.add, ALU.mult, r=[tk] + bkeys(b2), w=[t2k])
                    TT("pool", ma_fm[:, j, cs], ma_fm[:, j, cs], t2b[:], ALU.add, r=[t2k, ("ma", j, tt4)], w=[("ma", j, tt4)])
            k.barrier()
        dump_fm("merged", ma_fm, 8)

        with ExitStack() as pC:
            sbx = lambda name, shape, dty=F32: pC.enter_context(nc.sbuf_tensor("sb_" + name, shape, dty))
            xin = [sbx(f"xin{i}", [128, D]) for i in range(3)]
            xo = [sbx(f"xo{i}", [128, D]) for i in range(3)]
            outt = [sbx(f"outt{i}", [128, D]) for i in range(3)]
            junkc = sbx("junkc", [128, D], BF16)
            ssc = sbx("ssc", [128, NT]); rsc = sbx("rsc", [128, NT])
            k.op("dve", lambda e: e.memset(ssc[:], 0.0), w=["ssc"])
            for tt in range(NT):
                i2 = tt % 3
                k.dma("sp", xin[i2][:], x_d[tt * 128:(tt + 1) * 128, :], w=[("xin", i2)])
                for nh in range(2):
                    b = k.bank()
                    cs = slice(nh * 512, (nh + 1) * 512)
                    for kt in range(8):
                        MM(psf[b][:, :], ma_fm[:, kt, tt * 128:(tt + 1) * 128], wo[:, kt, cs], r=[("wo", nh)], w=bkeys(b),
                           start=(kt == 0), stop=(kt == 7), sig=(kt == 7))
                    TT("dve", xo[i2][:, cs], psf[b][:, :], xin[i2][:, cs], ALU.add, r=bkeys(b) + [("xin", i2)], w=[("xo", i2, nh)])
                k.op("act", lambda e: e.activation(out=junkc[:], in_=xo[i2][:], func=AF.Square, accum_out=ssc[:, tt:tt + 1]),
                     r=[("xo", i2, 0), ("xo", i2, 1), "ssc"], w=["junkc", ("ssc", tt)])
                ACTV(rsc[:, tt:tt + 1], ssc[:, tt:tt + 1], AF.Ln, r=[("ssc", tt)], w=[("rsc", tt)], scale=1.0 / D, bias=epsc[:, 0:1])
                ACTV(rsc[:, tt:tt + 1], rsc[:, tt:tt + 1], AF.Exp, r=[("rsc", tt)], w=[("rsc", tt)], scale=-0.5)
                STT(outt[i2][:], xo[i2][:], rsc[:, tt:tt + 1], fnw[:], ALU.mult, ALU.mult,
                    r=[("xo", i2, 0), ("xo", i2, 1), ("rsc", tt), "fnw"], w=[("outt", i2)])
                k.dma("pool", out_d[tt * 128:(tt + 1) * 128, :], outt[i2][:], r=[("outt", i2)])
            k.barrier()
        pB.close()
        k.barrier()
        for e in ("sp",):
            for c in k.E[e].ring:
                k.wait(k.E[e], c, c.count)
    return nc


def make_consts():
    i = np.arange(128)
    e, c = i[:, None], i[None, :]
    cf = np.zeros((128, 768), np.float32)
    cf[:, 0:128] = 1.0
    cf[:, 128:256] = (e <= c)
    cf[:, 256:384] = (e > c)
    cf[:, 384:512] = np.eye(128)
    cf[:, 512:640] = np.where(e <= c, 0.0, NEG)
    cf[:, 640:768] = np.where(e < c, 0.0, NEG)
    cb = np.zeros((128, 1664), np.float32)
    cb[:, 0:128] = np.eye(128)
    cb[:, 128:256] = 1.0
    cb[:, 256:384] = np.where(c <= e, 0.0, NEG)
    cb[:, 384:512] = np.where(c >= e, 0.0, NEG)
    cb[:, 512:640] = (c < 64)
    cb[:, 640:768] = (c >= 64)
    for n_, b_ in enumerate((1, 2, 4, 8, 16, 32, 64)):
        cb[:, 768 + n_ * 128: 768 + (n_ + 1) * 128] = ((e // b_) % 2 == 0) & (c // b_ == e // b_ + 1)
    return cf, cb.astype(ml_dtypes.bfloat16)


def make_in_maps(inp):
    cf, cb = make_consts()
    f = lambda a: np.ascontiguousarray(a, dtype=np.float32)
    rep = lambda v: f(np.broadcast_to(np.asarray(v).reshape(1, -1), (128, np.asarray(v).size)))
    shared = {
        "normw_bc": rep(inp["norm_w"][0]), "ada_w": f(inp["ada_w"][0]), "adab_bc": rep(inp["ada_b"][0]),
        "w_in": f(inp["w_in"][0]),
        "convw": f(np.asarray(inp["conv_w"][0]).reshape(4, 24, 128).transpose(2, 1, 0)),
        "alog_bc": rep(inp["a_log"][0]), "dtb_bc": rep(inp["dt_bias"][0]),
        "dnw_col": f(np.asarray(inp["dn_norm_w"][0]).reshape(128, 1)),
        "w_proj_a": f(inp["w_proj_a"][0]), "w_proj_b": f(inp["w_proj_b"][0]), "w_out": f(inp["w_out"][0]),
        "fnw_bc": rep(inp["final_norm_w"]), "cf": cf, "cb": cb,
    }
    maps = []
    for b in range(8):
        m = dict(shared)
        m["x"] = f(inp["x"][b])
        m["cT"] = f(np.asarray(inp["c"][b]).reshape(8, 128).T)
        maps.append(m)
    return maps


def kernel(**inputs):
    nc = build_program()
    maps = make_in_maps(inputs)
    res = run_bass_kernel_spmd(nc, maps, core_ids=list(range(8)))
    return np.stack([np.asarray(r["out"], dtype=np.float32) for r in res.results], axis=0)
```

```python
import numpy as np
import ml_dtypes
from contextlib import ExitStack
import concourse.bass as bass
import concourse.mybir as mybir
from concourse.bass_utils import run_bass_kernel_spmd

F32 = mybir.dt.float32
BF16 = mybir.dt.bfloat16
AF = mybir.ActivationFunctionType
ALU = mybir.AluOpType
AX = mybir.AxisListType

T = 2048
D = 1024
NT = 16
NIN = 9232
EPS = 1e-6
NEG = -30000.0
DEBUG = False
ATTACH_WAIT = True
import os
STOP = int(os.environ.get("KSTOP", "9"))


class Ctr:
    def __init__(self, sem):
        self.sem = sem
        self.count = 0


class Eng:
    def __init__(self, name, eng, ctr):
        self.name, self.eng, self.ctr = name, eng, ctr
        self.seen = {}
        self.ring = []
        self.ri = 0


class K:
    def __init__(self, nc, st):
        self.nc, self.st = nc, st
        self.E = {}
        for name, e in (("pe", nc.tensor), ("act", nc.scalar), ("dve", nc.vector),
                        ("pool", nc.gpsimd), ("sp", nc.sync)):
            self.E[name] = Eng(name, e, Ctr(st.enter_context(nc.semaphore("s_" + name))))
        for name, n in (("sp", 12), ("pool", 6)):
            self.E[name].ring = [Ctr(st.enter_context(nc.semaphore(f"d_{name}{i}"))) for i in range(n)]
        self.lastw, self.readers = {}, {}
        self.nbank = 0
        self.nq = 0
        self.banks = [0, 1, 2, 3]
        self.quarters = [(b, 0) for b in (4, 5)]
        self.qfreelist = [0, 1, 2, 3, 4, 5, 6, 7]

    def wait(self, E, ctr, val):
        if val <= 0 or (ctr is E.ctr and E.name in ("pe", "sp")):
            return
        if E.seen.get(id(ctr), 0) >= val:
            return
        E.eng.wait_ge(ctr.sem, val)
        E.seen[id(ctr)] = val

    def _deps(self, E, r, w):
        deps = {}

        def add(cv):
            c, v = cv
            if id(c) not in deps or deps[id(c)][1] < v:
                deps[id(c)] = (c, v)
        for k in r:
            if k in self.lastw:
                add(self.lastw[k])
        for k in w:
            if k in self.lastw:
                add(self.lastw[k])
            for cv in self.readers.get(k, {}).values():
                add(cv)
        need = []
        for c, v in deps.values():
            if v <= 0 or (c is E.ctr and E.name in ("pe", "sp")):
                continue
            if E.seen.get(id(c), 0) >= v:
                continue
            need.append((c, v))
        if not ATTACH_WAIT:
            for c, v in need:
                self.wait(E, c, v)
            return None
        for c, v in need[:-1]:
            self.wait(E, c, v)
        if need:
            c, v = need[-1]
            E.seen[id(c)] = v
            return (c, v)
        return None

    def _mark(self, ctr, val, r, w):
        for k in w:
            self.lastw[k] = (ctr, val)
            self.readers[k] = {}
        for k in r:
            self.readers.setdefault(k, {})[id(ctr)] = (ctr, val)

    @staticmethod
    def _norm(r, w):
        isps = lambda x: isinstance(x, tuple) and x[0] in ("ps", "psb")
        nb = lambda x: (x[0], x[1])
        w2 = [nb(x) if isps(x) else x for x in w] + [nb(x) for x in r if isps(x)]
        r2 = [x for x in r if not isps(x)]
        return r2, list(dict.fromkeys(w2))

    def op(self, en, fn, r=(), w=(), sig=True):
        E = self.E[en]
        r, w = self._norm(r, w)
        att = self._deps(E, r, w)
        inst = fn(E.eng)
        if att is not None:
            inst._wait_ge(att[0].sem, att[1])
        val = E.ctr.count + 1
        if sig:
            inst.then_inc(E.ctr.sem, 1)
            E.ctr.count = val
        self._mark(E.ctr, val, r, w)
        return inst

    def dma(self, qn, out, in_, r=(), w=()):
        E = self.E[qn]
        att = self._deps(E, r, w)
        slot = E.ring[E.ri % len(E.ring)]
        E.ri += 1
        ring_wait = None
        if slot.count > 0 and E.seen.get(id(slot), 0) < slot.count:
            ring_wait = (slot, slot.count)
        if att is not None and ring_wait is not None:
            E.eng.wait_ge(att[0].sem, att[1])
            att = None
        inst = E.eng.dma_start(out=out, in_=in_)
        one = att if att is not None else ring_wait
        if one is not None:
            inst._wait_ge(one[0].sem, one[1])
            E.seen[id(one[0])] = one[1]
        inst.then_inc(slot.sem, 16)
        slot.count += 16
        self._mark(slot, slot.count, r, w)

    def barrier(self):
        ctrs = [e.ctr for e in self.E.values()]
        for e in self.E.values():
            ctrs += e.ring
        for e in self.E.values():
            for c in ctrs:
                self.wait(e, c, c.count)
        self.lastw, self.readers = {}, {}

    def bank(self):
        b = self.banks[self.nbank % len(self.banks)]
        self.nbank += 1
        return b

    def qalloc(self, n):
        if len(self.qfreelist) < n:
            return None
        out = [(self.qfreelist.pop(0), 0) for _ in range(n)]
        return out

    def qfree(self, *bqs):
        for bq in bqs:
            self.qfreelist.append(bq[0])

    def quarter(self):
        bq = self.quarters[self.nq % len(self.quarters)]
        self.nq += 1
        return bq


def bkeys(b):
    return [("ps", b, q) for q in range(4)]


def build_program():
    nc = bass.Bass("TRN2", target_bir_lowering=False)
    dt = lambda name, shape, dty, kind: nc.dram_tensor(name, shape, dty, kind=kind).ap()
    x_d = dt("x", [T, D], F32, "ExternalInput")
    cT_d = dt("cT", [128, 8], F32, "ExternalInput")
    normw_d = dt("normw_bc", [128, D], F32, "ExternalInput")
    adaw_d = dt("ada_w", [D, 3 * D], F32, "ExternalInput")
    adab_d = dt("adab_bc", [128, 3 * D], F32, "ExternalInput")
    win_d = dt("w_in", [D, NIN], F32, "ExternalInput")
    convw_d = dt("convw", [128, 24, 4], F32, "ExternalInput")
    alog_d = dt("alog_bc", [128, 8], F32, "ExternalInput")
    dtb_d = dt("dtb_bc", [128, 8], F32, "ExternalInput")
    dnw_d = dt("dnw_col", [128, 1], F32, "ExternalInput")
    pa_d = dt("w_proj_a", [D, D], F32, "ExternalInput")
    pb_d = dt("w_proj_b", [768, D], F32, "ExternalInput")
    wo_d = dt("w_out", [D, D], F32, "ExternalInput")
    fnw_d = dt("fnw_bc", [128, D], F32, "ExternalInput")
    cf_d = dt("cf", [128, 768], F32, "ExternalInput")
    cb_d = dt("cb", [128, 1664], BF16, "ExternalInput")
    out_d = dt("out", [T, D], F32, "ExternalOutput")
    dbg_d = dt("dbg", [128, 8 * T], F32, "ExternalOutput") if DEBUG else None

    with ExitStack() as st:
        k = K(nc, st)
        sb = lambda name, shape, dty=F32: st.enter_context(nc.sbuf_tensor("sb_" + name, shape, dty))
        psf = [st.enter_context(nc.psum_tensor(f"psf{i}", [128, 512], F32)) for i in range(6)]
        psb = [st.enter_context(nc.psum_tensor(f"psb{i}", [128, 1024], BF16)) for i in range(2)]
        psf = psf + [psb[i][:].bitcast(F32) for i in range(2)]

        def psq(bq):
            b, q = bq
            return psf[b][:, q * 128:(q + 1) * 128]

        cf = sb("cf", [128, 768])
        cb = sb("cb", [128, 1664], BF16)
        k.dma("sp", cf[:], cf_d[:, :], w=["cf"])
        k.dma("sp", cb[:], cb_d[:, :], w=["cb"])
        ones_f, TLE, TGT, ident_f, mU_incl, mU_strict = [cf[:, i * 128:(i + 1) * 128] for i in range(6)]
        (ident, ones_b, NEGprev, NEGcur, onesA, onesB) = [cb[:, i * 128:(i + 1) * 128] for i in range(6)]
        offall = cb[:, 6 * 128:13 * 128]
        OFFI = {1: 0, 2: 1, 4: 2, 8: 3, 16: 4, 32: 5, 64: 6}

        gate = sb("gate", [128, D])
        epsc = sb("epsc", [128, 4])
        k.op("dve", lambda e: e.memset(epsc[:, 0:1], EPS), w=["epsc"])
        k.op("dve", lambda e: e.memset(epsc[:, 1:2], 4 * EPS), w=["epsc"])
        k.op("dve", lambda e: e.memset(epsc[:, 2:3], 1.0), w=["epsc"])
        k.barrier()
        h_fm = sb("h_fm", [128, 8, T], BF16)
        p01 = ExitStack()
        mod = p01.enter_context(nc.sbuf_tensor("sb_mod", [128, 3 * D], F32))
        a_bc = p01.enter_context(nc.sbuf_tensor("sb_a_bc", [128, D], F32))

        p0 = ExitStack()
        if True:
            sb0 = lambda name, shape, dty=F32: p0.enter_context(nc.sbuf_tensor("sb_" + name, shape, dty))
            ss = sb0("ss1", [128, NT])
            rs = sb0("rs1", [128, NT])
            k.op("pool", lambda e: e.memset(ss[:], 0.0), w=["ss"])
            cT = sb0("cT", [128, 8])
            th = sb0("c_th", [128, 8])
            sc = sb0("c_sc", [128, 8])
            screp = sb0("screp", [128, 8, 128], BF16)
            normw = sb0("normw", [128, D])
            adab = sb0("adab", [128, 3 * D])
            aw = [sb0(f"aw{i}", [128, 8, 512], BF16) for i in range(2)]
            k.dma("sp", cT[:], cT_d[:, :], w=["cT"])
            k.dma("sp", normw[:], normw_d[:, :], w=["normw"])
            k.dma("sp", adab[:], adab_d[:, :], w=["adab"])
            k.op("act", lambda e: e.activation(out=th[:], in_=cT[:], func=AF.Tanh, scale=0.5), r=["cT"], w=["th"])
            k.op("dve", lambda e: e.scalar_tensor_tensor(out=sc[:], in0=th[:], scalar=1.0, in1=cT[:],
                                                         op0=ALU.add, op1=ALU.mult), r=["th", "cT"], w=["sc"])
            k.op("dve", lambda e: e.tensor_scalar(out=sc[:], in0=sc[:], scalar1=0.5, scalar2=None, op0=ALU.mult),
                 r=["sc"], w=["sc"])
            for kt in range(8):
                k.op("dve", lambda e: e.tensor_copy(out=screp[:, kt, :], in_=sc[:, kt:kt + 1].to_broadcast([128, 128])),
                     r=["sc"], w=["screp"])
            def ada_block(blk):
                a = aw[blk % 2]
                b = k.bank()
                for kt in range(8):
                    k.op("pe", lambda e: e.matmul(psf[b][:, :], lhsT=screp[:, kt, :], rhs=a[:, kt, :],
                                                  start=(kt == 0), stop=(kt == 7)),
                         r=["screp", ("aw", blk % 2)], w=bkeys(b), sig=(kt == 7))
                k.op("dve", lambda e: e.tensor_tensor(out=mod[:, blk * 512:(blk + 1) * 512], in0=psf[b][:, :],
                                                      in1=adab[:, blk * 512:(blk + 1) * 512], op=ALU.add),
                     r=bkeys(b) + ["adab"], w=[("mod", blk)])

            for blk in range(4):
                k.dma("pool", aw[blk % 2][:], adaw_d[:, blk * 512:(blk + 1) * 512].rearrange("(kt p) n -> p kt n", p=128),
                      w=[("aw", blk % 2)])
                ada_block(blk)
            for blk in (4, 5):
                k.dma("pool", aw[blk % 2][:], adaw_d[:, blk * 512:(blk + 1) * 512].rearrange("(kt p) n -> p kt n", p=128),
                      w=[("aw", blk % 2)])
            k.op("dve", lambda e: e.scalar_tensor_tensor(out=a_bc[:], in0=mod[:, D:2 * D], scalar=1.0, in1=normw[:],
                                                         op0=ALU.add, op1=ALU.mult),
                 r=[("mod", 2), ("mod", 3), "normw"], w=["a_bc"])
            if DEBUG == "mod":
                k.dma("sp", dbg_d[:, 0:3 * D], mod[:], r=[("mod", i) for i in range(6)])
                k.dma("sp", dbg_d[:, 3 * D:4 * D], a_bc[:], r=["a_bc"])
                k.dma("sp", dbg_d[:, 4 * D:4 * D + 8], sc[:], r=["sc"])
                k.dma("sp", dbg_d[:, 4 * D + 8:4 * D + 16], cT[:], r=["cT"])
                k.dma("sp", dbg_d[:, 4 * D + 16:4 * D + 24], th[:], r=["th"])

        with ExitStack() as p1:
            sb1 = lambda name, shape, dty=F32: p1.enter_context(nc.sbuf_tensor("sb_" + name, shape, dty))
            xt = [sb1(f"xt{i}", [128, D]) for i in range(4)]
            junk = [sb1(f"junk1{i}", [128, D], BF16) for i in range(4)]
            hn = [sb1(f"hn{i}", [128, D]) for i in range(4)]
            ht = [sb1(f"ht{i}", [128, D], BF16) for i in range(4)]
            tbank = [(psb[0][:], ("psb", 0)), (psb[1][:], ("psb", 1)),
                     (psf[4][:].bitcast(BF16), ("ps", 4)), (psf[5][:].bitcast(BF16), ("ps", 5))]

            def tile_gen(tt, bi):
                xb, hb, hb2, jk = xt[bi], hn[bi], ht[bi], junk[bi]
                k.dma("sp", xb[:], x_d[tt * 128:(tt + 1) * 128, :], w=[("xt", bi)])
                k.op("act", lambda e: e.activation(out=jk[:], in_=xb[:], func=AF.Square, accum_out=ss[:, tt:tt + 1]),
                     r=[("xt", bi), "ss"], w=[("junk", bi), ("ss", tt)])
                yield
                k.op("act", lambda e: e.activation(out=rs[:, tt:tt + 1], in_=ss[:, tt:tt + 1], func=AF.Ln, scale=1.0 / D, bias=epsc[:, 0:1]),
                     r=[("ss", tt)], w=[("rs", tt)])
                k.op("act", lambda e: e.activation(out=rs[:, tt:tt + 1], in_=rs[:, tt:tt + 1], func=AF.Exp, scale=-0.5),
                     r=[("rs", tt)], w=[("rs", tt)])
                yield
                k.op("dve", lambda e: e.scalar_tensor_tensor(out=hb[:], in0=xb[:], scalar=rs[:, tt:tt + 1], in1=a_bc[:],
                                                             op0=ALU.mult, op1=ALU.mult),
                     r=[("xt", bi), ("rs", tt), "a_bc"], w=[("hn", bi)])
                yield
                k.op("pool", lambda e: e.tensor_tensor(out=hb2[:], in0=hb[:], in1=mod[:, 0:D], op=ALU.add),
                     r=[("hn", bi), ("mod", 0), ("mod", 1)], w=[("ht", bi)])
                yield
                tb_, tk_ = tbank[tt % 4]
                for kt in range(8):
                    k.op("pe", lambda e: e.transpose(tb_[:, kt * 128:(kt + 1) * 128], hb2[:, kt * 128:(kt + 1) * 128], ident),
                         r=[("ht", bi), "cb"], w=[tk_], sig=(kt == 7))
                yield
                if tt % 2 == 0:
                    k.op("act", lambda e: e.activation(out=h_fm[:, :, tt * 128:(tt + 1) * 128],
                                                       in_=tb_.rearrange("p (k t) -> p k t", k=8), func=AF.Copy),
                         r=[tk_], w=[("h", tt)])
                else:
                    k.op("dve", lambda e: e.tensor_copy(out=h_fm[:, :, tt * 128:(tt + 1) * 128], in_=tb_.rearrange("p (k t) -> p k t", k=8)),
                         r=[tk_], w=[("h", tt)])

            tl_todo = list(range(NT))
            tl_act = [None] * 4
            while True:
                prog = False
                for bi in range(4):
                    if tl_act[bi] is None and tl_todo:
                        tl_act[bi] = tile_gen(tl_todo.pop(0), bi)
                    if tl_act[bi] is not None:
                        prog = True
                        try:
                            next(tl_act[bi])
                        except StopIteration:
                            tl_act[bi] = None
                if not prog:
                    break
            ada_block(4)
            ada_block(5)
            k.op("dve", lambda e: e.tensor_scalar(out=gate[:], in0=mod[:, 2 * D:3 * D], scalar1=0.5,
                                                  scalar2=None, op0=ALU.mult),
                 r=[("mod", 4), ("mod", 5)], w=["gate"])
            k.barrier()
        p0.close()
        p01.close()

        if DEBUG == "h":
            dbg32 = sb("dbg32", [128, T])
            for kt in range(8):
                k.op("dve", lambda e: e.tensor_copy(out=dbg32[:], in_=h_fm[:, kt, :]), r=[("h", i) for i in range(NT)], w=["dbg32"])
                k.dma("sp", dbg_d[:, kt * T:(kt + 1) * T], dbg32[:], r=["dbg32"])

        hkeys = [("h", i) for i in range(NT)]

        def proj_fm(wt, wkey, wcol, tt4, b):
            for kt in range(8):
                k.op("pe", lambda e: e.matmul(psf[b][:, :], lhsT=wt[:, kt, wcol:wcol + 128],
                                              rhs=h_fm[:, kt, tt4 * 512:(tt4 + 1) * 512],
                                              start=(kt == 0), stop=(kt == 7)),
                     r=[wkey] + hkeys[tt4 * 4:tt4 * 4 + 4], w=bkeys(b), sig=(kt == 7))

        def TT(en, out, a, b, op, r, w):
            return k.op(en, lambda e: e.tensor_tensor(out=out, in0=a, in1=b, op=op), r=r, w=w)

        def TS(en, out, a, s1, op0, r, w, s2=None, op1=None):
            if op1 is None:
                return k.op(en, lambda e: e.tensor_scalar(out=out, in0=a, scalar1=s1, scalar2=None, op0=op0), r=r, w=w)
            return k.op(en, lambda e: e.tensor_scalar(out=out, in0=a, scalar1=s1, scalar2=s2, op0=op0, op1=op1), r=r, w=w)

        def STT(out, a, s, b, op0, op1, r, w):
            return k.op("dve", lambda e: e.scalar_tensor_tensor(out=out, in0=a, scalar=s, in1=b, op0=op0, op1=op1), r=r, w=w)

        def ACTV(out, in_, func, r, w, scale=1.0, bias=None):
            if bias is None:
                return k.op("act", lambda e: e.activation(out=out, in_=in_, func=func, scale=scale), r=r, w=w)
            return k.op("act", lambda e: e.activation(out=out, in_=in_, func=func, scale=scale, bias=bias), r=r, w=w)

        def MM(out, lhsT, rhs, r, w, start=True, stop=True, sig=True):
            return k.op("pe", lambda e: e.matmul(out, lhsT=lhsT, rhs=rhs, start=start, stop=stop), r=r, w=w, sig=sig)

        def DVT(out, in_, r, w):
            return k.op("dve", lambda e: e.transpose(out=out, in_=in_), r=r, w=w)

        def wload(dst, key, src_cols, nkt=8):
            k.dma("pool", dst, src_cols.rearrange("(kt p) n -> p kt n", p=128), w=[key])

        ya_fm = sb("ya_fm", [128, 8, T], BF16)
        ma_fm = ya_fm
        RS = float(128 ** -0.5)

        k.banks = [0, 1]
        k.quarters = [(b, 0) for b in (2, 3, 4, 5, 0, 1)]
        with ExitStack() as pA:
          if os.environ.get('KSKIPA') != '1':
              sbA = lambda name, shape, dty=F32: pA.enter_context(nc.sbuf_tensor("sb_" + name, shape, dty))
              convw = sbA("convw", [128, 24, 4])
              alog = sbA("alog", [128, 8]); dtb = sbA("dtb", [128, 8]); dnw = sbA("dnw", [128, 1])
              k.dma("sp", convw[:], convw_d[:, :, :], w=["convw"])
              k.dma("sp", alog[:], alog_d[:, :], w=["alog"])
              k.dma("sp", dtb[:], dtb_d[:, :], w=["dtb"])
              k.dma("sp", dnw[:], dnw_d[:, :], w=["dnw"])
              TS("dve", dnw[:], dnw[:], 0.5, ALU.mult, r=["dnw"], w=["dnw"])
              wba = sbA("wba", [128, 8, 128], BF16)
              if os.environ.get("KX") != "2":
                  wload(wba[:], "wba", win_d[:, 3984:4112])
              else:
                  k.op("pool", lambda e: e.memset(wba[:], 0.0), w=["wba"])
              ba = sbA("ba", [128, NT, 16])
              beta = sbA("beta", [128, NT, 8]); lnb = sbA("lnb", [128, NT, 8]); gt = sbA("gt", [128, NT, 8])
              ngc = sbA("ngc", [128, NT, 8]); egc = sbA("egc", [128, NT, 8]); ekd = sbA("ekd", [128, NT, 8])
              bg = sbA("bg", [128, NT, 8]); ea = sbA("ea", [128, 8])
              for tt in range(NT):
                  q_ = k.quarter()
                  for kt in range(8):
                      MM(psq(q_)[:, 0:16], h_fm[:, kt, tt * 128:(tt + 1) * 128], wba[:, kt, 112:128], r=["wba", ("h", tt)],
                         w=[("ps",) + q_], start=(kt == 0), stop=(kt == 7), sig=(kt == 7))
                  k.op("dve", lambda e: e.tensor_copy(out=ba[:, tt, :], in_=psq(q_)[:, 0:16]), r=[("ps",) + q_], w=[("ba", tt)])
              ACTV(ea[:], alog[:], AF.Exp, r=["alog"], w=["ea"])
              allk = [("col", tt) for tt in range(NT)]
              bak = [("ba", tt) for tt in range(NT)]
              F2 = lambda t_: t_[:].rearrange("p a b -> p (a b)")
              bc8 = lambda t_: t_[:].unsqueeze(1).to_broadcast([128, NT, 8])
              ACTV(beta[:], ba[:, :, 0:8], AF.Tanh, r=bak, w=allk, scale=0.5)
              TS("dve", F2(beta), F2(beta), 0.5, ALU.mult, r=allk, w=allk, s2=0.5, op1=ALU.add)
              TT("dve", gt[:], ba[:, :, 8:16], bc8(dtb), ALU.add, r=bak + ["dtb"], w=allk)
              ACTV(F2(lnb), F2(beta), AF.Ln, r=allk, w=allk)
              ACTV(F2(gt), F2(gt), AF.Exp, r=allk, w=allk)
              ACTV(F2(gt), F2(gt), AF.Ln, r=allk, w=allk, bias=epsc[:, 2:3])
              TS("dve", F2(gt), F2(gt), -1.0, ALU.mult, r=allk, w=allk)
              TT("dve", gt[:], gt[:], bc8(ea), ALU.mult, r=allk + ["ea"], w=allk)
              q1, q2 = k.quarter(), k.quarter()
              MM(psq(q1), TLE, F2(gt), r=allk + ["cf"], w=[("ps",) + q1])
              MM(psq(q2), TGT, F2(gt), r=allk + ["cf"], w=[("ps",) + q2])
              TS("dve", F2(ngc), psq(q1), -1.0, ALU.mult, r=[("ps",) + q1], w=allk)
              ACTV(F2(egc), psq(q1), AF.Exp, r=[("ps",) + q1], w=allk)
              ACTV(F2(ekd), psq(q2), AF.Exp, r=[("ps",) + q2], w=allk)
              TT("dve", F2(bg), F2(beta), F2(egc), ALU.mult, r=allk, w=allk)

              q_fm2 = sbA("q_fm2", [128, 2, T], BF16)
              k_fm2 = sbA("k_fm2", [128, 2, T], BF16)
              k_tok2 = sbA("k_tok2", [128, NT, 2, 128], BF16)
              v_tok2 = sbA("v_tok2", [128, NT, 2, 128], BF16)
              q_fm = [q_fm2[:, i, :] for i in range(2)]
              k_fm = [k_fm2[:, i, :] for i in range(2)]
              k_tok = [k_tok2[:, :, i, :] for i in range(2)]
              v_tok = [v_tok2[:, :, i, :] for i in range(2)]
              zg2 = sbA("zg2", [128, 2, T], BF16)
              zg = [zg2[:, i, :] for i in range(2)]
              S_f2 = sbA("S_f2", [128, 2, 128])
              S_b2 = sbA("S_b2", [128, 2, 128], BF16)
              S_f = [S_f2[:, i, :] for i in range(2)]
              S_b = [S_b2[:, i, :] for i in range(2)]
              RT = int(os.environ.get('KRT', '8'))
              NB = int(os.environ.get('KNB', '4'))
              UDELAY = int(os.environ.get('KUDELAY', '4'))
              BSTAG = int(os.environ.get('KBSTAG', '4'))
              wk_p = sbA("wk_p", [128, 8, 256], BF16)
              wload(wk_p[:], ("wblk", 1), win_d[:, 1024: 1024 + 256])
              npsb = [0]

              for gi in range(4 if STOP >= 2 else 0):
                  h0 = 2 * gi
                  sA1 = ExitStack()
                  sb1_ = lambda name, shape, dty=F32: sA1.enter_context(nc.sbuf_tensor(f"sb_{name}_g{gi}", shape, dty))
                  k.banks = [0, 1, 2, 3, 4, 5]
                  wblk = [sb1_(f"wblk{i}", [128, 8, 256], BF16) for i in range(3)]
                  wblk = [wblk[0], wk_p, wblk[1], wblk[2]]
                  for j, base in enumerate((0, 1024, 2048, 3072)):
                      if j != 1:
                          wload(wblk[j][:], ("wblk", j), win_d[:, base + h0 * 128: base + h0 * 128 + 256])
                  pre = [sb1_(f"pre{i}", [128, 3 + T], BF16) for i in range(2)]
                  dwg = sb1_("dwg", [128, 6, 4, 128], BF16)
                  for kd_ in range(3):
                      TT("pool", dwg[:, 2 * kd_:2 * kd_ + 2, :, :], ident.unsqueeze(1).unsqueeze(1).to_broadcast([128, 2, 4, 128]),
                         convw[:, kd_ * 8 + h0: kd_ * 8 + h0 + 2, :].unsqueeze(3).to_broadcast([128, 2, 4, 128]), ALU.mult,
                         r=["cb", "convw"], w=["dwg"])
                  accs = [[sb1_(f"acc{u}{i}", [128, 512]) for i in range(4)] for u in range(2)]
                  tnhs = [[sb1_(f"tnh{u}{i}", [128, 512]) for i in range(4)] for u in range(2)]
                  sqbs = [[sb1_(f"sqb{u}{i}", [128, 512], BF16) for i in range(4)] for u in range(2)]
                  for u in range(2):
                      k.op("pool", lambda e: e.memset(pre[u][:, 0:3], 0.0), w=[("prehalo", u)])

                  def unit_gen(kind, hh, ub, delay=0):
                      for _ in range(delay):
                          yield
                      ci = kind * 8 + h0 + hh
                      pre_ = pre[ub]
                      for tt4 in range(4):
                          b = k.bank()
                          proj_fm(wblk[kind], ("wblk", kind), hh * 128, tt4, b)
                          k.op("dve", lambda e: e.tensor_copy(out=pre_[:, 3 + tt4 * 512: 3 + (tt4 + 1) * 512], in_=psf[b][:, :]),
                               r=bkeys(b), w=[("pre", ub, tt4)])
                          yield
                      subs = [sub_gen(kind, hh, ub, ci, pre_, tt4) for tt4 in range(4)]
                      while subs:
                          for sg_ in list(subs):
                              try:
                                  next(sg_)
                              except StopIteration:
                                  subs.remove(sg_)
                          yield

                  def sub_gen(kind, hh, ub, ci, pre_, tt4):
                      if True:
                          sx = tt4
                          acc, tnh, sqb = accs[ub][sx], tnhs[ub][sx], sqbs[ub][sx]
                          ka, kt_, ks = ("acc", ub, sx), ("tnh", ub, sx), ("sqb", ub, sx)
                          prek = [("pre", ub, tt4)] + ([("pre", ub, tt4 - 1)] if tt4 else [("prehalo", ub)])
                          c0 = tt4 * 512
                          ui_ = kind * 2 + hh
                          b = k.bank()
                          for j in range(4):
                              MM(psf[b][:, :], dwg[:, ui_, j, :], pre_[:, c0 + j:c0 + j + 512], r=prek + ["dwg"], w=bkeys(b),
                                 start=(j == 0), stop=(j == 3), sig=(j == 3))
                          ACTV(tnh[:], psf[b][:, :], AF.Tanh, r=bkeys(b), w=[kt_], scale=0.5)
                          if kind < 2:
                              STT(acc[:], tnh[:], 1.0, psf[b][:, :], ALU.add, ALU.mult, r=[kt_] + bkeys(b), w=[ka])
                          else:
                              STT(sqb[:], tnh[:], 1.0, psf[b][:, :], ALU.add, ALU.mult, r=[kt_] + bkeys(b), w=[ks])
                          yield
                          if kind < 2:
                              dst = (q_fm, k_fm)[kind][hh]
                              dk_ = (("qfm", hh, tt4) if kind == 0 else ("kfm", hh, tt4))
                              ACTV(sqb[:], acc[:], AF.Square, r=[ka], w=[ks])
                              yield
                              b = k.bank()
                              MM(psf[b][:, :], ones_b, sqb[:], r=["cb", ks], w=bkeys(b))
                              ACTV(tnh[:], psf[b][:, :], AF.Ln, r=bkeys(b), w=[kt_], bias=epsc[:, 1:2])
                              yield
                              ACTV(tnh[:], tnh[:], AF.Exp, r=[kt_], w=[kt_], scale=-0.5)
                              yield
                              if kind == 0:
                                  STT(dst[:, c0:c0 + 512], acc[:], RS, tnh[:], ALU.mult, ALU.mult, r=[ka, kt_], w=[dk_])
                              else:
                                  TT("dve", dst[:, c0:c0 + 512], acc[:], tnh[:], ALU.mult, r=[ka, kt_], w=[dk_])
                              yield
                              if kind == 1:
                                  pb_ = npsb[0] % 2
                                  npsb[0] += 1
                                  for j in range(4):
                                      k.op("pe", lambda e: e.transpose(psb[pb_][:, j * 128:(j + 1) * 128],
                                                                       dst[:, c0 + j * 128:c0 + (j + 1) * 128], ident),
                                           r=[dk_, "cb"], w=[("psb", pb_)], sig=(j == 3))
                                  k.op("dve", lambda e: e.tensor_copy(out=k_tok[hh][:, tt4 * 4:(tt4 + 1) * 4, :],
                                                                      in_=psb[pb_][:, 0:512].rearrange("p (a b) -> p a b", a=4)),
                                       r=[("psb", pb_)], w=[("ktok", hh, tt4)])
                                  yield
                          else:
                              pb_ = npsb[0] % 2
                              npsb[0] += 1
                              for j in range(4):
                                  k.op("pe", lambda e: e.transpose(psb[pb_][:, j * 128:(j + 1) * 128], sqb[:, j * 128:(j + 1) * 128], ident),
                                       r=[ks, "cb"], w=[("psb", pb_)], sig=(j == 3))
                              TS("dve", v_tok[hh][:, tt4 * 4:(tt4 + 1) * 4, :], psb[pb_][:, 0:512].rearrange("p (a b) -> p a b", a=4),
                                 0.5, ALU.mult, r=[("psb", pb_)], w=[("vtok", hh, tt4)])
                              yield

                  def za_gen(hh, ub):
                      for tt4 in range(4):
                          tnh = tnhs[ub][tt4 % 2]
                          kt_ = ("tnh", ub, tt4 % 2)
                          b = k.bank()
                          proj_fm(wblk[3], ("wblk", 3), hh * 128, tt4, b)
                          ACTV(tnh[:], psf[b][:, :], AF.Tanh, r=bkeys(b), w=[kt_], scale=0.5)
                          STT(zg2[:, hh, tt4 * 512:(tt4 + 1) * 512], tnh[:], 1.0, psf[b][:, :], ALU.add, ALU.mult,
                              r=[kt_] + bkeys(b), w=[("zg", hh, tt4)])
                          yield

                  ulist = [("u", kind, hh) for kind in (1, 0, 2) for hh in range(2)] + [("z", 0, hh) for hh in range(2)]
                  uact = [None, None]
                  ufirst = [False]
                  while True:
                      prog = False
                      for ub in range(2):
                          if uact[ub] is None and ulist:
                              t_, kind, hh = ulist.pop(0)
                              dly = UDELAY if (ub == 1 and not ufirst[0]) else 0
                              if ub == 1:
                                  ufirst[0] = True
                              uact[ub] = unit_gen(kind, hh, ub, dly) if t_ == "u" else za_gen(hh, ub)
                          if uact[ub] is not None:
                              prog = True
                              try:
                                  next(uact[ub])
                              except StopIteration:
                                  uact[ub] = None
                      if not prog:
                          break
                  for hh in range(2):
                      k.op("pool", lambda e: e.memset(S_f[hh], 0.0), w=[("S", hh)])
                      k.op("pool", lambda e: e.memset(S_b[hh], 0.0), w=[("Sb", hh)])
                  k.barrier()
                  sA1.close()

                  sA2 = ExitStack()
                  sb2_ = lambda name, shape, dty=F32: sA2.enter_context(nc.sbuf_tensor(f"sb_{name}_g{gi}", shape, dty))
                  scr = {}
                  pbs = []
                  for si in range(NB):
                      d_ = {}
                      for nm, dty in (("Gm", F32), ("M2", F32), ("U", BF16),
                                      ("M1", BF16), ("vb", BF16), ("kbg", BF16),
                                      ("Uo0", BF16), ("Uo1", BF16)):
                          d_[nm] = sb2_(f"{nm}_b{si}", [128, 4, 128], dty)
                      for a_, b_ in (("t1", "Gm"), ("DT", "Gm"), ("t2", "M2"), ("CU", "M2"), ("M1b", "M1")):
                          d_[a_] = d_[b_]
                      gv8 = d_["Gm"][:].bitcast(BF16).rearrange("p g (h c) -> p (g h) c", h=2)
                      mv8 = d_["M2"][:].bitcast(BF16).rearrange("p g (h c) -> p (g h) c", h=2)
                      d_["N0"], d_["N1"] = gv8[:, 0:4, :], gv8[:, 4:8, :]
                      d_["V0"], d_["V1"] = mv8[:, 0:4, :], mv8[:, 4:8, :]
                      pbs.append(d_)
                  ring = {}
                  for nm, dty in (("EG", F32), ("u", F32), ("wT", BF16), ("Qd", BF16), ("Aq", BF16), ("kdec", BF16)):
                      ring[nm] = sb2_(f"ring_{nm}", [128, RT, 2, 128], dty)
                  scr2 = {}
                  for nm, dty in (("vnew", BF16), ("sq", BF16), ("osb", F32), ("ln", F32), ("tmp", F32)):
                      scr2[nm] = sb2_(f"{nm}_2", [128, 2, 128], dty)
                  if gi < 3:
                      wload(wk_p[:], ("wblk", 1), win_d[:, 1024 + (h0 + 2) * 128: 1024 + (h0 + 2) * 128 + 256])
                  ALIAS = {"t1": "Gm", "DT": "Gm", "t2": "M2", "CU": "M2", "M1b": "M1", "N0": "Gm", "N1": "Gm", "V0": "M2", "V1": "M2"}

                  def prepb_gen(t0, si):
                      P = pbs[si]
                      s0 = t0 % RT
                      chains = [(tl, hh) for tl in range(2) for hh in range(2)]
                      A3 = lambda nm: P[nm][:]
                      A2 = lambda nm: P[nm][:].rearrange("p g c -> p (g c)")
                      A4 = lambda nm: P[nm][:].rearrange("p (a b) c -> p a b c", a=2)
                      Ag = lambda nm, g: P[nm][:, g, :]
                      PK = lambda nm: ("pb_" + ALIAS.get(nm, nm), si)
                      colv = lambda t_: t_[:, t0:t0 + 2, h0:h0 + 2].unsqueeze(3).to_broadcast([128, 2, 2, 128])
                      c3 = lambda c_: c_.unsqueeze(1).to_broadcast([128, 4, 128])
                      c4 = lambda c_: c_.unsqueeze(1).unsqueeze(1).to_broadcast([128, 2, 2, 128])
                      R4 = lambda nm: ring[nm][:, s0:s0 + 2, :, :]
                      RKs = lambda nm: [(nm, s0 + tl, hh) for tl, hh in chains]
                      ps3 = lambda b: psf[b][:, :].rearrange("p (g c) -> p g c", g=4)
                      ps4 = lambda b: psf[b][:, :].rearrange("p (a b c) -> p a b c", a=2, b=2)
                      psg = lambda b, g: psf[b][:, g * 128:(g + 1) * 128]
                      ck = [("col", t0), ("col", t0 + 1)]
                      qk_ = [("qfm", hh, t_ // 4) for hh in range(2) for t_ in (t0, t0 + 1)]
                      kk_ = [("kfm", hh, t_ // 4) for hh in range(2) for t_ in (t0, t0 + 1)]
                      vk_ = [("vtok", hh, t_ // 4) for hh in range(2) for t_ in (t0, t0 + 1)]
                      tk_ = [("ktok", hh, t_ // 4) for hh in range(2) for t_ in (t0, t0 + 1)]

                      def bank1():
                          while (bs_ := k.qalloc(1)) is None:
                              yield None
                          yield bs_[0][0]

                      TT("pool", A4("Gm"), c4(TLE), colv(gt), ALU.mult, r=ck + ["cf"], w=[PK("Gm")])
                      TT("pool", A4("M2"), c4(ident_f), colv(lnb), ALU.mult, r=ck + ["cf"], w=[PK("M2")])
                      for g, (tl, hh) in enumerate(chains):
                          tt = t0 + tl
                          h = h0 + hh
                          ACTV(Ag("vb", g), v_tok[hh][:, tt, :], AF.Copy, r=vk_ + ck, w=[PK("vb")], scale=beta[:, tt, h:h + 1])
                          ACTV(Ag("kbg", g), k_tok[hh][:, tt, :], AF.Copy, r=tk_ + ck, w=[PK("kbg")], scale=bg[:, tt, h:h + 1])
                          ACTV(ring["kdec"][:, s0 + tl, hh, :], k_tok[hh][:, tt, :], AF.Copy, r=tk_ + ck, w=[("kdec", s0 + tl, hh)],
                               scale=ekd[:, tt, h:h + 1])
                      yield
                      TT("pool", A3("M2"), A3("M2"), A3("Gm"), ALU.add, r=[PK("M2"), PK("Gm")], w=[PK("M2")])
                      while (bs_ := k.qalloc(2)) is None:
                          yield
                      bA, bB = bs_[0][0], bs_[1][0]
                      MM(psf[bA][:, :], ones_f, A2("Gm"), r=["cf", PK("Gm")], w=bkeys(bA))
                      MM(psf[bB][:, :], ones_f, A2("M2"), r=["cf", PK("M2")], w=bkeys(bB))
                      yield
                      TT("dve", A3("t1"), ps3(bA), c3(mU_incl), ALU.add, r=bkeys(bA) + ["cf"], w=[PK("t1")])
                      ACTV(R4("EG"), ps4(bA), AF.Exp, r=bkeys(bA), w=RKs("EG"))
                      TT("dve", A3("t2"), ps3(bB), c3(mU_strict), ALU.add, r=bkeys(bB) + ["cf"], w=[PK("t2")])
                      k.qfree((bA, 0), (bB, 0))
                      yield
                      TT("dve", A4("t1"), A4("t1"), colv(ngc), ALU.add, r=[PK("t1")] + ck, w=[PK("t1")])
                      TT("dve", A4("t2"), A4("t2"), colv(ngc), ALU.add, r=[PK("t2")] + ck, w=[PK("t2")])
                      yield
                      ACTV(A2("DT"), A2("t1"), AF.Exp, r=[PK("t1")], w=[PK("DT")])
                      ACTV(A2("CU"), A2("t2"), AF.Exp, r=[PK("t2")], w=[PK("CU")])
                      while (bs_ := k.qalloc(2)) is None:
                          yield
                      bC, bD = bs_[0][0], bs_[1][0]
                      for g, (tl, hh) in enumerate(chains):
                          sl = slice((t0 + tl) * 128, (t0 + tl + 1) * 128)
                          MM(psg(bC, g), k_fm[hh][:, sl], k_fm[hh][:, sl], r=kk_, w=bkeys(bC), sig=(g == 3))
                      for g, (tl, hh) in enumerate(chains):
                          sl = slice((t0 + tl) * 128, (t0 + tl + 1) * 128)
                          MM(psg(bD, g), k_fm[hh][:, sl], q_fm[hh][:, sl], r=kk_ + qk_, w=bkeys(bD), sig=(g == 3))
                      yield
                      TT("dve", A3("U"), ps3(bC), A3("CU"), ALU.mult, r=bkeys(bC) + [PK("CU")], w=[PK("U")])
                      TT("dve", R4("Aq"), ps4(bD), A4("DT"), ALU.mult, r=bkeys(bD) + [PK("DT")], w=RKs("Aq"))
                      k.qfree((bC, 0), (bD, 0))
                      TT("pool", R4("Qd"), q_fm2[:, :, t0 * 128:(t0 + 2) * 128].rearrange("p h (t c) -> p t h c", t=2), R4("EG"),
                         ALU.mult, r=qk_ + RKs("EG"), w=RKs("Qd"))
                      yield
                      Tn, Ttn, Un = ("N0", "N1"), ("V0", "V1"), ("Uo0", "Uo1")
                      offb = lambda b_: c3(offall[:, OFFI[b_] * 128:(OFFI[b_] + 1) * 128])
                      TT("pool", A3(Un[0]), A3("U"), offb(1), ALU.mult, r=[PK("U"), "cb"], w=[PK(Un[0])])
                      TT("pool", A3(Ttn[0]), c3(ident), A3(Un[0]), ALU.subtract, r=["cb", PK(Un[0])], w=[PK(Ttn[0])])
                      TT("pool", A3(Un[1]), A3("U"), offb(2), ALU.mult, r=[PK("U"), "cb"], w=[PK(Un[1])])
                      yield
                      DVT(A2(Tn[0]), A2(Ttn[0]), r=[PK(Ttn[0])], w=[PK(Tn[0])])
                      cur = 0
                      levels = (2, 4, 8, 16, 32, 64)
                      for li, bsz in enumerate(levels):
                          un = Un[(li + 1) % 2]
                          while (bs_ := k.qalloc(1)) is None:
                              yield
                          b1 = bs_[0][0]
                          for g in range(4):
                              MM(psg(b1, g), Ag(un, g), Ag(Tn[cur], g), r=[PK(un), PK(Tn[cur])], w=bkeys(b1), sig=(g == 3))
                          yield
                          ACTV(A2("M1"), psf[b1][:, :], AF.Copy, r=bkeys(b1), w=[PK("M1")])
                          k.qfree((b1, 0))
                          if li + 1 < len(levels):
                              TT("pool", A3(Un[li % 2]), A3("U"), offb(levels[li + 1]), ALU.mult, r=[PK("U"), "cb"], w=[PK(Un[li % 2])])
                          yield
                          if bsz < 32:
                              while (bs_ := k.qalloc(1)) is None:
                                  yield
                              b2 = bs_[0][0]
                              for g in range(4):
                                  MM(psg(b2, g), Ag(Ttn[cur], g), Ag("M1", g), r=[PK(Ttn[cur]), PK("M1")], w=bkeys(b2), sig=(g == 3))
                              yield
                              TT("dve", A3(Tn[1 - cur]), A3(Tn[cur]), ps3(b2), ALU.subtract, r=bkeys(b2) + [PK(Tn[cur])], w=[PK(Tn[1 - cur])])
                              k.qfree((b2, 0))
                              DVT(A2(Ttn[1 - cur]), A2(Tn[1 - cur]), r=[PK(Tn[1 - cur])], w=[PK(Ttn[1 - cur])])
                              cur = 1 - cur
                              yield
                          elif bsz == 32:
                              T32, T32T = Tn[cur], Ttn[cur]
                              while (bs_ := k.qalloc(2)) is None:
                                  yield
                              b2, b3 = bs_[0][0], bs_[1][0]
                              for g in range(4):
                                  MM(psg(b2, g), Ag("M1", g), Ag(T32T, g), r=[PK("M1"), PK(T32T)], w=bkeys(b2), sig=(g == 3))
                              for g in range(4):
                                  MM(psg(b3, g), Ag(T32T, g), Ag("M1", g), r=[PK("M1"), PK(T32T)], w=bkeys(b3), sig=(g == 3))
                              yield
                              T64n, T64Tn = Tn[1 - cur], Ttn[1 - cur]
                              TT("dve", A3(T64Tn), A3(T32T), ps3(b2), ALU.subtract, r=bkeys(b2) + [PK(T32T)], w=[PK(T64Tn)])
                              TT("dve", A3(T64n), A3(T32), ps3(b3), ALU.subtract, r=bkeys(b3) + [PK(T32)], w=[PK(T64n)])
                              k.qfree((b2, 0), (b3, 0))
                              TTn = T32T
                              Tn = (T64n, T64n)
                              cur = 0
                              yield
                          else:
                              while (bs_ := k.qalloc(1)) is None:
                                  yield
                              b2 = bs_[0][0]
                              for g in range(4):
                                  MM(psg(b2, g), Ag("M1", g), Ag(T64Tn, g), r=[PK("M1"), PK(T64Tn)], w=bkeys(b2), sig=(g == 3))
                              yield
                              TT("dve", A3(TTn), A3(T64Tn), ps3(b2), ALU.subtract, r=bkeys(b2) + [PK(T64Tn)], w=[PK(TTn)])
                              k.qfree((b2, 0))
                              yield
                      while (bs_ := k.qalloc(2)) is None:
                          yield
                      bU, bW = bs_[0][0], bs_[1][0]
                      for g in range(4):
                          MM(psg(bU, g), Ag(TTn, g), Ag("vb", g), r=[PK(TTn), PK("vb")], w=bkeys(bU), sig=(g == 3))
                      for g in range(4):
                          MM(psg(bW, g), Ag("kbg", g), Ag(TTn, g), r=[PK(TTn), PK("kbg")], w=bkeys(bW), sig=(g == 3))
                      yield
                      ACTV(R4("u"), ps4(bU), AF.Copy, r=bkeys(bU), w=RKs("u"))
                      ACTV(R4("wT"), ps4(bW), AF.Copy, r=bkeys(bW), w=RKs("wT"))
                      k.qfree((bU, 0), (bW, 0))

                  def rec_gen(hh_unused, tt):
                      sl = slice(tt * 128, (tt + 1) * 128)
                      tt4 = tt // 4
                      s_ = tt % RT
                      X2 = lambda nm: scr2[nm][:]
                      X2f = lambda nm: scr2[nm][:].rearrange("p h c -> p (h c)")
                      XK = lambda nm: ("rec", nm)
                      R3 = lambda nm: ring[nm][:, s_, :, :]
                      RKs = lambda nm: [(nm, s_, 0), (nm, s_, 1)]
                      half = lambda b, hh: psf[b][:, hh * 128:(hh + 1) * 128]
                      ps2 = lambda b: psf[b][:, 0:256].rearrange("p (h c) -> p h c", h=2)
                      SK = [("S", 0), ("S", 1)]
                      SbK = [("Sb", 0), ("Sb", 1)]
                      while (bs_ := k.qalloc(1)) is None:
                          yield
                      qa = bs_[0][0]
                      for hh in range(2):
                          MM(half(qa, hh), ring["wT"][:, s_, hh, :], S_b2[:, hh, :], r=RKs("wT") + SbK, w=bkeys(qa), sig=(hh == 1))
                      yield
                      TT("dve", X2("vnew"), R3("u"), ps2(qa), ALU.subtract, r=RKs("u") + bkeys(qa), w=[XK("vnew")])
                      k.qfree((qa, 0))
                      yield
                      while (bs_ := k.qalloc(2)) is None:
                          yield
                      qo, qs = bs_[0][0], bs_[1][0]
                      for hh in range(2):
                          MM(half(qs, hh), ring["kdec"][:, s_, hh, :], scr2["vnew"][:, hh, :], r=RKs("kdec") + [XK("vnew")], w=bkeys(qs), sig=(hh == 1))
                      for hh in range(2):
                          MM(half(qo, hh), S_b2[:, hh, :], ring["Qd"][:, s_, hh, :], r=SbK + RKs("Qd"), w=bkeys(qo), stop=False, sig=False)
                          MM(half(qo, hh), scr2["vnew"][:, hh, :], ring["Aq"][:, s_, hh, :], r=[XK("vnew")] + RKs("Aq"), w=bkeys(qo), start=False,
                             sig=(hh == 1))
                      yield
                      for hh in range(2):
                          STT(S_f2[:, hh, :], S_f2[:, hh, :], ring["EG"][:, s_, hh, 127:128], half(qs, hh), ALU.mult, ALU.add,
                              r=SK + RKs("EG") + bkeys(qs), w=[("S", hh)])
                      k.qfree((qs, 0))
                      ACTV(X2("sq"), ps2(qo), AF.Square, r=bkeys(qo), w=[XK("sq")])
                      yield
                      ACTV(S_b2[:], S_f2[:], AF.Copy, r=SK, w=SbK)
                      k.op("dve", lambda e: e.tensor_copy(out=X2("osb"), in_=ps2(qo)), r=bkeys(qo), w=[XK("osb")])
                      k.qfree((qo, 0))
                      while (bs_ := k.qalloc(1)) is None:
                          yield
                      qn = bs_[0][0]
                      MM(psf[qn][:, 0:256], ones_b, X2f("sq"), r=["cb", XK("sq")], w=bkeys(qn))
                      yield
                      ACTV(X2("ln"), ps2(qn), AF.Ln, r=bkeys(qn), w=[XK("ln")], scale=1.0 / 128, bias=epsc[:, 0:1])
                      k.qfree((qn, 0))
                      ACTV(X2("ln"), X2("ln"), AF.Exp, r=[XK("ln")], w=[XK("ln")], scale=-0.5)
                      yield
                      STT(X2f("tmp"), X2f("osb"), dnw[:, 0:1], X2f("ln"), ALU.mult, ALU.mult, r=[XK("osb"), XK("ln"), "dnw"], w=[XK("tmp")])
                      TT("dve", ya_fm[:, h0:h0 + 2, sl], X2("tmp"), zg2[:, :, sl], ALU.mult, r=[XK("tmp"), ("zg", 0, tt4), ("zg", 1, tt4)],
                         w=[("ya", h0, tt), ("ya", h0 + 1, tt)])

                  ntl = NT if STOP >= 3 else 0
                  todo = list(range(0, ntl, 2))
                  act = [None] * NB
                  done = set()
                  rec_act = [None, None]
                  rec_next = [0, 0]
                  rnd_ = 0
                  while True:
                      progressed = False
                      rnd_ += 1
                      for si in range(NB):
                          if act[si] is None and todo:
                              t0_ = todo[0]
                              if rnd_ <= si * BSTAG:
                                  progressed = True
                              elif t0_ + 1 < min(rec_next) + RT:
                                  todo.pop(0)
                                  act[si] = (prepb_gen(t0_, si), t0_)
                          if act[si] is not None:
                              progressed = True
                              try:
                                  next(act[si][0])
                              except StopIteration:
                                  for hh in range(2):
                                      done.add((hh, act[si][1]))
                                      done.add((hh, act[si][1] + 1))
                                  act[si] = None
                      if rec_act[0] is None and rec_next[0] < ntl and (0, rec_next[0]) in done:
                          rec_act[0] = rec_gen(0, rec_next[0])
                      if rec_act[0] is not None:
                          progressed = True
                          try:
                              next(rec_act[0])
                          except StopIteration:
                              rec_act[0] = None
                              rec_next[0] += 1
                              rec_next[1] += 1
                      if not progressed:
                          break
                  assert rec_next == [ntl, ntl] and not todo
                  k.barrier()
                  sA2.close()
              k.barrier()

        def dump_fm(tag, src, nk):
            if DEBUG != tag:
                return
            with ExitStack() as dbs:
                dbg32 = dbs.enter_context(nc.sbuf_tensor("sb_dbg32_" + tag, [128, T], F32))
                for kt in range(nk):
                    k.op("dve", lambda e: e.tensor_copy(out=dbg32[:], in_=src[:, kt, :]), w=["dbg32"])
                    k.dma("sp", dbg_d[:, kt * T:(kt + 1) * T], dbg32[:], r=["dbg32"])
                k.barrier()

        dump_fm("ya", ya_fm, 8)

        k.banks = [0, 1, 2, 3, 4, 5]
        with ExitStack() as pA2:
            sbx = lambda name, shape, dty=F32: pA2.enter_context(nc.sbuf_tensor("sb_" + name, shape, dty))
            wga = [sbx(f"wga{i}", [128, 8, 128], BF16) for i in range(2)]
            paj = [sbx(f"paj{i}", [128, 8, 128], BF16) for i in range(2)]
            tnh2 = [sbx(f"tnhA{i}", [128, 512]) for i in range(2)]
            ma_tmp = sbx("ma_tmp", [128, 8, T], BF16)
            n_ = 0
            def a2_loads(j):
                wload(wga[j % 2][:], ("wga", j % 2), win_d[:, 7184 + j * 128: 7184 + (j + 1) * 128])
                wload(paj[j % 2][:], ("paj", j % 2), pa_d[:, j * 128:(j + 1) * 128])
            a2_loads(0)
            for j in range(8):
                if j + 1 < 8:
                    a2_loads(j + 1)
                for tt4 in range(4):
                    cs = slice(tt4 * 512, (tt4 + 1) * 512)
                    tb = tnh2[n_ % 2]
                    tk = ("tnhA", n_ % 2)
                    n_ += 1
                    b1 = k.bank()
                    proj_fm(wga[j % 2], ("wga", j % 2), 0, tt4, b1)
                    ACTV(tb[:], psf[b1][:, :], AF.Tanh, r=bkeys(b1), w=[tk], scale=0.5)
                    b2 = k.bank()
                    for hk in range(8):
                        MM(psf[b2][:, :], paj[j % 2][:, hk, :], ya_fm[:, hk, cs], r=[("paj", j % 2)], w=bkeys(b2),
                           start=(hk == 0), stop=(hk == 7), sig=(hk == 7))
                    STT(ma_tmp[:, j, cs], tb[:], 1.0, psf[b2][:, :], ALU.add, ALU.mult, r=[tk] + bkeys(b2), w=[("mat", j, tt4)])
            k.barrier()
            for j in range(8):
                if j % 4 == 3:
                    ACTV(ma_fm[:, j, :], ma_tmp[:, j, :], AF.Copy, r=[], w=[("ma", j, t_) for t_ in range(4)])
                else:
                    k.op("dve", lambda e: e.tensor_copy(out=ma_fm[:, j, :], in_=ma_tmp[:, j, :]), w=[("ma", j, t_) for t_ in range(4)])
            k.barrier()

        pB = ExitStack()
        if True:
            sbx = lambda name, shape, dty=F32: pB.enter_context(nc.sbuf_tensor("sb_" + name, shape, dty))
            yb_fm = sbx("yb_fm", [128, 6, T], BF16)
            pBi = ExitStack()
            sbi = lambda name, shape, dty=F32: pBi.enter_context(nc.sbuf_tensor("sb_" + name, shape, dty))
            zbg = [sbi(f"zbg{i}", [128, T], BF16) for i in range(3)]
            num = [sbi(f"num{i}", [128, T], BF16) for i in range(3)]
            den = [sbi(f"den{i}", [128, T]) for i in range(3)]
            qb = sbi("qb", [128, T], BF16)
            kb = sbi("kb", [128, T], BF16)
            vp = [sbi(f"vp{i}", [128, 16, 2, 128], BF16) for i in range(2)]
            PTb = [sbi(f"PT{i}", [128, 512], BF16) for i in range(3)]
            wq2 = [sbi(f"wq{i}", [128, 8, 128], BF16) for i in range(2)]; wk2 = [sbi(f"wk{i}", [128, 8, 128], BF16) for i in range(2)]
            wv2 = [sbi(f"wv{i}", [128, 8, 128], BF16) for i in range(2)]; wz2 = [sbi(f"wz{i}", [128, 8, 128], BF16) for i in range(2)]
            tnhB = [sbi(f"tnhBi{i}", [128, 512]) for i in range(2)]
            dtot = sbi("dtot", [128, T])
            tmpB = [sbi(f"tmpBi{i}", [128, 512]) for i in range(2)]
            for i in range(2):
                k.op("pool", lambda e: e.memset(vp[i][:], 0.0), w=[("vp", i, blk) for blk in range(16)])
            onesP = (onesA, onesB)
            npt = 0
            nB = 0
            for p in range(2):
                for g in range(3):
                    pt = 2 * g + p
                    d = (1, 4, 16)[g]
                    nbk = 16 // d
                    vpi = npt % 2
                    npt += 1
                    vpt = vp[vpi]
                    wq, wk, wv, wz = wq2[vpi], wk2[vpi], wv2[vpi], wz2[vpi]
                    wqk, wkk, wvk, wzk = ("wq", vpi), ("wk", vpi), ("wv", vpi), ("wz", vpi)

                    def att_loads(pt_, vi_):
                        wload(wq2[vi_][:], ("wq", vi_), win_d[:, 4112 + pt_ * 128: 4112 + (pt_ + 1) * 128])
                        wload(wk2[vi_][:], ("wk", vi_), win_d[:, 4880 + pt_ * 128: 4880 + (pt_ + 1) * 128])
                        wload(wv2[vi_][:], ("wv", vi_), win_d[:, 5648 + pt_ * 128: 5648 + (pt_ + 1) * 128])
                        wload(wz2[vi_][:], ("wz", vi_), win_d[:, 6416 + pt_ * 128: 6416 + (pt_ + 1) * 128])
                    if npt == 1:
                        att_loads(pt, vpi)
                    nxt = [(p_, g_) for p_ in range(2) for g_ in range(3)]
                    ni_ = nxt.index((p, g)) + 1
                    if ni_ < len(nxt):
                        att_loads(2 * nxt[ni_][1] + nxt[ni_][0], 1 - vpi)
                    for tt4 in range(4):
                        cs = slice(tt4 * 512, (tt4 + 1) * 512)
                        b = k.bank()
                        proj_fm(wq, wqk, 0, tt4, b)
                        ACTV(qb[:, cs], psf[b][:, :], AF.Copy, r=bkeys(b), w=[("qb", tt4)])
                        b = k.bank()
                        proj_fm(wk, wkk, 0, tt4, b)
                        k.op("dve", lambda e: e.tensor_copy(out=kb[:, cs], in_=psf[b][:, :]), r=bkeys(b), w=[("kb", tt4)])
                        b = k.bank()
                        proj_fm(wz, wzk, 0, tt4, b)
                        tb = tnhB[tt4 % 2]
                        ACTV(tb[:], psf[b][:, :], AF.Tanh, r=bkeys(b), w=[("tnhB", tt4 % 2)], scale=0.5)
                        STT(zbg[g][:, cs], tb[:], 1.0, psf[b][:, :], ALU.add, ALU.mult, r=[("tnhB", tt4 % 2)] + bkeys(b), w=[("zbg", g, tt4)])

                    def tsl(i, r_):
                        s0 = 128 * i * d + r_
                        return slice(s0, s0 + 127 * d + 1, d), list(range(s0 // 128, (s0 + 127 * d) // 128 + 1))

                    for r_ in range(d):
                        for i in range(nbk):
                            blk = r_ * nbk + i
                            sl_, tiles = tsl(i, r_)
                            b = k.bank()
                            for kt in range(8):
                                MM(psf[b][:, 0:128], h_fm[:, kt, sl_], wv[:, kt, :], r=[wvk] + [("h", t_) for t_ in tiles],
                                   w=bkeys(b), start=(kt == 0), stop=(kt == 7), sig=(kt == 7))
                            ACTV(vpt[:, blk, 0, 0:64], psf[b][:, 0:64], AF.Copy, r=bkeys(b), w=[("vp", vpi, blk)])
                            k.op("dve", lambda e: e.tensor_copy(out=vpt[:, blk, 1, 64:128], in_=psf[b][:, 64:128]),
                                 r=bkeys(b), w=[("vp", vpi, blk)])
                    def scores_blk(r_, i):
                        nonlocal nB
                        qs, qtiles = tsl(i, r_)
                        qk_ = [("qb", t_ // 4) for t_ in qtiles]
                        kck = [("kb", t_ // 4) for t_ in qtiles]
                        if i > 0:
                            kps, ptiles = tsl(i - 1, r_)
                            kpk = [("kb", t_ // 4) for t_ in ptiles]
                        b = k.bank()
                        for hd in range(2):
                            prt = slice(hd * 64, hd * 64 + 64)
                            if i > 0:
                                o_ = psf[b][:, hd * 128:(hd + 1) * 128]
                                MM(o_, kb[prt, kps], qb[prt, qs], r=kpk + qk_, w=bkeys(b), stop=False, sig=False)
                                MM(o_, ident, NEGprev, r=["cb"], w=bkeys(b), start=False, sig=False)
                            o_ = psf[b][:, 256 + hd * 128: 256 + (hd + 1) * 128]
                            MM(o_, kb[prt, qs], qb[prt, qs], r=kck + qk_, w=bkeys(b), stop=False, sig=False)
                            MM(o_, ident, NEGcur, r=["cb"], w=bkeys(b), start=False, sig=(hd == 1))
                        pt_ = PTb[nB % 3]
                        ptk = ("PT", nB % 3)
                        nB += 1
                        c_lo = 0 if i > 0 else 256
                        ACTV(pt_[:, c_lo:512], psf[b][:, c_lo:512], AF.Exp, r=bkeys(b), w=[ptk], scale=0.125)
                        return pt_, ptk, qs

                    def pv_blk(r_, i, ctx):
                        pt_, ptk, qs = ctx
                        blk = r_ * nbk + i
                        terms = []
                        for hd in range(2):
                            if i > 0:
                                terms.append((hd * 128, blk - 1, hd))
                            terms.append((256 + hd * 128, blk, hd))
                        bo = k.bank()
                        for n2, (c0, bk_, hd) in enumerate(terms):
                            MM(psf[bo][:, 0:128], vpt[:, bk_, hd, :], pt_[:, c0:c0 + 128], r=[ptk, ("vp", vpi, bk_)], w=bkeys(bo),
                               start=(n2 == 0), stop=(n2 == len(terms) - 1), sig=False)
                        for n2, (c0, bk_, hd) in enumerate(terms):
                            MM(psf[bo][:, 128:256], onesP[hd], pt_[:, c0:c0 + 128], r=[ptk, "cb"], w=bkeys(bo),
                               start=(n2 == 0), stop=(n2 == len(terms) - 1), sig=(n2 == len(terms) - 1))
                        k.op("dve", lambda e: e.tensor_copy(out=num[g][:, qs], in_=psf[bo][:, 0:128]), r=bkeys(bo), w=[("num", g, blk)])
                        ACTV(den[g][:, qs], psf[bo][:, 128:256], AF.Copy, r=bkeys(bo), w=[("den", g, blk)])

                    blks = [(r_, i) for r_ in range(d) for i in range(nbk)]
                    ctxs = {0: scores_blk(*blks[0])}
                    for n_, (r_, i) in enumerate(blks):
                        if n_ + 1 < len(blks):
                            ctxs[n_ + 1] = scores_blk(*blks[n_ + 1])
                        pv_blk(r_, i, ctxs.pop(n_))
                allnd = [(nm, g_, blk) for nm in ("num", "den") for g_ in range(3) for blk in range(16)]
                for c4 in range(4):
                    cs = slice(c4 * 512, (c4 + 1) * 512)
                    TT("dve", dtot[:, cs], den[0][:, cs], den[1][:, cs], ALU.add, r=allnd, w=[("dtot", c4)])
                    TT("dve", dtot[:, cs], dtot[:, cs], den[2][:, cs], ALU.add, r=allnd, w=[("dtot", c4)])
                    ACTV(dtot[:, cs], dtot[:, cs], AF.Ln, r=[("dtot", c4)], w=[("dtot", c4)])
                    ACTV(dtot[:, cs], dtot[:, cs], AF.Exp, r=[("dtot", c4)], w=[("dtot", c4)], scale=-1.0)
                    for g in range(3):
                        tb = tmpB[(c4 * 3 + g) % 2]
                        tk = ("tmpB", (c4 * 3 + g) % 2)
                        STT(tb[:], num[g][:, cs], 0.5, dtot[:, cs], ALU.mult, ALU.mult, r=allnd + [("dtot", c4)], w=[tk])
                        TT("pool", yb_fm[:, 2 * g + p, cs], tb[:], zbg[g][:, cs], ALU.mult, r=[tk, ("zbg", g, c4)], w=[("yb", 2 * g + p, c4)])
            k.barrier()
            pBi.close()
            dump_fm("yb", yb_fm, 6)

            wo = sbx("wo", [128, 8, D], BF16)
            fnw = sbx("fnw", [128, D])
            wgb = [sbx(f"wgb{i}", [128, 8, 128], BF16) for i in range(2)]
            pbj = [sbx(f"pbj{i}", [128, 6, 128], BF16) for i in range(2)]
            tnhB = [sbx(f"tnhB{i}", [128, 512]) for i in range(2)]
            tmpB = [sbx(f"tmpB{i}", [128, 512]) for i in range(2)]
            wload(wgb[0][:], ("wgb", 0), win_d[:, 8208: 8208 + 128])
            wload(pbj[0][:], ("pbj", 0), pb_d[:, 0:128])
            k.dma("sp", fnw[:], fnw_d[:, :], w=["fnw"])
            wload(wo[:, :, 0:512], ("wo", 0), wo_d[:, 0:512])
            wload(wo[:, :, 512:1024], ("wo", 1), wo_d[:, 512:1024])
            n_ = 0
            for j in range(8):
                if j + 1 < 8:
                    wload(wgb[(j + 1) % 2][:], ("wgb", (j + 1) % 2), win_d[:, 8208 + (j + 1) * 128: 8208 + (j + 2) * 128])
                    wload(pbj[(j + 1) % 2][:], ("pbj", (j + 1) % 2), pb_d[:, (j + 1) * 128:(j + 2) * 128])
                TT("dve", wo[:, j, :], wo[:, j, :], gate[:], ALU.mult, r=[("wo", 0), ("wo", 1), "gate"], w=[("wofold", j)])
                for tt4 in range(4):
                    cs = slice(tt4 * 512, (tt4 + 1) * 512)
                    tb, tk = tnhB[n_ % 2], ("tnhB", n_ % 2)
                    t2b, t2k = tmpB[n_ % 2], ("tmpB", n_ % 2)
                    n_ += 1
                    b1 = k.bank()
                    proj_fm(wgb[j % 2], ("wgb", j % 2), 0, tt4, b1)
                    ACTV(tb[:], psf[b1][:, :], AF.Tanh, r=bkeys(b1), w=[tk], scale=0.5)
                    b2 = k.bank()
                    for pt in range(6):
                        MM(psf[b2][:, :], pbj[j % 2][:, pt, :], yb_fm[:, pt, cs], r=[("pbj", j % 2), ("yb", pt, tt4)], w=bkeys(b2),
                           start=(pt == 0), stop=(pt == 5), sig=(pt == 5))
                    STT(t2b[:], tb[:], 1.0, psf[b2][:, :], ALU.add, ALU.mult, r=[tk] + bkeys(b2), w=[t2k])
                    TT("pool", ma_fm[:, j, cs], ma_fm[:, j, cs], t2b[:], ALU.add, r=[t2k, ("ma", j, tt4)], w=[("ma", j, tt4)])
            k.barrier()
        dump_fm("merged", ma_fm, 8)

        with ExitStack() as pC:
            sbx = lambda name, shape, dty=F32: pC.enter_context(nc.sbuf_tensor("sb_" + name, shape, dty))
            xin = [sbx(f"xin{i}", [128, D]) for i in range(3)]
            xo = [sbx(f"xo{i}", [128, D]) for i in range(3)]
            outt = [sbx(f"outt{i}", [128, D]) for i in range(3)]
            junkc = sbx("junkc", [128, D], BF16)
            ssc = sbx("ssc", [128, NT]); rsc = sbx("rsc", [128, NT])
            k.op("dve", lambda e: e.memset(ssc[:], 0.0), w=["ssc"])
            for tt in range(NT):
                i2 = tt % 3
                k.dma("sp", xin[i2][:], x_d[tt * 128:(tt + 1) * 128, :], w=[("xin", i2)])
                for nh in range(2):
                    b = k.bank()
                    cs = slice(nh * 512, (nh + 1) * 512)
                    for kt in range(8):
                        MM(psf[b][:, :], ma_fm[:, kt, tt * 128:(tt + 1) * 128], wo[:, kt, cs], r=[("wo", nh)], w=bkeys(b),
                           start=(kt == 0), stop=(kt == 7), sig=(kt == 7))
                    TT("dve", xo[i2][:, cs], psf[b][:, :], xin[i2][:, cs], ALU.add, r=bkeys(b) + [("xin", i2)], w=[("xo", i2, nh)])
                k.op("act", lambda e: e.activation(out=junkc[:], in_=xo[i2][:], func=AF.Square, accum_out=ssc[:, tt:tt + 1]),
                     r=[("xo", i2, 0), ("xo", i2, 1), "ssc"], w=["junkc", ("ssc", tt)])
                ACTV(rsc[:, tt:tt + 1], ssc[:, tt:tt + 1], AF.Ln, r=[("ssc", tt)], w=[("rsc", tt)], scale=1.0 / D, bias=epsc[:, 0:1])
                ACTV(rsc[:, tt:tt + 1], rsc[:, tt:tt + 1], AF.Exp, r=[("rsc", tt)], w=[("rsc", tt)], scale=-0.5)
                STT(outt[i2][:], xo[i2][:], rsc[:, tt:tt + 1], fnw[:], ALU.mult, ALU.mult,
                    r=[("xo", i2, 0), ("xo", i2, 1), ("rsc", tt), "fnw"], w=[("outt", i2)])
                k.dma("pool", out_d[tt * 128:(tt + 1) * 128, :], outt[i2][:], r=[("outt", i2)])
            k.barrier()
        pB.close()
        k.barrier()
        for e in ("sp",):
            for c in k.E[e].ring:
                k.wait(k.E[e], c, c.count)
    return nc


def make_consts():
    i = np.arange(128)
    e, c = i[:, None], i[None, :]
    cf = np.zeros((128, 768), np.float32)
    cf[:, 0:128] = 1.0
    cf[:, 128:256] = (e <= c)
    cf[:, 256:384] = (e > c)
    cf[:, 384:512] = np.eye(128)
    cf[:, 512:640] = np.where(e <= c, 0.0, NEG)
    cf[:, 640:768] = np.where(e < c, 0.0, NEG)
    cb = np.zeros((128, 1664), np.float32)
    cb[:, 0:128] = np.eye(128)
    cb[:, 128:256] = 1.0
    cb[:, 256:384] = np.where(c <= e, 0.0, NEG)
    cb[:, 384:512] = np.where(c >= e, 0.0, NEG)
    cb[:, 512:640] = (c < 64)
    cb[:, 640:768] = (c >= 64)
    for n_, b_ in enumerate((1, 2, 4, 8, 16, 32, 64)):
        cb[:, 768 + n_ * 128: 768 + (n_ + 1) * 128] = ((e // b_) % 2 == 0) & (c // b_ == e // b_ + 1)
    return cf, cb.astype(ml_dtypes.bfloat16)


def make_in_maps(inp):
    cf, cb = make_consts()
    f = lambda a: np.ascontiguousarray(a, dtype=np.float32)
    rep = lambda v: f(np.broadcast_to(np.asarray(v).reshape(1, -1), (128, np.asarray(v).size)))
    shared = {
        "normw_bc": rep(inp["norm_w"][0]), "ada_w": f(inp["ada_w"][0]), "adab_bc": rep(inp["ada_b"][0]),
        "w_in": f(inp["w_in"][0]),
        "convw": f(np.asarray(inp["conv_w"][0]).reshape(4, 24, 128).transpose(2, 1, 0)),
        "alog_bc": rep(inp["a_log"][0]), "dtb_bc": rep(inp["dt_bias"][0]),
        "dnw_col": f(np.asarray(inp["dn_norm_w"][0]).reshape(128, 1)),
        "w_proj_a": f(inp["w_proj_a"][0]), "w_proj_b": f(inp["w_proj_b"][0]), "w_out": f(inp["w_out"][0]),
        "fnw_bc": rep(inp["final_norm_w"]), "cf": cf, "cb": cb,
    }
    maps = []
    for b in range(8):
        m = dict(shared)
        m["x"] = f(inp["x"][b])
        m["cT"] = f(np.asarray(inp["c"][b]).reshape(8, 128).T)
        maps.append(m)
    return maps


def kernel(**inputs):
    nc = build_program()
    maps = make_in_maps(inputs)
    res = run_bass_kernel_spmd(nc, maps, core_ids=list(range(8)))
    return np.stack([np.asarray(r["out"], dtype=np.float32) for r in res.results], axis=0)
```
